# Optimizing a Trainium2 kernel written in Bass

```python
import math
import jax, jax.numpy as jnp
from jax import lax
import numpy as np

D_MODEL = 2048
BATCH = 4
SEQ = 4096
DEPTH = 2
DEC_BATCH = 32
DEC_SEQ = 16
PAST_LEN = 4096

CHUNK = 64
N_META = 16
N_EVEN = (DEPTH + 1) // 2
N_ODD = DEPTH // 2
RMS_EPS = 1e-6
FFN_RESIDUAL = 0.5
FFN_DIM = 256 * ((8 * D_MODEL // 3 + 255) // 256)
AB_WIDTH = D_MODEL
POOL_WIDTH = AB_WIDTH // 2
POOL_WINDOWS = (2, 4, 8, 16)
POOL_GROUP = POOL_WIDTH // len(POOL_WINDOWS)
POOL_HIST = max(POOL_WINDOWS) - 1
SSM_WIDTH = AB_WIDTH - POOL_WIDTH
SSM_GROUP = 16
SSM_GROUPS = SSM_WIDTH // SSM_GROUP
SSM_STATE = 64
SSM_BLOCK = CHUNK
SSM_DT_MIN = 1e-3
SSM_DT_MAX = 1e-1
SB_HEADS = 16
SB_HEAD_DIM = D_MODEL // SB_HEADS
SB_BLOCK = 128

kernel_name = 'streaming_pool_s5_stickbreak_macaron'


def rmsnorm(x, g):
    xf = x.astype(jnp.float32)
    xf = xf * lax.rsqrt(jnp.mean(xf * xf, axis=-1, keepdims=True) + RMS_EPS)
    return (xf * g.astype(jnp.float32)).astype(x.dtype)


def swiglu(x, w_gate, w_up, w_down):
    return ((jax.nn.silu(x @ w_gate) * (x @ w_up)) @ w_down).astype(x.dtype)


def pool_mixer(u, hist, w_pool, scale):
    L = u.shape[1]
    ext = u if hist is None else jnp.concatenate([hist.astype(u.dtype), u], axis=1)
    T = ext.shape[1]
    H = T - L
    ef = ext.astype(jnp.float32)
    csum = jnp.concatenate([jnp.zeros_like(ef[:, :1]), jnp.cumsum(ef, axis=1)], axis=1)
    end = jnp.arange(H, T) + 1
    outs = []
    for g, w in enumerate(POOL_WINDOWS):
        sl = slice(g * POOL_GROUP, (g + 1) * POOL_GROUP)
        start = jnp.maximum(end - w, 0)
        cnt = (end - start).astype(jnp.float32)
        mean = (csum[:, end, sl] - csum[:, start, sl]) / cnt[None, :, None]
        outs.append(jnp.einsum('blc,cd->bld', mean - ef[:, H:, sl], w_pool[g].astype(jnp.float32)))
    y = jnp.concatenate(outs, axis=-1) * scale.astype(jnp.float32)
    return y.astype(u.dtype), ext[:, T - POOL_HIST:]


def ssm_discretize(a_re, a_im, log_dt, b_re, b_im):
    f32 = jnp.float32
    a_re = a_re.astype(f32); a_im = a_im.astype(f32)
    dt = jnp.exp(log_dt.astype(f32))[:, None]
    mag = jnp.exp(a_re * dt)
    lb_re = mag * jnp.cos(a_im * dt)
    lb_im = mag * jnp.sin(a_im * dt)
    den = a_re * a_re + a_im * a_im
    c_re = ((lb_re - 1.0) * a_re + lb_im * a_im) / den
    c_im = (lb_im * a_re - (lb_re - 1.0) * a_im) / den
    b_re = b_re.astype(f32); b_im = b_im.astype(f32)
    bb_re = c_re[..., None] * b_re - c_im[..., None] * b_im
    bb_im = c_re[..., None] * b_im + c_im[..., None] * b_re
    return lb_re, lb_im, bb_re, bb_im


def complex_affine_combine(e1, e2):
    a1r, a1i, b1r, b1i = e1
    a2r, a2i, b2r, b2i = e2
    return (a1r * a2r - a1i * a2i,
            a1r * a2i + a1i * a2r,
            a2r * b1r - a2i * b1i + b2r,
            a2r * b1i + a2i * b1r + b2i)


def ssm_mixer(u, h0_re, h0_im, a_re, a_im, log_dt, b_re, b_im, c_re, c_im, d, w_glu, b_glu):
    f32 = jnp.float32
    Bn, L, _ = u.shape
    G, P = SSM_GROUPS, SSM_GROUP
    uf = u.astype(f32).reshape(Bn, L, G, P)
    lb_re, lb_im, bb_re, bb_im = ssm_discretize(a_re, a_im, log_dt, b_re, b_im)
    cr = c_re.astype(f32); ci = c_im.astype(f32)
    nblk = -(-L // SSM_BLOCK)
    pad = nblk * SSM_BLOCK - L
    ub = jnp.pad(uf, ((0, 0), (pad, 0), (0, 0), (0, 0))).reshape(Bn, nblk, SSM_BLOCK, G, P).transpose(1, 0, 2, 3, 4)
    valid = (jnp.arange(nblk * SSM_BLOCK) >= pad).reshape(nblk, SSM_BLOCK)

    def body(carry, xs):
        hr0, hi0 = carry
        ublk, vblk = xs
        br = jnp.einsum('btgp,gnp->btgn', ublk, bb_re)
        bi = jnp.einsum('btgp,gnp->btgn', ublk, bb_im)
        m = vblk[None, :, None, None]
        ar = jnp.broadcast_to(jnp.where(m, lb_re, 1.0), br.shape)
        ai = jnp.broadcast_to(jnp.where(m, lb_im, 0.0), br.shape)
        Ar, Ai, Hr, Hi = lax.associative_scan(complex_affine_combine, (ar, ai, br, bi), axis=1)
        hr = Hr + Ar * hr0[:, None] - Ai * hi0[:, None]
        hi = Hi + Ar * hi0[:, None] + Ai * hr0[:, None]
        yblk = jnp.einsum('btgn,gpn->btgp', hr, cr) - jnp.einsum('btgn,gpn->btgp', hi, ci)
        return (hr[:, -1], hi[:, -1]), yblk

    (hr_T, hi_T), ys = lax.scan(body, (h0_re.astype(f32), h0_im.astype(f32)), (ub, valid))
    y = ys.transpose(1, 0, 2, 3, 4).reshape(Bn, nblk * SSM_BLOCK, G, P)[:, pad:]
    y = (y + d.astype(f32).reshape(G, P) * uf).reshape(Bn, L, SSM_WIDTH)
    z = jax.nn.gelu(y)
    out = z * jax.nn.sigmoid(z @ w_glu.astype(f32) + b_glu.astype(f32))
    return out.astype(u.dtype), hr_T, hi_T


def stick_breaking_attention(q, k, v, q_offset):
    f32 = jnp.float32
    Bn, Lq, H, dh = q.shape
    Lk = k.shape[1]
    bq = SB_BLOCK if Lq >= SB_BLOCK else Lq
    nblk = -(-Lq // bq)
    pad = nblk * bq - Lq
    qb_all = jnp.pad(q, ((0, 0), (0, pad), (0, 0), (0, 0))).reshape(Bn, nblk, bq, H, dh).transpose(1, 0, 2, 3, 4)
    kf = k.astype(f32); vf = v.astype(f32)
    key_idx = jnp.arange(Lk)
    scale = 1.0 / math.sqrt(dh)

    def block(args):
        qb, b = args
        q_idx = q_offset + b * bq + jnp.arange(bq)
        mask = key_idx[None, :] < q_idx[:, None]
        z = jnp.einsum('bqhd,bkhd->bhqk', qb.astype(f32), kf) * scale
        log_beta = jax.nn.log_sigmoid(z)
        log_1mb = jnp.where(mask, log_beta - z, 0.0)
        suffix = lax.cumsum(log_1mb, axis=3, reverse=True)
        after = jnp.concatenate([suffix[..., 1:], jnp.zeros_like(suffix[..., :1])], axis=-1)
        w = jnp.where(mask, jnp.exp(log_beta + after), 0.0)
        return jnp.einsum('bhqk,bkhd->bqhd', w, vf)

    out = lax.map(block, (qb_all, jnp.arange(nblk)))
    return out.transpose(1, 0, 2, 3, 4).reshape(Bn, nblk * bq, H, dh)[:, :Lq]


def sb_mixer(h, w_qkv, w_out, k_hist, v_hist):
    Bn, L, _ = h.shape
    qkv = (h @ w_qkv).reshape(Bn, L, 3, SB_HEADS, SB_HEAD_DIM)
    q, k, v = qkv[:, :, 0], qkv[:, :, 1], qkv[:, :, 2]
    if k_hist is None:
        k_all, v_all, off = k, v, 0
    else:
        k_all = jnp.concatenate([k_hist.astype(k.dtype), k], axis=1)
        v_all = jnp.concatenate([v_hist.astype(v.dtype), v], axis=1)
        off = k_hist.shape[1]
    o = stick_breaking_attention(q, k_all, v_all, off).astype(h.dtype)
    return o.reshape(Bn, L, SB_HEADS * SB_HEAD_DIM) @ w_out, k, v


def run_trunk(x, pool_hist, h0_re, h0_im, k_hist, v_hist, p):
    pools, hres, hims, ks, vs = [], [], [], [], []
    for layer in range(DEPTH):
        h = rmsnorm(x, p['ffn_norm'][layer, 0])
        x = x + FFN_RESIDUAL * swiglu(h, p['ffn_w_gate'][layer, 0], p['ffn_w_up'][layer, 0], p['ffn_w_down'][layer, 0])
        h = rmsnorm(x, p['mix_norm'][layer])
        if layer % 2 == 0:
            e = layer // 2
            u = h @ p['ab_w_in'][e]
            ya, tail = pool_mixer(u[..., :POOL_WIDTH], None if pool_hist is None else pool_hist[e],
                                  p['pool_w'][e], p['pool_scale'][e])
            yb, hr, hi = ssm_mixer(u[..., POOL_WIDTH:], h0_re[e], h0_im[e], p['ssm_a_re'][e], p['ssm_a_im'][e],
                                   p['ssm_log_dt'][e], p['ssm_b_re'][e], p['ssm_b_im'][e], p['ssm_c_re'][e],
                                   p['ssm_c_im'][e], p['ssm_d'][e], p['ssm_w_glu'][e], p['ssm_b_glu'][e])
            y = jnp.concatenate([ya, yb], axis=-1) @ p['ab_w_out'][e]
            pools.append(tail); hres.append(hr); hims.append(hi)
        else:
            o = layer // 2
            y, k, v = sb_mixer(h, p['sb_w_qkv'][o], p['sb_w_out'][o],
                               None if k_hist is None else k_hist[o], None if v_hist is None else v_hist[o])
            ks.append(k); vs.append(v)
        x = x + y.astype(x.dtype)
        h = rmsnorm(x, p['ffn_norm'][layer, 1])
        x = x + FFN_RESIDUAL * swiglu(h, p['ffn_w_gate'][layer, 1], p['ffn_w_up'][layer, 1], p['ffn_w_down'][layer, 1])
    x = rmsnorm(x, p['final_norm'])
    return x, jnp.stack(pools), jnp.stack(hres), jnp.stack(hims), jnp.stack(ks), jnp.stack(vs)


def setup_inputs(seed: int = 0) -> dict:
    key = jax.random.key(seed)
    ks = jax.random.split(key, 32)
    f32 = jnp.float32
    G, N, P = SSM_GROUPS, SSM_STATE, SSM_GROUP

    def nrm(k, shape, scale=1.0):
        return jax.random.normal(k, shape, f32) * scale

    return {
        'x_prompt': nrm(ks[0], (BATCH, SEQ, D_MODEL)),
        'x_sample': nrm(ks[1], (DEC_BATCH, DEC_SEQ, D_MODEL)),
        'cache_pool': nrm(ks[2], (N_EVEN, DEC_BATCH, POOL_HIST, POOL_WIDTH)),
        'state_ssm_re': nrm(ks[3], (N_EVEN, DEC_BATCH, G, N), 0.1),
        'state_ssm_im': nrm(ks[4], (N_EVEN, DEC_BATCH, G, N), 0.1),
        'cache_k': nrm(ks[5], (N_ODD, DEC_BATCH, PAST_LEN, SB_HEADS, SB_HEAD_DIM)),
        'cache_v': nrm(ks[6], (N_ODD, DEC_BATCH, PAST_LEN, SB_HEADS, SB_HEAD_DIM)),
        'meta_tokens': nrm(ks[7], (N_META, D_MODEL)),
        'ffn_norm': 1.0 + nrm(ks[8], (DEPTH, 2, D_MODEL), 0.02),
        'ffn_w_gate': nrm(ks[9], (DEPTH, 2, D_MODEL, FFN_DIM), D_MODEL ** -0.5),
        'ffn_w_up': nrm(ks[10], (DEPTH, 2, D_MODEL, FFN_DIM), D_MODEL ** -0.5),
        'ffn_w_down': nrm(ks[11], (DEPTH, 2, FFN_DIM, D_MODEL), FFN_DIM ** -0.5),
        'mix_norm': 1.0 + nrm(ks[12], (DEPTH, D_MODEL), 0.02),
        'ab_w_in': nrm(ks[13], (N_EVEN, D_MODEL, AB_WIDTH), D_MODEL ** -0.5),
        'pool_w': nrm(ks[14], (N_EVEN, len(POOL_WINDOWS), POOL_GROUP, POOL_GROUP), POOL_GROUP ** -0.5),
        'pool_scale': 1.0 + nrm(ks[15], (N_EVEN, POOL_WIDTH), 0.02),
        'ssm_a_re': -0.5 + nrm(ks[16], (N_EVEN, G, N), 0.01),
        'ssm_a_im': jnp.pi * jnp.arange(N, dtype=f32) + nrm(ks[17], (N_EVEN, G, N), 0.01),
        'ssm_log_dt': jax.random.uniform(ks[18], (N_EVEN, G), f32, math.log(SSM_DT_MIN), math.log(SSM_DT_MAX)),
        'ssm_b_re': nrm(ks[19], (N_EVEN, G, N, P), (2 * P) ** -0.5),
        'ssm_b_im': nrm(ks[20], (N_EVEN, G, N, P), (2 * P) ** -0.5),
        'ssm_c_re': nrm(ks[21], (N_EVEN, G, P, N), N ** -0.5),
        'ssm_c_im': nrm(ks[22], (N_EVEN, G, P, N), N ** -0.5),
        'ssm_d': nrm(ks[23], (N_EVEN, SSM_WIDTH)),
        'ssm_w_glu': nrm(ks[24], (N_EVEN, SSM_WIDTH, SSM_WIDTH), SSM_WIDTH ** -0.5),
        'ssm_b_glu': nrm(ks[25], (N_EVEN, SSM_WIDTH), 0.01),
        'ab_w_out': nrm(ks[26], (N_EVEN, AB_WIDTH, D_MODEL), AB_WIDTH ** -0.5),
        'sb_w_qkv': nrm(ks[27], (N_ODD, D_MODEL, 3 * D_MODEL), D_MODEL ** -0.5),
        'sb_w_out': nrm(ks[28], (N_ODD, D_MODEL, D_MODEL), D_MODEL ** -0.5),
        'final_norm': 1.0 + nrm(ks[29], (D_MODEL,), 0.02),
    }


def reference(x_prompt, x_sample, cache_pool, state_ssm_re, state_ssm_im, cache_k, cache_v,
              meta_tokens, ffn_norm, ffn_w_gate, ffn_w_up, ffn_w_down, mix_norm,
              ab_w_in, pool_w, pool_scale, ssm_a_re, ssm_a_im, ssm_log_dt,
              ssm_b_re, ssm_b_im, ssm_c_re, ssm_c_im, ssm_d, ssm_w_glu, ssm_b_glu, ab_w_out,
              sb_w_qkv, sb_w_out, final_norm):
    p = dict(ffn_norm=ffn_norm, ffn_w_gate=ffn_w_gate, ffn_w_up=ffn_w_up, ffn_w_down=ffn_w_down,
             mix_norm=mix_norm, ab_w_in=ab_w_in, pool_w=pool_w, pool_scale=pool_scale,
             ssm_a_re=ssm_a_re, ssm_a_im=ssm_a_im, ssm_log_dt=ssm_log_dt, ssm_b_re=ssm_b_re,
             ssm_b_im=ssm_b_im, ssm_c_re=ssm_c_re, ssm_c_im=ssm_c_im, ssm_d=ssm_d,
             ssm_w_glu=ssm_w_glu, ssm_b_glu=ssm_b_glu, ab_w_out=ab_w_out,
             sb_w_qkv=sb_w_qkv, sb_w_out=sb_w_out, final_norm=final_norm)
    nb = x_prompt.shape[0]
    meta = jnp.broadcast_to(meta_tokens.astype(x_prompt.dtype)[None], (nb, N_META, D_MODEL))
    xp = jnp.concatenate([meta, x_prompt], axis=1)
    h0 = jnp.zeros((N_EVEN, nb, SSM_GROUPS, SSM_STATE), jnp.float32)
    yp, pool_p, re_p, im_p, k_p, v_p = run_trunk(xp, None, h0, h0, None, None, p)
    y_sample, pool_s, re_s, im_s, k_s, v_s = run_trunk(x_sample, cache_pool, state_ssm_re, state_ssm_im,
                                                       cache_k, cache_v, p)
    y_prompt = yp[:, N_META:]
    return (y_prompt, y_sample, pool_p, pool_s, re_p, im_p, re_s, im_s, k_p, v_p, k_s, v_s)
```

```python
import os, math
import numpy as np
from contextlib import ExitStack
import concourse.bass as bass
import concourse.mybir as mybir
from concourse.bass_utils import run_bass_kernel_spmd

F32 = mybir.dt.float32
BF = mybir.dt.bfloat16
AF = mybir.ActivationFunctionType
ALU = mybir.AluOpType

RP = 4224
NPR = 4112
SC0 = 4112
NSQ = 4
SQL = 16
D = 2048
FF = 5632
NFC = 44
PAST = 4096
NHEAD = 16
SUP = [(0, 896), (896, 896), (1792, 896), (2688, 896), (3584, 640)]
LCH = 64
EPS = 1e-6


def cchunks(c0, W):
    h = W // 2
    return [(c0, h), (c0 + h, W - h)]


_UID = [0]


def _sbt(nc, name, shape, dt):
    _UID[0] += 1
    return nc.sbuf_tensor("%s_%d" % (name, _UID[0]), shape, dt)


def _pst(nc, name, shape, dt):
    _UID[0] += 1
    return nc.psum_tensor("%s_%d" % (name, _UID[0]), shape, dt)


class Buf:
    __slots__ = ("w", "r")

    def __init__(self):
        self.w = []
        self.r = {}


class _Cap:
    def __getattr__(self, name):
        def f(*a, **k):
            self.call = (name, a, k)
            return None
        return f


class Rec:
    ENG = ("pe", "act", "dve", "pool", "sp")
    K = 8

    def __init__(self, nc):
        self.nc = nc
        self.ops = {e: [] for e in self.ENG}
        self.cnt = {e: 0 for e in self.ENG}
        self.dq = {"sp": 0, "pool": 0, "act": 0}
        self.floor = {}
        self.nops = {e: 0 for e in self.ENG}

    def _deps(self, reads, writes, extra):
        deps = list(extra)
        for b in reads:
            deps += b.w
        for b in writes:
            deps += b.w
            deps += list(b.r.values())
        return deps

    def _upd(self, tok, key, reads, writes):
        for b in reads:
            b.r[key] = tok
        for b in writes:
            b.w = [tok]
            b.r = {}

    def op(self, eng, fn, reads=(), writes=(), inc=True, extra=()):
        deps = self._deps(reads, writes, extra)
        deps += self.floor.pop(eng, [])
        if inc:
            self.cnt[eng] += 1
            tok = ("c", eng, self.cnt[eng])
        else:
            tok = ("c", eng, self.cnt[eng] + 1)
        cap = _Cap()
        fn(cap)
        self.ops[eng].append(("op", cap.call, deps, inc))
        self._upd(tok, ("c", eng), reads, writes)
        return tok

    def dma(self, q, out, in_, reads=(), writes=(), extra=(), **kw):
        deps = self._deps(reads, writes, extra)
        deps += self.floor.pop(q, [])
        j = self.dq[q]
        self.dq[q] = j + 1
        slot = j % self.K
        val = 16 * (j // self.K + 1)
        if j >= self.K:
            deps.append(("d", q, slot, val - 16))
        tok = ("d", q, slot, val)
        self.ops[q].append(("dma", (lambda e, out=out, in_=in_, kw=kw: e.dma_start(out=out, in_=in_, **kw)), deps, slot))
        self._upd(tok, ("d", q, slot), reads, writes)
        return tok

    def all_tokens(self):
        toks = []
        for e in self.ENG:
            if self.cnt[e] > 0:
                toks.append(("c", e, self.cnt[e]))
        for q, j in self.dq.items():
            for slot in range(min(j, self.K)):
                n = (j - 1 - slot) // self.K + 1
                toks.append(("d", q, slot, 16 * n))
        return toks

    def barrier(self):
        toks = self.all_tokens()
        for e in self.ENG:
            self.floor[e] = list(toks) + self.floor.get(e, [])

    def begin(self, es):
        nc = self.nc
        self.csem = {e: es.enter_context(nc.semaphore("c_" + e)) for e in self.ENG}
        self.dsem = {q: [es.enter_context(nc.semaphore("d_%s%d" % (q, i))) for i in range(self.K)]
                     for q in self.dq}
        self.block = es.enter_context(nc.Block())
        self.waited = {e: {} for e in self.ENG}

    def flush(self):
        bname = {"pe": "tensor", "act": "scalar", "dve": "vector", "pool": "gpsimd", "sp": "sync"}
        csem, dsem = self.csem, self.dsem
        for ename in self.ENG:
            ops = self.ops[ename]
            self.nops[ename] += len(ops)
            self.ops[ename] = []
            if not ops:
                continue

            def body(e, ename=ename, ops=ops):
                waited = self.waited[ename]
                for (kind, fn, deps, aux) in ops:
                    for t in deps:
                        if t[0] == "c":
                            if t[1] == "pe" and ename == "pe":
                                continue
                            key = ("c", t[1]); val = t[2]; sem = csem[t[1]]
                        else:
                            key = ("d", t[1], t[2]); val = t[3]; sem = dsem[t[1]][t[2]]
                        if waited.get(key, 0) >= val:
                            continue
                        waited[key] = val
                        e.wait_ge(sem, val)
                    if kind == "wait":
                        continue
                    if kind == "op":
                        ins = getattr(e, fn[0])(*fn[1], **fn[2])
                    else:
                        ins = fn(e)
                    if kind == "dma":
                        ins.then_inc(dsem[ename][aux], 16)
                    elif aux:
                        ins.then_inc(csem[ename], 1)
            getattr(self.block, bname[ename])(body)

    def finish(self):
        final = self.all_tokens()
        self.ops["sp"].append(("wait", None, final, False))
        self.flush()


class Ring:
    def __init__(self, es, nc, name, n, shape, dtype, psum=False):
        self.t = []
        self.b = []
        for i in range(n):
            if psum:
                t = es.enter_context(_pst(nc, "%s%d" % (name, i), shape, dtype))
            else:
                t = es.enter_context(_sbt(nc, "%s%d" % (name, i), shape, dtype))
            self.t.append(t)
            self.b.append(Buf())
        self.i = 0

    def next(self):
        k = self.i % len(self.t)
        self.i += 1
        return self.t[k], self.b[k]


class Ctx:
    pass


def xbuf(C, name, dk, si):
    key = (name, dk, si)
    if key not in C.dbufs:
        C.dbufs[key] = Buf()
    return C.dbufs[key]


def phase_norm(R, nc, C, S, Xsrc, xname, si, c0, W, gi, hout, hbuf, hview=None):
    ch = cchunks(0, W)
    for dk in range(16):
        xt, xb = S.xr.next()
        R.dma("sp", xt[:, 0:W], Xsrc[dk * 128:(dk + 1) * 128, c0:c0 + W],
              reads=[xbuf(C, xname, dk, si)], writes=[xb])
        st, sb = S.sqr.next()
        R.op("act", lambda e, o=st[:, 0:W], i=xt[:, 0:W]: e.activation(out=o, in_=i, func=AF.Square),
             reads=[xb], writes=[sb])
        for ci, (cc, n) in enumerate(ch):
            R.op("pe", lambda e, o=S.pss[ci][:, 0:n], r=st[:, cc:cc + n], s=(dk == 0), p=(dk == 15):
                 e.matmul(o, lhsT=C.ones32[:], rhs=r, start=s, stop=p),
                 reads=[sb], writes=[S.pssb[ci]], inc=True)
    for ci, (cc, n) in enumerate(ch):
        R.op("act", lambda e, o=S.srt[:, cc:cc + n], i=S.pss[ci][:, 0:n]:
             e.activation(out=o, in_=i, func=AF.Sqrt, bias=C.epsb[:, 0:1], scale=1.0 / D),
             reads=[S.pssb[ci]], writes=[S.srtb])
    R.op("dve", lambda e: e.reciprocal(out=S.rstd[:, 0:W], in_=S.srt[:, 0:W]),
         reads=[S.srtb], writes=[S.rstdb])
    for dk in range(16):
        xt, xb = S.xr.next()
        R.dma("sp", xt[:, 0:W], Xsrc[dk * 128:(dk + 1) * 128, c0:c0 + W],
              reads=[xbuf(C, xname, dk, si)], writes=[xb])
        o = hout(dk)
        R.op("dve", lambda e, o=o, i=xt[:, 0:W], g=C.gall[:, gi, dk:dk + 1]:
             e.scalar_tensor_tensor(out=o, in0=i, scalar=g, in1=S.rstd[:, 0:W], op0=ALU.mult, op1=ALU.mult),
             reads=[xb, S.rstdb], writes=[hbuf(dk)])


def alloc_norm(es, nc, S):
    S.xr = Ring(es, nc, "xr", 3, [128, 896], F32)
    S.sqr = Ring(es, nc, "sqr", 2, [128, 896], F32)
    S.pss = [es.enter_context(_pst(nc, "pss%d" % i, [128, 512], F32)) for i in range(2)]
    S.pssb = [Buf(), Buf()]
    S.srt = es.enter_context(_sbt(nc, "srt", [128, 896], F32))
    S.srtb = Buf()
    S.rstd = es.enter_context(_sbt(nc, "rstd", [128, 896], F32))
    S.rstdb = Buf()


def ffn_stage(R, nc, C, Xsrc, sname, Xdst, dname, wg, wu, wd, gi):
    wgv = wg.rearrange("(kt p) f -> p kt f", p=128)
    wuv = wu.rearrange("(kt p) f -> p kt f", p=128)
    wdv = wd.rearrange("(fc p) d -> p fc d", p=128)
    with ExitStack() as es:
        S = Ctx()
        alloc_norm(es, nc, S)
        hfm = es.enter_context(_sbt(nc, "hfm", [128, 16, 896], BF))
        hb = [Buf() for _ in range(16)]
        act = es.enter_context(_sbt(nc, "actf", [128, NFC, 896], BF))
        ab = [Buf() for _ in range(NFC)]
        wgr = Ring(es, nc, "wg", 2, [128, 16, 128], BF)
        wur = Ring(es, nc, "wu", 2, [128, 16, 128], BF)
        wdr = Ring(es, nc, "wd", 2, [128, NFC, 128], BF)
        pgr = Ring(es, nc, "pg", 2, [128, 512], F32, psum=True)
        pur = Ring(es, nc, "pu", 2, [128, 512], F32, psum=True)
        sgr = Ring(es, nc, "sg", 2, [128, 448], F32)
        xor_ = Ring(es, nc, "xo", 2, [128, 448], F32)
        for si, (c0, W) in enumerate(SUP):
            phase_norm(R, nc, C, S, Xsrc, sname, si, c0, W, gi,
                       hout=lambda dk: hfm[:, dk, 0:W], hbuf=lambda dk: hb[dk])
            ch = cchunks(0, W)
            for fc in range(NFC):
                wgt, wgb = wgr.next()
                wut, wub = wur.next()
                R.dma("pool", wgt[:], wgv[:, :, fc * 128:(fc + 1) * 128], writes=[wgb])
                R.dma("pool", wut[:], wuv[:, :, fc * 128:(fc + 1) * 128], writes=[wub])
                for (cc, n) in ch:
                    pg, pgb = pgr.next()
                    pu, pub = pur.next()
                    for kt in range(16):
                        R.op("pe", lambda e, o=pg[:, 0:n], l=wgt[:, kt, :], r=hfm[:, kt, cc:cc + n], s=(kt == 0), p=(kt == 15):
                             e.matmul(o, lhsT=l, rhs=r, start=s, stop=p),
                             reads=[wgb, hb[kt]], writes=[pgb], inc=(kt == 15))
                    for kt in range(16):
                        R.op("pe", lambda e, o=pu[:, 0:n], l=wut[:, kt, :], r=hfm[:, kt, cc:cc + n], s=(kt == 0), p=(kt == 15):
                             e.matmul(o, lhsT=l, rhs=r, start=s, stop=p),
                             reads=[wub, hb[kt]], writes=[pub], inc=(kt == 15))
                    sg, sgb = sgr.next()
                    R.op("act", lambda e, o=sg[:, 0:n], i=pg[:, 0:n]: e.activation(out=o, in_=i, func=AF.Silu),
                         reads=[pgb], writes=[sgb])
                    R.op("dve", lambda e, o=act[:, fc, cc:cc + n], a=sg[:, 0:n], b=pu[:, 0:n]:
                         e.tensor_tensor(out=o, in0=a, in1=b, op=ALU.mult),
                         reads=[sgb, pub], writes=[ab[fc]])
            for dcc in range(16):
                wdt, wdb = wdr.next()
                R.dma("pool", wdt[:], wdv[:, :, dcc * 128:(dcc + 1) * 128], writes=[wdb])
                for (cc, n) in ch:
                    py, pyb = pgr.next()
                    for fc in range(NFC):
                        R.op("pe", lambda e, o=py[:, 0:n], l=wdt[:, fc, :], r=act[:, fc, cc:cc + n], s=(fc == 0), p=(fc == NFC - 1):
                             e.matmul(o, lhsT=l, rhs=r, start=s, stop=p),
                             reads=[wdb, ab[fc]], writes=[pyb], inc=(fc == NFC - 1))
                    xt, xb = S.xr.next()
                    R.dma("sp", xt[:, 0:n], Xsrc[dcc * 128:(dcc + 1) * 128, c0 + cc:c0 + cc + n],
                          reads=[xbuf(C, sname, dcc, si)], writes=[xb])
                    xo, xob = xor_.next()
                    R.op("dve", lambda e, o=xo[:, 0:n], a=py[:, 0:n], b=xt[:, 0:n]:
                         e.scalar_tensor_tensor(out=o, in0=a, scalar=0.5, in1=b, op0=ALU.mult, op1=ALU.add),
                         reads=[pyb, xb], writes=[xob])
                    R.dma("sp", Xdst[dcc * 128:(dcc + 1) * 128, c0 + cc:c0 + cc + n], xo[:, 0:n],
                          reads=[xob], writes=[xbuf(C, dname, dcc, si)])
        R.flush()
    R.barrier()


def norm_linear_stage(R, nc, C, Xsrc, sname, gi, Wd, M, out_cb):
    wv = Wd.rearrange("(kt p) m -> p kt m", p=128)
    with ExitStack() as es:
        S = Ctx()
        alloc_norm(es, nc, S)
        hfm = es.enter_context(_sbt(nc, "hfm", [128, 16, 896], BF))
        hb = [Buf() for _ in range(16)]
        wr = Ring(es, nc, "wl", 2, [128, 16, 128], BF)
        pr = Ring(es, nc, "pl", 3, [128, 512], F32, psum=True)
        S.es = es
        cbs = out_cb(es)
        for si, (c0, W) in enumerate(SUP):
            phase_norm(R, nc, C, S, Xsrc, sname, si, c0, W, gi,
                       hout=lambda dk: hfm[:, dk, 0:W], hbuf=lambda dk: hb[dk])
            for mc in range(M // 128):
                wt, wb = wr.next()
                R.dma("pool", wt[:], wv[:, :, mc * 128:(mc + 1) * 128], writes=[wb])
                for (cc, n) in cchunks(0, W):
                    p, pb = pr.next()
                    for kt in range(16):
                        R.op("pe", lambda e, o=p[:, 0:n], l=wt[:, kt, :], r=hfm[:, kt, cc:cc + n], s=(kt == 0), q=(kt == 15):
                             e.matmul(o, lhsT=l, rhs=r, start=s, stop=q),
                             reads=[wb, hb[kt]], writes=[pb], inc=(kt == 15))
                    cbs(si, mc, c0 + cc, n, p, pb)
        R.flush()
    R.barrier()


def linear_residual_stage(R, nc, C, Asrc, aname, Wd, X, xname):
    wv = Wd.rearrange("(kt p) m -> p kt m", p=128)
    with ExitStack() as es:
        afm = es.enter_context(_sbt(nc, "afm", [128, 16, 896], BF))
        ab = [Buf() for _ in range(16)]
        wr = Ring(es, nc, "wl", 2, [128, 16, 128], BF)
        pr = Ring(es, nc, "pl", 3, [128, 512], F32, psum=True)
        xr = Ring(es, nc, "xr", 3, [128, 448], F32)
        xor_ = Ring(es, nc, "xo", 3, [128, 448], F32)
        for si, (c0, W) in enumerate(SUP):
            for kt in range(16):
                R.dma("sp", afm[:, kt, 0:W], Asrc[kt * 128:(kt + 1) * 128, c0:c0 + W],
                      reads=[xbuf(C, aname, kt, si)], writes=[ab[kt]])
            for mc in range(16):
                wt, wb = wr.next()
                R.dma("pool", wt[:], wv[:, :, mc * 128:(mc + 1) * 128], writes=[wb])
                for (cc, n) in cchunks(0, W):
                    p, pb = pr.next()
                    for kt in range(16):
                        R.op("pe", lambda e, o=p[:, 0:n], l=wt[:, kt, :], r=afm[:, kt, cc:cc + n], s=(kt == 0), q=(kt == 15):
                             e.matmul(o, lhsT=l, rhs=r, start=s, stop=q),
                             reads=[wb, ab[kt]], writes=[pb], inc=(kt == 15))
                    xt, xb = xr.next()
                    R.dma("sp", xt[:, 0:n], X[mc * 128:(mc + 1) * 128, c0 + cc:c0 + cc + n],
                          reads=[xbuf(C, xname, mc, si)], writes=[xb])
                    xo, xob = xor_.next()
                    R.op("dve", lambda e, o=xo[:, 0:n], a=p[:, 0:n], b=xt[:, 0:n]:
                         e.tensor_tensor(out=o, in0=a, in1=b, op=ALU.add),
                         reads=[pb, xb], writes=[xob])
                    R.dma("sp", X[mc * 128:(mc + 1) * 128, c0 + cc:c0 + cc + n], xo[:, 0:n],
                          reads=[xob], writes=[xbuf(C, xname, mc, si)])
        R.flush()
    R.barrier()


def final_stage2(R, nc, C, X, xname, Y):
    with ExitStack() as es:
        S = Ctx()
        alloc_norm(es, nc, S)
        yr = Ring(es, nc, "yr", 3, [128, 896], F32)
        for si, (c0, W) in enumerate(SUP):
            ch = cchunks(0, W)
            for dk in range(16):
                xt, xb = S.xr.next()
                R.dma("sp", xt[:, 0:W], X[dk * 128:(dk + 1) * 128, c0:c0 + W],
                      reads=[xbuf(C, xname, dk, si)], writes=[xb])
                st, sb = S.sqr.next()
                R.op("act", lambda e, o=st[:, 0:W], i=xt[:, 0:W]: e.activation(out=o, in_=i, func=AF.Square),
                     reads=[xb], writes=[sb])
                for ci, (cc, n) in enumerate(ch):
                    R.op("pe", lambda e, o=S.pss[ci][:, 0:n], r=st[:, cc:cc + n], s=(dk == 0), p=(dk == 15):
                         e.matmul(o, lhsT=C.ones32[:], rhs=r, start=s, stop=p),
                         reads=[sb], writes=[S.pssb[ci]], inc=True)
            for ci, (cc, n) in enumerate(ch):
                R.op("act", lambda e, o=S.srt[:, cc:cc + n], i=S.pss[ci][:, 0:n]:
                     e.activation(out=o, in_=i, func=AF.Sqrt, bias=C.epsb[:, 0:1], scale=1.0 / D),
                     reads=[S.pssb[ci]], writes=[S.srtb])
            R.op("dve", lambda e: e.reciprocal(out=S.rstd[:, 0:W], in_=S.srt[:, 0:W]),
                 reads=[S.srtb], writes=[S.rstdb])
            for dk in range(16):
                xt, xb = S.xr.next()
                R.dma("sp", xt[:, 0:W], X[dk * 128:(dk + 1) * 128, c0:c0 + W],
                      reads=[xbuf(C, xname, dk, si)], writes=[xb])
                yt, yb = yr.next()
                R.op("dve", lambda e, o=yt[:, 0:W], i=xt[:, 0:W], g=C.gall[:, 6, dk:dk + 1]:
                     e.scalar_tensor_tensor(out=o, in0=i, scalar=g, in1=S.rstd[:, 0:W], op0=ALU.mult, op1=ALU.mult),
                     reads=[xb, S.rstdb], writes=[yb])
                R.dma("sp", Y[dk * 128:(dk + 1) * 128, c0:c0 + W], yt[:, 0:W], reads=[yb],
                      writes=[xbuf(C, "Y", dk, si)])
        R.flush()
    R.barrier()


def pool_stage(R, nc, C, U, YP, I):
    PADC = 16
    with ExitStack() as es:
        NCOL = PADC + NPR
        lev = [es.enter_context(_sbt(nc, "lev%d" % i, [128, NCOL], F32)) for i in range(3)]
        levb = [Buf() for _ in range(3)]
        slev = [es.enter_context(_sbt(nc, "slev%d" % i, [128, NSQ, 32], F32)) for i in range(3)]
        slevb = [Buf() for _ in range(3)]
        dfr = Ring(es, nc, "dfm", 2, [128, RP], BF)
        wp = es.enter_context(_sbt(nc, "wp", [128, 2, 256], BF))
        wpb = Buf()
        pr = Ring(es, nc, "pp", 2, [128, 512], F32, psum=True)
        orr = Ring(es, nc, "po", 2, [128, 512], BF)
        tfix = es.enter_context(_sbt(nc, "tfix", [128, 16], F32))
        tfb = Buf()
        for i in range(3):
            R.op("pool", lambda e, t=lev[i]: e.memset(t[:, 0:PADC], 0.0), writes=[levb[i]])
            R.op("pool", lambda e, t=slev[i]: e.memset(t[:], 0.0), writes=[slevb[i]])
        for g in range(4):
            w = 2 << g
            nst = g + 1
            R.dma("pool", wp[:], I["pool_w"][g].rearrange("(kt p) m -> p kt m", p=128), writes=[wpb])
            dts = []
            for kt2 in range(2):
                ct = 2 * g + kt2
                R.dma("sp", lev[0][:, PADC:PADC + NPR], U[ct * 128:(ct + 1) * 128, 0:NPR],
                      reads=[xbuf(C, "U", ct, 0)], writes=[levb[0]])
                R.dma("sp", slev[0][:, :, 1:16], I["pool_hist"][:, ct, :, :], writes=[slevb[0]])
                R.dma("sp", slev[0][:, :, 16:32],
                      U[ct * 128:(ct + 1) * 128, SC0:SC0 + NSQ * SQL].rearrange("p (s t) -> p s t", t=SQL),
                      reads=[xbuf(C, "U", ct, 0)], writes=[slevb[0]])
                src = 0
                srcb = levb[0]
                ssrc = 0
                for s in range(nst):
                    sh = 1 << s
                    dst = 1 if src != 1 else 2
                    R.op("dve", lambda e, o=lev[dst][:, PADC:NCOL], a=lev[src][:, PADC:NCOL], b=lev[src][:, PADC - sh:NCOL - sh]:
                         e.tensor_tensor(out=o, in0=a, in1=b, op=ALU.add),
                         reads=[levb[src]], writes=[levb[dst]])
                    R.op("pool", lambda e, o=slev[dst][:, :, sh:32], a=slev[ssrc][:, :, sh:32], b=slev[ssrc][:, :, 0:32 - sh]:
                         e.tensor_tensor(out=o, in0=a, in1=b, op=ALU.add),
                         reads=[slevb[ssrc]], writes=[slevb[dst]])
                    src = dst
                    ssrc = dst
                df, dfb = dfr.next()
                dts.append((df, dfb))
                R.op("dve", lambda e, o=df[:, 0:NPR], a=lev[src][:, PADC:NCOL], b=lev[0][:, PADC:NCOL]:
                     e.scalar_tensor_tensor(out=o, in0=a, scalar=1.0 / w, in1=b, op0=ALU.mult, op1=ALU.subtract),
                     reads=[levb[src], levb[0]], writes=[dfb])
                R.op("dve", lambda e, a=lev[src][:, PADC:PADC + 16], b=C.invcnt[:, g, :]:
                     e.tensor_tensor(out=tfix[:], in0=a, in1=b, op=ALU.mult),
                     reads=[levb[src]], writes=[tfb])
                R.op("dve", lambda e, o=df[:, 0:16], b=lev[0][:, PADC:PADC + 16]:
                     e.tensor_tensor(out=o, in0=tfix[:], in1=b, op=ALU.subtract),
                     reads=[tfb, levb[0]], writes=[dfb])
                R.op("dve", lambda e, o=df[:, SC0:SC0 + NSQ * SQL].rearrange("p (s t) -> p s t", t=SQL), a=slev[ssrc][:, :, 16:32], b=slev[0][:, :, 16:32]:
                     e.scalar_tensor_tensor(out=o, in0=a, scalar=1.0 / w, in1=b, op0=ALU.mult, op1=ALU.subtract),
                     reads=[slevb[ssrc], slevb[0]], writes=[dfb])
                R.op("pool", lambda e, o=df[:, SC0 + NSQ * SQL:RP]: e.memset(o, 0.0), writes=[dfb])
            for m in range(2):
                oc = 2 * g + m
                for cc in range(0, RP, 512):
                    n = min(512, RP - cc)
                    p, pb = pr.next()
                    for kt2 in range(2):
                        R.op("pe", lambda e, o=p[:, 0:n], l=wp[:, kt2, m * 128:(m + 1) * 128], r=dts[kt2][0][:, cc:cc + n], s=(kt2 == 0), q=(kt2 == 1):
                             e.matmul(o, lhsT=l, rhs=r, start=s, stop=q),
                             reads=[wpb, dts[kt2][1]], writes=[pb], inc=(kt2 == 1))
                    ot, ob = orr.next()
                    R.op("act", lambda e, o=ot[:, 0:n], i=p[:, 0:n], sc=C.pscale[:, oc:oc + 1]:
                         e.activation(out=o, in_=i, func=AF.Copy, scale=sc),
                         reads=[pb], writes=[ob])
                    R.dma("sp", YP[oc * 128:(oc + 1) * 128, cc:cc + n], ot[:, 0:n], reads=[ob],
                          writes=[xbuf(C, "YP", oc, 0)])
        R.flush()
    R.barrier()


def ssm_stage(R, nc, C, U, Z, I, O):
    L = LCH
    HT = 16
    with ExitStack() as es:
        def sb(name, shape, dt=F32):
            return es.enter_context(_sbt(nc, name, shape, dt))
        are = sb("are", [128, 32]); aim = sb("aim", [128, 32]); ldt = sb("ldt", [128, 32])
        t0 = sb("t0", [128, 32]); t1 = sb("t1", [128, 32]); t2 = sb("t2", [128, 32]); t3 = sb("t3", [128, 32])
        cs = sb("cs", [128, 32]); sn = sb("sn", [128, 32]); mag = sb("mag", [128, 32])
        cre = sb("cre", [128, 32]); cim = sb("cim", [128, 32])
        Ec = sb("Ec", [128, 32, L]); Es = sb("Es", [128, 32, L])
        nEs = sb("nEs", [128, 32, L]); nEc = sb("nEc", [128, 32, L])
        TA = sb("TA", [128, 32, L]); TB = sb("TB", [128, 32, L])
        Rb = sb("Rb", [128, 32, L])
        ELc = sb("ELc", [128, 32]); ELs = sb("ELs", [128, 32])
        one = Buf()
        Bre = sb("Bre", [128, 32, 128], BF); Bim = sb("Bim", [128, 32, 128], BF)
        Cre = sb("Cre", [128, 32, 128], BF); Cim = sb("Cim", [128, 32, 128], BF)
        wb_ = Buf()
        R.dma("sp", are[:], I["ssm_are"], writes=[one])
        R.dma("sp", aim[:], I["ssm_aim"], writes=[one])
        R.dma("sp", ldt[:], I["ssm_ldt"], writes=[one])
        R.dma("pool", Bre[:], I["ssm_Bre"], writes=[wb_])
        R.dma("pool", Bim[:], I["ssm_Bim"], writes=[wb_])
        R.dma("pool", Cre[:], I["ssm_Cre"], writes=[wb_])
        R.dma("pool", Cim[:], I["ssm_Cim"], writes=[wb_])

        def A(fn):
            R.op("act", fn, reads=[one], writes=[one])

        def V(fn):
            R.op("dve", fn, reads=[one], writes=[one])
        A(lambda e: e.activation(out=t0[:], in_=ldt[:], func=AF.Exp))
        V(lambda e: e.tensor_tensor(out=t1[:], in0=are[:], in1=t0[:], op=ALU.mult))
        V(lambda e: e.tensor_tensor(out=t2[:], in0=aim[:], in1=t0[:], op=ALU.mult))
        A(lambda e: e.activation(out=mag[:], in_=t1[:], func=AF.Exp))
        A(lambda e: e.activation(out=sn[:], in_=t2[:], func=AF.Sin, scale=1.0 / 16))
        A(lambda e: e.activation(out=cs[:], in_=t2[:], func=AF.Sin, scale=-1.0 / 16, bias=C.halfpi[:, 0:1]))
        for _ in range(4):
            V(lambda e: e.tensor_tensor(out=t0[:], in0=cs[:], in1=cs[:], op=ALU.mult))
            V(lambda e: e.tensor_tensor(out=t1[:], in0=sn[:], in1=sn[:], op=ALU.mult))
            V(lambda e: e.tensor_tensor(out=t3[:], in0=cs[:], in1=sn[:], op=ALU.mult))
            V(lambda e: e.tensor_tensor(out=cs[:], in0=t0[:], in1=t1[:], op=ALU.subtract))
            V(lambda e: e.tensor_scalar(out=sn[:], in0=t3[:], scalar1=2.0, scalar2=None, op0=ALU.mult))
        V(lambda e: e.tensor_tensor(out=t0[:], in0=mag[:], in1=cs[:], op=ALU.mult))
        V(lambda e: e.tensor_tensor(out=t1[:], in0=mag[:], in1=sn[:], op=ALU.mult))
        V(lambda e: e.tensor_scalar(out=t0[:], in0=t0[:], scalar1=-1.0, scalar2=None, op0=ALU.add))
        V(lambda e: e.tensor_tensor(out=t2[:], in0=are[:], in1=are[:], op=ALU.mult))
        V(lambda e: e.tensor_tensor(out=t3[:], in0=aim[:], in1=aim[:], op=ALU.mult))
        V(lambda e: e.tensor_tensor(out=t2[:], in0=t2[:], in1=t3[:], op=ALU.add))
        V(lambda e: e.reciprocal(out=t2[:], in_=t2[:]))
        V(lambda e: e.tensor_tensor(out=cre[:], in0=t0[:], in1=are[:], op=ALU.mult))
        V(lambda e: e.tensor_tensor(out=t3[:], in0=t1[:], in1=aim[:], op=ALU.mult))
        V(lambda e: e.tensor_tensor(out=cre[:], in0=cre[:], in1=t3[:], op=ALU.add))
        V(lambda e: e.tensor_tensor(out=cre[:], in0=cre[:], in1=t2[:], op=ALU.mult))
        V(lambda e: e.tensor_tensor(out=cim[:], in0=t1[:], in1=are[:], op=ALU.mult))
        V(lambda e: e.tensor_tensor(out=t3[:], in0=t0[:], in1=aim[:], op=ALU.mult))
        V(lambda e: e.tensor_tensor(out=cim[:], in0=cim[:], in1=t3[:], op=ALU.subtract))
        V(lambda e: e.tensor_tensor(out=cim[:], in0=cim[:], in1=t2[:], op=ALU.mult))
        V(lambda e: e.tensor_copy(out=Ec[:, :, 0:1], in_=cs[:].unsqueeze(2)))
        V(lambda e: e.tensor_copy(out=Es[:, :, 0:1], in_=sn[:].unsqueeze(2)))
        m = 1
        while m < L:
            bc = Ec[:, :, m - 1:m].broadcast_to([128, 32, m])
            bs = Es[:, :, m - 1:m].broadcast_to([128, 32, m])
            V(lambda e, m=m, bc=bc: e.tensor_tensor(out=TA[:, :, 0:m], in0=Ec[:, :, 0:m], in1=bc, op=ALU.mult))
            V(lambda e, m=m, bs=bs: e.tensor_tensor(out=TB[:, :, 0:m], in0=Es[:, :, 0:m], in1=bs, op=ALU.mult))
            V(lambda e, m=m: e.tensor_tensor(out=Ec[:, :, m:2 * m], in0=TA[:, :, 0:m], in1=TB[:, :, 0:m], op=ALU.subtract))
            V(lambda e, m=m, bs=bs: e.tensor_tensor(out=TA[:, :, 0:m], in0=Ec[:, :, 0:m], in1=bs, op=ALU.mult))
            V(lambda e, m=m, bc=bc: e.tensor_tensor(out=TB[:, :, 0:m], in0=Es[:, :, 0:m], in1=bc, op=ALU.mult))
            V(lambda e, m=m: e.tensor_tensor(out=Es[:, :, m:2 * m], in0=TA[:, :, 0:m], in1=TB[:, :, 0:m], op=ALU.add))
            m *= 2
        V(lambda e: e.tensor_scalar(out=nEs[:], in0=Es[:], scalar1=-1.0, scalar2=None, op0=ALU.mult))
        V(lambda e: e.tensor_scalar(out=nEc[:], in0=Ec[:], scalar1=-1.0, scalar2=None, op0=ALU.mult))
        crb = cre[:].unsqueeze(2).broadcast_to([128, 32, L])
        cib = cim[:].unsqueeze(2).broadcast_to([128, 32, L])
        V(lambda e: e.tensor_tensor(out=TA[:], in0=Ec[:], in1=crb, op=ALU.mult))
        V(lambda e: e.tensor_tensor(out=Rb[:], in0=Es[:], in1=cib, op=ALU.mult))
        V(lambda e: e.tensor_tensor(out=TA[:], in0=TA[:], in1=Rb[:], op=ALU.add))
        V(lambda e: e.tensor_tensor(out=TB[:], in0=Ec[:], in1=cib, op=ALU.mult))
        V(lambda e: e.tensor_tensor(out=Rb[:], in0=Es[:], in1=crb, op=ALU.mult))
        V(lambda e: e.tensor_tensor(out=TB[:], in0=TB[:], in1=Rb[:], op=ALU.subtract))
        V(lambda e: e.tensor_copy(out=Rb[:], in_=mag[:].unsqueeze(2).broadcast_to([128, 32, L])))
        tabs = one

        ur = Ring(es, nc, "ubf", 2, [128, 8, L], BF)
        u32 = Ring(es, nc, "u32", 2, [128, 8, L], F32)
        Pre = es.enter_context(_pst(nc, "Pre", [128, HT, L], F32)); Preb = Buf()
        Pim = es.enter_context(_pst(nc, "Pim", [128, HT, L], F32)); Pimb = Buf()
        Yp = Ring(es, nc, "Yp", 2, [128, 8, L], F32, psum=True)
        m1 = sb("m1", [128, HT, L]); m2 = sb("m2", [128, HT, L])
        cr_ = sb("cr_", [128, HT, L]); ci_ = sb("ci_", [128, HT, L])
        kr = sb("kr", [128, 32, L]); ki = sb("ki", [128, 32, L])
        m1b, m2b, crb_, cib_ = Buf(), Buf(), Buf(), Buf()
        krb = [Buf(), Buf()]; kib = [Buf(), Buf()]
        q1 = Ring(es, nc, "q1", 2, [128, 32, L], BF); q2 = Ring(es, nc, "q2", 2, [128, 32, L], BF)
        q3 = Ring(es, nc, "q3", 2, [128, 32, L], BF); q4 = Ring(es, nc, "q4", 2, [128, 32, L], BF)
        hre = sb("hre", [128, 32]); him = sb("him", [128, 32]); hb_ = Buf()
        hta = sb("hta", [128, 32]); htb = sb("htb", [128, 32])
        ysr = Ring(es, nc, "ys", 2, [128, 8, L], F32)
        g1 = Ring(es, nc, "g1", 2, [128, 8, L], F32)
        g2 = Ring(es, nc, "g2", 2, [128, 8, L], F32)
        zr = Ring(es, nc, "zr", 2, [128, 8, L], BF)

        zpad = sb("zpad", [128, 8, RP - SC0 - NSQ * SQL], BF)
        zpb = Buf()
        R.op("pool", lambda e: e.memset(zpad[:], 0.0), writes=[zpb])
        R.dma("sp", Z[:, SC0 + NSQ * SQL:RP].rearrange("(ct p) c -> p ct c", p=128), zpad[:], reads=[zpb], writes=[xbuf(C, "Zpad", 0, 0)])
        seqs = [(0, NPR, None)] + [(SC0 + s * SQL, SQL, s) for s in range(NSQ)]
        for (q0, qlen, sidx) in seqs:
            if sidx is None:
                R.op("dve", lambda e: e.memset(hre[:], 0.0), writes=[hb_])
                R.op("dve", lambda e: e.memset(him[:], 0.0), writes=[hb_])
            else:
                R.dma("sp", hre[:], I["ssm_h0re"][sidx], writes=[hb_])
                R.dma("sp", him[:], I["ssm_h0im"][sidx], writes=[hb_])
            for t0_ in range(0, qlen, L):
                n = min(L, qlen - t0_)
                col = q0 + t0_
                ub, ubb = ur.next()
                uf, ufb = u32.next()
                R.dma("pool", ub[:, :, 0:n], U[1024:2048, col:col + n].rearrange("(ct p) c -> p ct c", p=128),
                      reads=[xbuf(C, "U", 8, 0)], writes=[ubb])
                R.dma("sp", uf[:, :, 0:n], U[1024:2048, col:col + n].rearrange("(ct p) c -> p ct c", p=128),
                      reads=[xbuf(C, "U", 8, 0)], writes=[ufb])
                for hf in range(2):
                    for j in range(HT):
                        st = hf * HT + j
                        R.op("pe", lambda e, o=Pre[:, j, 0:n], l=Bre[:, st, :], r=ub[:, st // 4, 0:n]:
                             e.matmul(o, lhsT=l, rhs=r, start=True, stop=True),
                             reads=[wb_, ubb], writes=[Preb], inc=False)
                        R.op("pe", lambda e, o=Pim[:, j, 0:n], l=Bim[:, st, :], r=ub[:, st // 4, 0:n]:
                             e.matmul(o, lhsT=l, rhs=r, start=True, stop=True),
                             reads=[wb_, ubb], writes=[Pimb], inc=(j == HT - 1))
                    sl = slice(hf * HT, (hf + 1) * HT)
                    R.op("dve", lambda e, sl=sl: e.tensor_tensor(out=m1[:, :, 0:n], in0=Pre[:, :, 0:n], in1=TA[:, sl, 0:n], op=ALU.mult),
                         reads=[Preb, tabs], writes=[m1b])
                    R.op("dve", lambda e, sl=sl: e.tensor_tensor(out=m2[:, :, 0:n], in0=Pim[:, :, 0:n], in1=TB[:, sl, 0:n], op=ALU.mult),
                         reads=[Pimb, tabs], writes=[m2b])
                    R.op("dve", lambda e: e.tensor_tensor(out=cr_[:, :, 0:n], in0=m1[:, :, 0:n], in1=m2[:, :, 0:n], op=ALU.subtract),
                         reads=[m1b, m2b], writes=[crb_])
                    R.op("dve", lambda e, sl=sl: e.tensor_tensor(out=m1[:, :, 0:n], in0=Pre[:, :, 0:n], in1=TB[:, sl, 0:n], op=ALU.mult),
                         reads=[Preb, tabs], writes=[m1b])
                    R.op("dve", lambda e, sl=sl: e.tensor_tensor(out=m2[:, :, 0:n], in0=Pim[:, :, 0:n], in1=TA[:, sl, 0:n], op=ALU.mult),
                         reads=[Pimb, tabs], writes=[m2b])
                    R.op("dve", lambda e: e.tensor_tensor(out=ci_[:, :, 0:n], in0=m1[:, :, 0:n], in1=m2[:, :, 0:n], op=ALU.add),
                         reads=[m1b, m2b], writes=[cib_])
                    for j in range(HT):
                        st = hf * HT + j
                        R.op("dve", lambda e, st=st, j=j: e.tensor_tensor_scan(out=kr[:, st, 0:n], data0=Rb[:, st, 0:n], data1=cr_[:, j, 0:n],
                                                                             initial=hre[:, st:st + 1], op0=ALU.mult, op1=ALU.add),
                             reads=[crb_, hb_, tabs], writes=[krb[hf]])
                        R.op("dve", lambda e, st=st, j=j: e.tensor_tensor_scan(out=ki[:, st, 0:n], data0=Rb[:, st, 0:n], data1=ci_[:, j, 0:n],
                                                                             initial=him[:, st:st + 1], op0=ALU.mult, op1=ALU.add),
                             reads=[cib_, hb_, tabs], writes=[kib[hf]])
                R.op("dve", lambda e: e.tensor_tensor(out=hta[:].unsqueeze(2), in0=kr[:, :, n - 1:n], in1=Ec[:, :, n - 1:n], op=ALU.mult),
                     reads=[krb[0], krb[1], tabs], writes=[one])
                R.op("dve", lambda e: e.tensor_tensor(out=htb[:].unsqueeze(2), in0=ki[:, :, n - 1:n], in1=Es[:, :, n - 1:n], op=ALU.mult),
                     reads=[kib[0], kib[1], tabs], writes=[one])
                R.op("dve", lambda e: e.tensor_tensor(out=hre[:], in0=hta[:], in1=htb[:], op=ALU.subtract),
                     reads=[one], writes=[hb_])
                R.op("dve", lambda e: e.tensor_tensor(out=hta[:].unsqueeze(2), in0=kr[:, :, n - 1:n], in1=Es[:, :, n - 1:n], op=ALU.mult),
                     reads=[krb[0], krb[1], tabs], writes=[one])
                R.op("dve", lambda e: e.tensor_tensor(out=htb[:].unsqueeze(2), in0=ki[:, :, n - 1:n], in1=Ec[:, :, n - 1:n], op=ALU.mult),
                     reads=[kib[0], kib[1], tabs], writes=[one])
                R.op("dve", lambda e: e.tensor_tensor(out=him[:], in0=hta[:], in1=htb[:], op=ALU.add),
                     reads=[one], writes=[hb_])
                p1, p1b = q1.next(); p2, p2b = q2.next(); p3, p3b = q3.next(); p4, p4b = q4.next()
                R.op("pool", lambda e, o=p1: e.tensor_tensor(out=o[:, :, 0:n], in0=kr[:, :, 0:n], in1=Ec[:, :, 0:n], op=ALU.mult),
                     reads=[krb[0], krb[1], tabs], writes=[p1b])
                R.op("pool", lambda e, o=p2: e.tensor_tensor(out=o[:, :, 0:n], in0=ki[:, :, 0:n], in1=nEs[:, :, 0:n], op=ALU.mult),
                     reads=[kib[0], kib[1], tabs], writes=[p2b])
                R.op("pool", lambda e, o=p3: e.tensor_tensor(out=o[:, :, 0:n], in0=kr[:, :, 0:n], in1=nEs[:, :, 0:n], op=ALU.mult),
                     reads=[krb[0], krb[1], tabs], writes=[p3b])
                R.op("pool", lambda e, o=p4: e.tensor_tensor(out=o[:, :, 0:n], in0=ki[:, :, 0:n], in1=nEc[:, :, 0:n], op=ALU.mult),
                     reads=[kib[0], kib[1], tabs], writes=[p4b])
                yp, ypb = Yp.next()
                first = True
                for ct in range(8):
                    for jj in range(4):
                        st = ct * 4 + jj
                        for (pp, ppb, Wt) in ((p1, p1b, Cre), (p2, p2b, Cre), (p3, p3b, Cim), (p4, p4b, Cim)):
                            last = (ct == 7 and jj == 3 and pp is p4)
                            R.op("pe", lambda e, o=yp[:, ct, 0:n], l=Wt[:, st, :], r=pp[:, st, 0:n], s=first:
                                 e.matmul(o, lhsT=l, rhs=r, start=s, stop=False, skip_group_check=True),
                                 reads=[wb_, ppb], writes=[ypb], inc=last)
                            first = False
                ys, ysb = ysr.next()
                for ct in range(8):
                    R.op("dve", lambda e, ct=ct, ys=ys, uf=uf, yp=yp: e.scalar_tensor_tensor(out=ys[:, ct, 0:n], in0=uf[:, ct, 0:n], scalar=C.ssmd[:, ct:ct + 1],
                                                                        in1=yp[:, ct, 0:n], op0=ALU.mult, op1=ALU.add),
                         reads=[ufb, ypb], writes=[ysb])
                a1, a1b = g1.next(); a2, a2b = g2.next()
                R.op("act", lambda e, a1=a1, ys=ys: e.activation(out=a1[:, :, 0:n], in_=ys[:, :, 0:n], func=AF.Square), reads=[ysb], writes=[a1b])
                R.op("dve", lambda e, a1=a1, a2=a2: e.tensor_scalar(out=a2[:, :, 0:n], in0=a1[:, :, 0:n], scalar1=0.044715, scalar2=1.0, op0=ALU.mult, op1=ALU.add),
                     reads=[a1b], writes=[a2b])
                R.op("dve", lambda e, a1=a1, a2=a2, ys=ys: e.tensor_tensor(out=a1[:, :, 0:n], in0=a2[:, :, 0:n], in1=ys[:, :, 0:n], op=ALU.mult),
                     reads=[a2b, ysb], writes=[a1b])
                R.op("act", lambda e, a1=a1, a2=a2: e.activation(out=a2[:, :, 0:n], in_=a1[:, :, 0:n], func=AF.Sigmoid, scale=2.0 * 0.7978845608028654),
                     reads=[a1b], writes=[a2b])
                zt, ztb = zr.next()
                R.op("dve", lambda e, zt=zt, a2=a2, ys=ys: e.tensor_tensor(out=zt[:, :, 0:n], in0=a2[:, :, 0:n], in1=ys[:, :, 0:n], op=ALU.mult),
                     reads=[a2b, ysb], writes=[ztb])
                R.dma("sp", Z[:, col:col + n].rearrange("(ct p) c -> p ct c", p=128), zt[:, :, 0:n], reads=[ztb],
                      writes=[xbuf(C, "Z", 0, 0)])
            if sidx is None:
                R.dma("sp", O["hre_p"], hre[:], reads=[hb_], writes=[xbuf(C, "hst", 0, 0)])
                R.dma("sp", O["him_p"], him[:], reads=[hb_], writes=[xbuf(C, "hst", 1, 0)])
            else:
                R.dma("sp", O["hre_s"][sidx], hre[:], reads=[hb_], writes=[xbuf(C, "hst", 2, sidx)])
                R.dma("sp", O["him_s"][sidx], him[:], reads=[hb_], writes=[xbuf(C, "hst", 3, sidx)])
        R.flush()
    R.barrier()


def glu_stage(R, nc, C, Z, YP, I):
    wv = I["ssm_w_glu"].rearrange("(kt p) m -> p kt m", p=128)
    with ExitStack() as es:
        zf = es.enter_context(_sbt(nc, "zf", [128, 8, RP], BF))
        zb = Buf()
        wr = Ring(es, nc, "wg", 2, [128, 8, 128], BF)
        pr = Ring(es, nc, "pgl", 3, [128, 512], F32, psum=True)
        sr = Ring(es, nc, "sgl", 2, [128, 512], F32)
        orr = Ring(es, nc, "ogl", 2, [128, 512], BF)
        for ct in range(8):
            R.dma("sp", zf[:, ct, :], Z[ct * 128:(ct + 1) * 128, :], reads=[xbuf(C, "Z", 0, 0)], writes=[zb])
        for mc in range(8):
            wt, wb = wr.next()
            R.dma("pool", wt[:], wv[:, :, mc * 128:(mc + 1) * 128], writes=[wb])
            for cc in range(0, RP, 512):
                n = min(512, RP - cc)
                p, pb = pr.next()
                for kt in range(8):
                    R.op("pe", lambda e, o=p[:, 0:n], l=wt[:, kt, :], r=zf[:, kt, cc:cc + n], s=(kt == 0), q=(kt == 7):
                         e.matmul(o, lhsT=l, rhs=r, start=s, stop=q), reads=[wb, zb], writes=[pb], inc=(kt == 7))
                st, sb_ = sr.next()
                R.op("act", lambda e, o=st[:, 0:n], i=p[:, 0:n], b=C.bglu[:, mc:mc + 1]:
                     e.activation(out=o, in_=i, func=AF.Sigmoid, bias=b), reads=[pb], writes=[sb_])
                ot, ob = orr.next()
                R.op("dve", lambda e, o=ot[:, 0:n], a=st[:, 0:n], b=zf[:, mc, cc:cc + n]:
                     e.tensor_tensor(out=o, in0=a, in1=b, op=ALU.mult), reads=[sb_, zb], writes=[ob])
                R.dma("sp", YP[(8 + mc) * 128:(9 + mc) * 128, cc:cc + n], ot[:, 0:n], reads=[ob],
                      writes=[xbuf(C, "YP", 8 + mc, 0)])
        R.flush()
    R.barrier()


def attn_core(R, nc, C, A, groups, kblocks, ncols, diag):
    blks = []
    for bi in range(len(kblocks) - 1, -1, -1):
        k0, nk = kblocks[bi]
        cstart, mask = diag(k0, nk)
        if cstart < ncols:
            blks.append((k0, nk, cstart, mask))
    n = len(blks)
    st = [dict() for _ in range(n)]
    Rs = []
    for _ in range(3):
        rt, rb = A.rr.next()
        R.op("pool", lambda e, o=rt: e.memset(o[:, 0:ncols], 0.0), writes=[rb])
        Rs.append((rt, rb))
    firstpv = [True]

    def P1(i):
        k0, nk, cs, mask = blks[i]
        ps, psb = A.psr.next()
        for (gc0, gn, qT, kTf, vf, gdeps) in groups:
            lo = max(gc0, cs)
            if lo >= gc0 + gn:
                continue
            R.op("pe", lambda e, o=ps[0:nk, lo:gc0 + gn], l=kTf(k0, nk), r=qT[:, lo - gc0:gn]:
                 e.matmul(o, lhsT=l, rhs=r, start=True, stop=True), reads=gdeps, writes=[psb], inc=True)
        st[i]["ps"] = (ps, psb)

    def A1(i):
        k0, nk, cs, mask = blks[i]
        ps, psb = st[i]["ps"]
        et, eb = A.er.next()
        R.op("act", lambda e, o=et[0:nk, cs:ncols], i_=ps[0:nk, cs:ncols]:
             e.activation(out=o, in_=i_, func=AF.Exp, scale=A.scale), reads=[psb], writes=[eb])
        spt, spb = A.spr.next()
        R.op("act", lambda e, o=spt[0:nk, cs:ncols], i_=et[0:nk, cs:ncols], b=C.oneb[0:nk, 0:1]:
             e.activation(out=o, in_=i_, func=AF.Ln, bias=b), reads=[eb], writes=[spb])
        if mask is not None:
            for (mc0, mn, map_) in mask:
                R.op("dve", lambda e, o=spt[0:nk, mc0:mc0 + mn], m_=map_:
                     e.tensor_tensor(out=o, in0=o, in1=m_, op=ALU.mult), reads=[spb], writes=[spb])
        st[i]["e"] = (et, eb)
        st[i]["sp"] = (spt, spb)

    def D1(i):
        if i == n - 1:
            return
        k0, nk, cs, mask = blks[i]
        spt, spb = st[i]["sp"]
        rp, rpb = Rs[i % 3]
        rn, rnb = Rs[(i + 1) % 3]
        R.op("dve", lambda e, o=rn[0:nk, cs:ncols], a=spt[0:nk, cs:ncols], b=rp[0:nk, cs:ncols]:
             e.tensor_tensor(out=o, in0=a, in1=b, op=ALU.add), reads=[spb, rpb], writes=[rnb])

    def P2(i):
        k0, nk, cs, mask = blks[i]
        spt, spb = st[i]["sp"]
        pc, pcb = A.pcr.next()
        R.op("pe", lambda e, o=pc[0:nk, cs:ncols], l=C.LTb[0:nk, 0:nk], r=spt[0:nk, cs:ncols], s_=(i == 0):
             e.matmul(o, lhsT=l, rhs=r, start=True, stop=s_), reads=[spb], writes=[pcb], inc=(i == 0))
        if i > 0:
            rp, rpb = Rs[i % 3]
            R.op("pe", lambda e, o=pc[0:nk, cs:ncols], l=C.onesb[0:128, 0:nk], r=rp[0:128, cs:ncols]:
                 e.matmul(o, lhsT=l, rhs=r, start=False, stop=True, skip_group_check=True), reads=[rpb], writes=[pcb], inc=True)
        st[i]["pc"] = (pc, pcb)

    def A2(i):
        k0, nk, cs, mask = blks[i]
        pc, pcb = st[i]["pc"]
        rt2, r2b = A.xr.next()
        R.op("act", lambda e, o=rt2[0:nk, cs:ncols], i_=pc[0:nk, cs:ncols]:
             e.activation(out=o, in_=i_, func=AF.Exp, scale=-1.0), reads=[pcb], writes=[r2b])
        st[i]["r"] = (rt2, r2b)

    def D2(i):
        k0, nk, cs, mask = blks[i]
        et, eb = st[i]["e"]
        rt2, r2b = st[i]["r"]
        wt, wb = A.wr.next()
        R.op("dve", lambda e, o=wt[0:nk, cs:ncols], a=et[0:nk, cs:ncols], b=rt2[0:nk, cs:ncols]:
             e.tensor_tensor(out=o, in0=a, in1=b, op=ALU.mult), reads=[eb, r2b], writes=[wb])
        if mask is not None:
            for (mc0, mn, map_) in mask:
                R.op("dve", lambda e, o=wt[0:nk, mc0:mc0 + mn], m_=map_:
                     e.tensor_tensor(out=o, in0=o, in1=m_, op=ALU.mult), reads=[wb], writes=[wb])
        st[i]["w"] = (wt, wb)

    def P3(i):
        k0, nk, cs, mask = blks[i]
        wt, wb = st[i]["w"]
        for (gc0, gn, qT, kTf, vf, gdeps) in groups:
            lo = max(gc0, cs)
            if lo >= gc0 + gn:
                continue
            R.op("pe", lambda e, o=A.po[:, lo:gc0 + gn], l=vf(k0, nk), r=wt[0:nk, lo:gc0 + gn], s_=firstpv[0]:
                 e.matmul(o, lhsT=l, rhs=r, start=s_, stop=False, skip_group_check=True),
                 reads=[wb] + list(gdeps), writes=[A.pob], inc=True)
            firstpv[0] = False

    for step in range(n + 2):
        if step < n:
            P1(step)
        if 0 <= step - 1 < n:
            P2(step - 1)
        if 0 <= step - 2 < n:
            P3(step - 2)
        if step < n:
            A1(step)
            D1(step)
        if 0 <= step - 1 < n:
            A2(step - 1)
            D2(step - 1)


def alloc_attn(es, nc, A):
    A.psr = Ring(es, nc, "aps", 2, [128, 512], F32, psum=True)
    A.pcr = Ring(es, nc, "apc", 2, [128, 512], F32, psum=True)
    A.po = es.enter_context(_pst(nc, "apo", [128, 512], F32))
    A.pob = Buf()
    A.er = Ring(es, nc, "ae", 4, [128, 512], F32)
    A.spr = Ring(es, nc, "asp", 4, [128, 512], BF)
    A.rr = Ring(es, nc, "arr", 3, [128, 512], BF)
    A.xr = Ring(es, nc, "axr", 3, [128, 512], F32)
    A.wr = Ring(es, nc, "awr", 4, [128, 512], BF)
    A.scale = 1.0 / math.sqrt(128.0)


def attn_prompt_stage(R, nc, C, Q, K, V, OUT):
    with ExitStack() as es:
        A = Ctx()
        alloc_attn(es, nc, A)
        qr = Ring(es, nc, "aq", 2, [128, NPR], BF)
        kr = Ring(es, nc, "ak", 2, [128, NPR], BF)
        vfr = Ring(es, nc, "avf", 2, [128, NPR], BF)
        vtr = Ring(es, nc, "avt", 2, [128, 33, 128], BF)
        ptr = Ring(es, nc, "apt", 2, [128, 512], BF, psum=True)
        oo = Ring(es, nc, "aoo", 2, [128, 512], BF)
        kblocks = [(k0, min(128, NPR - k0)) for k0 in range(0, NPR, 128)]
        for h in range(NHEAD):
            qt, qb = qr.next(); kt, kb = kr.next(); vf, vfb = vfr.next(); vt, vtb = vtr.next()
            R.dma("sp", qt[:], Q[h * 128:(h + 1) * 128, 0:NPR], reads=[xbuf(C, "Q", h, 0)], writes=[qb])
            R.dma("sp", kt[:], K[h * 128:(h + 1) * 128, 0:NPR], reads=[xbuf(C, "K", h, 0)], writes=[kb])
            R.dma("sp", vf[:], V[h * 128:(h + 1) * 128, 0:NPR], reads=[xbuf(C, "V", h, 0)], writes=[vfb])
            for b0 in range(0, len(kblocks), 4):
                pt, ptb = ptr.next()
                blks = kblocks[b0:b0 + 4]
                for j, (k0, nk) in enumerate(blks):
                    R.op("pe", lambda e, o=pt[0:nk, j * 128:(j + 1) * 128], i=vf[:, k0:k0 + nk]:
                         e.transpose(o, i, C.identb[:]), reads=[vfb], writes=[ptb], inc=(j == len(blks) - 1))
                for j, (k0, nk) in enumerate(blks):
                    R.op("dve", lambda e, o=vt[0:nk, b0 + j, :], i=pt[0:nk, j * 128:(j + 1) * 128]:
                         e.tensor_copy(out=o, in_=i), reads=[ptb], writes=[vtb])
            for q0 in range(0, NPR, 512):
                nq = min(512, NPR - q0)
                kb_list = [kbk for kbk in kblocks if kbk[0] < q0 + nq]

                def diag(k0, nk, q0=q0, nq=nq):
                    if k0 + nk <= q0:
                        return 0, None
                    cs = k0 - q0
                    mn = min(nk, nq - cs)
                    return cs, [(cs, mn, C.trib[0:nk, 0:mn])]
                groups = [(0, nq, qt[:, q0:q0 + nq],
                           (lambda k0, nk, kt=kt: kt[:, k0:k0 + nk]),
                           (lambda k0, nk, vt=vt: vt[0:nk, k0 // 128, :]),
                           [qb, kb, vtb])]
                attn_core(R, nc, C, A, groups, kb_list, nq, diag)
                ot, ob = oo.next()
                R.op("act", lambda e, o=ot[:, 0:nq], i=A.po[:, 0:nq]: e.activation(out=o, in_=i, func=AF.Copy),
                     reads=[A.pob], writes=[ob])
                R.dma("sp", OUT[h * 128:(h + 1) * 128, q0:q0 + nq], ot[:, 0:nq], reads=[ob],
                      writes=[xbuf(C, "YP", h, 0)])
        R.flush()
    R.barrier()


def attn_sample_stage(R, nc, C, Q, K, V, OUT, I):
    HG = 4
    NK = PAST + SQL
    with ExitStack() as es:
        A = Ctx()
        alloc_attn(es, nc, A)
        kr = Ring(es, nc, "sk", 2, [128, HG, NK], BF)
        vr = Ring(es, nc, "sv", 2, [128, 32, HG * 128], BF)
        qr = Ring(es, nc, "sq", 2, [128, HG, SQL], BF)
        vnf = Ring(es, nc, "svn", 2, [128, HG, SQL], BF)
        vnt = Ring(es, nc, "svt", 2, [SQL, HG, 128], BF)
        ptr = Ring(es, nc, "spt", 2, [128, 512], BF, psum=True)
        oo = Ring(es, nc, "soo", 2, [128, HG * SQL], BF)
        kblocks = [(k0, 128) for k0 in range(0, PAST, 128)] + [(PAST, SQL)]
        for s in range(NSQ):
            col = SC0 + s * SQL
            for hg in range(16 // HG):
                kt, kb = kr.next(); vt, vb = vr.next(); qt, qb = qr.next(); vn, vnb = vnf.next(); vtt, vttb = vnt.next()
                h0 = hg * HG
                R.dma("pool", kt[:, :, 0:PAST], I["cache_kT"][s, h0:h0 + HG].rearrange("h d k -> d h k"), writes=[kb])
                R.dma("sp", kt[:, :, PAST:NK], K[h0 * 128:(h0 + HG) * 128, col:col + SQL].rearrange("(h d) c -> d h c", d=128),
                      reads=[xbuf(C, "K", 0, 0)], writes=[kb])
                R.dma("pool", vt[:], I["cache_v"][s][:, h0 * 128:(h0 + HG) * 128].rearrange("(b p) c -> p b c", p=128), writes=[vb])
                R.dma("sp", qt[:], Q[h0 * 128:(h0 + HG) * 128, col:col + SQL].rearrange("(h d) c -> d h c", d=128),
                      reads=[xbuf(C, "Q", 0, 0)], writes=[qb])
                R.dma("sp", vn[:], V[h0 * 128:(h0 + HG) * 128, col:col + SQL].rearrange("(h d) c -> d h c", d=128),
                      reads=[xbuf(C, "V", 0, 0)], writes=[vnb])
                pt, ptb = ptr.next()
                for j in range(HG):
                    R.op("pe", lambda e, o=pt[0:SQL, j * 128:(j + 1) * 128], i=vn[:, j, :]:
                         e.transpose(o, i, C.identb[:]), reads=[vnb], writes=[ptb], inc=(j == HG - 1))
                R.op("dve", lambda e, o=vtt[:], i=pt[0:SQL, 0:HG * 128].rearrange("p (h d) -> p h d", d=128):
                     e.tensor_copy(out=o, in_=i), reads=[ptb], writes=[vttb])

                def diag(k0, nk):
                    if k0 < PAST:
                        return 0, None
                    return 0, [(j * SQL, SQL, C.trib[0:SQL, 0:SQL]) for j in range(HG)]
                groups = []
                for j in range(HG):
                    groups.append((j * SQL, SQL, qt[:, j, :],
                                   (lambda k0, nk, kt=kt, j=j: kt[:, j, k0:k0 + nk]),
                                   (lambda k0, nk, vt=vt, vtt=vtt, j=j: (vt[:, k0 // 128, j * 128:(j + 1) * 128] if k0 < PAST else vtt[:, j, :])),
                                   [qb, kb, vb, vttb]))
                attn_core(R, nc, C, A, groups, kblocks, HG * SQL, diag)
                ot, ob = oo.next()
                R.op("act", lambda e, o=ot[:], i=A.po[:, 0:HG * SQL]: e.activation(out=o, in_=i, func=AF.Copy),
                     reads=[A.pob], writes=[ob])
                R.dma("sp", OUT[h0 * 128:(h0 + HG) * 128, col:col + SQL].rearrange("(h d) c -> d h c", d=128),
                      ot[:].rearrange("p (h c) -> p h c", c=SQL), reads=[ob], writes=[xbuf(C, "YP", 0, 1)])
        R.flush()
    R.barrier()


def build_program(upto=99):
    nc = bass.Bass("TRN2", target_bir_lowering=False)
    I = {}

    def inp(name, shape, dt=F32):
        I[name] = nc.dram_tensor(name, list(shape), dt, kind="ExternalInput").ap()
        return I[name]
    inp("xin", [D, RP])
    inp("ffn_w_gate", [4, D, FF]); inp("ffn_w_up", [4, D, FF]); inp("ffn_w_down", [4, FF, D])
    inp("ab_w_in", [D, D]); inp("ab_w_out", [D, D]); inp("pool_w", [4, 256, 256])
    inp("ssm_w_glu", [1024, 1024]); inp("sb_w_qkv", [D, 3 * D]); inp("sb_w_out", [D, D])
    inp("gall", [128, 7, 16]); inp("pscale", [128, 8]); inp("ssmd", [128, 8]); inp("bglu", [128, 8])
    inp("invcnt", [128, 4, 16])
    inp("ssm_are", [128, 32]); inp("ssm_aim", [128, 32]); inp("ssm_ldt", [128, 32])
    inp("ssm_Bre", [128, 32, 128]); inp("ssm_Bim", [128, 32, 128])
    inp("ssm_Cre", [128, 32, 128]); inp("ssm_Cim", [128, 32, 128])
    inp("ssm_h0re", [NSQ, 128, 32]); inp("ssm_h0im", [NSQ, 128, 32])
    inp("pool_hist", [128, 8, NSQ, 15])
    inp("cache_kT", [NSQ, 16, 128, PAST]); inp("cache_v", [NSQ, PAST, D])
    inp("c_tri", [128, 128]); inp("c_lt", [128, 128]); inp("c_ident", [128, 128])
    O = {}

    def outp(name, shape, dt=F32):
        O[name] = nc.dram_tensor(name, list(shape), dt, kind="ExternalOutput").ap()
        return O[name]
    outp("y", [D, RP]); outp("kout", [D, RP]); outp("vout", [D, RP])
    outp("pool_p", [1024, 15]); outp("pool_s", [1024, NSQ, 15])
    outp("hre_p", [128, 32]); outp("him_p", [128, 32]); outp("hre_s", [NSQ, 128, 32]); outp("him_s", [NSQ, 128, 32])
    X = nc.dram_tensor("Xs", [D, RP], F32).ap()
    U = nc.dram_tensor("Us", [D, RP], F32).ap()
    Z = nc.dram_tensor("Zs", [1024, RP], BF).ap()
    YP = nc.dram_tensor("YPs", [D, RP], BF).ap()
    Qs = nc.dram_tensor("Qs", [D, RP], BF).ap()
    Ks = nc.dram_tensor("Ks", [D, RP], BF).ap()
    Vs = nc.dram_tensor("Vs", [D, RP], BF).ap()

    R = Rec(nc)
    C = Ctx()
    C.dbufs = {}
    with ExitStack() as es:
        R.begin(es)
        def sb(name, shape, dt=F32):
            return es.enter_context(_sbt(nc, name, shape, dt))
        C.ones32 = sb("ones32", [128, 128]); C.onesb = sb("onesb", [128, 128], BF)
        C.trib = sb("trib", [128, 128], BF); C.LTb = sb("LTb", [128, 128], BF); C.identb = sb("identb", [128, 128], BF)
        C.gall = sb("gall", [128, 7, 16]); C.pscale = sb("pscale", [128, 8]); C.ssmd = sb("ssmd", [128, 8])
        C.bglu = sb("bglu", [128, 8]); C.invcnt = sb("invcnt", [128, 4, 16])
        C.epsb = sb("epsb", [128, 1]); C.oneb = sb("oneb", [128, 1]); C.halfpi = sb("halfpi", [128, 1])
        cb = Buf()
        R.op("dve", lambda e: e.memset(C.ones32[:], 1.0), writes=[cb])
        R.op("dve", lambda e: e.memset(C.onesb[:], 1.0), writes=[cb])
        R.op("dve", lambda e: e.memset(C.epsb[:], EPS), writes=[cb])
        R.op("dve", lambda e: e.memset(C.oneb[:], 1.0), writes=[cb])
        R.op("dve", lambda e: e.memset(C.halfpi[:], math.pi / 2), writes=[cb])
        R.dma("pool", C.trib[:], I["c_tri"], writes=[cb])
        R.dma("pool", C.LTb[:], I["c_lt"], writes=[cb])
        R.dma("pool", C.identb[:], I["c_ident"], writes=[cb])
        for nm in ("gall", "pscale", "ssmd", "bglu", "invcnt"):
            R.dma("sp", getattr(C, nm)[:], I[nm], writes=[cb])
        R.barrier()

        wgate = I["ffn_w_gate"]; wup = I["ffn_w_up"]; wdn = I["ffn_w_down"]
        stage = 0

        def go():
            nonlocal stage
            stage += 1
            return stage <= upto
        skipffn = os.environ.get("MK_SKIPFFN", "0") == "1"
        if go():
            if skipffn:
                for dk in range(16):
                    R.dma("sp", X[dk * 128:(dk + 1) * 128, :], I["xin"][dk * 128:(dk + 1) * 128, :], writes=[xbuf(C, "X", dk, 0)])
                R.barrier()
            else:
                ffn_stage(R, nc, C, I["xin"], "xin", X, "X", wgate[0], wup[0], wdn[0], 0)
        if go():
            def cb_u(es2):
                orr = Ring(es2, nc, "uo", 3, [128, 448], F32)

                def f(si, mc, c0, n, p, pb):
                    ot, ob = orr.next()
                    R.op("act", lambda e, o=ot[:, 0:n], i=p[:, 0:n]: e.activation(out=o, in_=i, func=AF.Copy), reads=[pb], writes=[ob])
                    R.dma("sp", U[mc * 128:(mc + 1) * 128, c0:c0 + n], ot[:, 0:n], reads=[ob],
                          writes=[xbuf(C, "U", mc, 0), xbuf(C, "U", 8, 0)] if mc >= 8 else [xbuf(C, "U", mc, 0)])
                return f
            norm_linear_stage(R, nc, C, X, "X", 4, I["ab_w_in"], D, cb_u)
            R.dma("sp", O["pool_p"], U[0:1024, NPR - 15:NPR], reads=[xbuf(C, "U", k, 0) for k in range(8)], writes=[xbuf(C, "pp", 0, 0)])
            for s in range(NSQ):
                R.dma("sp", O["pool_s"][:, s, :], U[0:1024, SC0 + s * SQL + 1:SC0 + (s + 1) * SQL],
                      reads=[xbuf(C, "U", k, 0) for k in range(8)], writes=[xbuf(C, "pp", 1, s)])
        if go():
            pool_stage(R, nc, C, U, YP, I)
        if go():
            ssm_stage(R, nc, C, U, Z, I, O)
        if go():
            glu_stage(R, nc, C, Z, YP, I)
        if go():
            for kt in range(16):
                for si in range(len(SUP)):
                    C.dbufs[("YPl", kt, si)] = Buf()
            linear_residual_stage(R, nc, C, YP, "YPl", I["ab_w_out"], X, "X")
        if go() and not skipffn:
            ffn_stage(R, nc, C, X, "X", X, "X", wgate[1], wup[1], wdn[1], 1)
        if go() and not skipffn:
            ffn_stage(R, nc, C, X, "X", X, "X", wgate[2], wup[2], wdn[2], 2)
        if go():
            def cb_qkv(es2):
                o32 = Ring(es2, nc, "qo32", 3, [128, 448], F32)
                obf = Ring(es2, nc, "qobf", 3, [128, 448], BF)

                def f(si, mc, c0, n, p, pb):
                    which = mc // 16
                    hh = mc % 16
                    dst = (Qs, Ks, Vs)[which]
                    ot, ob = obf.next()
                    if which == 0:
                        R.op("act", lambda e, o=ot[:, 0:n], i=p[:, 0:n]: e.activation(out=o, in_=i, func=AF.Copy), reads=[pb], writes=[ob])
                    else:
                        o2, o2b = o32.next()
                        R.op("act", lambda e, o=o2[:, 0:n], i=p[:, 0:n]: e.activation(out=o, in_=i, func=AF.Copy), reads=[pb], writes=[o2b])
                        R.dma("sp", (O["kout"], O["vout"])[which - 1][hh * 128:(hh + 1) * 128, c0:c0 + n], o2[:, 0:n],
                              reads=[o2b], writes=[xbuf(C, "kvout", which, mc)])
                        R.op("dve", lambda e, o=ot[:, 0:n], i=o2[:, 0:n]: e.tensor_copy(out=o, in_=i), reads=[o2b], writes=[ob])
                    R.dma("sp", dst[hh * 128:(hh + 1) * 128, c0:c0 + n], ot[:, 0:n], reads=[ob],
                          writes=[xbuf(C, "QKV", which, mc)])
                return f
            norm_linear_stage(R, nc, C, X, "X", 5, I["sb_w_qkv"], 3 * D, cb_qkv)
        if go():
            attn_prompt_stage(R, nc, C, Qs, Ks, Vs, YP)
        if go():
            attn_sample_stage(R, nc, C, Qs, Ks, Vs, YP, I)
        if go():
            for kt in range(16):
                for si in range(len(SUP)):
                    C.dbufs[("YPm", kt, si)] = Buf()
            linear_residual_stage(R, nc, C, YP, "YPm", I["sb_w_out"], X, "X")
        if go() and not skipffn:
            ffn_stage(R, nc, C, X, "X", X, "X", wgate[3], wup[3], wdn[3], 3)
        if go():
            final_stage2(R, nc, C, X, "X", O["y"])
        if upto < 99:
            dsrc = {1: X, 2: U, 6: X, 7: X, 8: X, 12: X, 13: X}.get(upto, X)
            for dk in range(16):
                R.dma("sp", O["y"][dk * 128:(dk + 1) * 128, :], dsrc[dk * 128:(dk + 1) * 128, :], writes=[xbuf(C, "Ydump", dk, 0)])
        R.finish()
        print('NOPS', upto, R.nops, 'sem cnt', R.cnt, 'dq', R.dq)
    return nc


def _state_layout(a):
    return np.ascontiguousarray(a.reshape(32, 2, 64).transpose(1, 2, 0).reshape(128, 32))


def _vec128(v):
    return np.ascontiguousarray(v.reshape(-1, 128).T)


_PROG = {}


def kernel(**inp):
    f32 = np.float32
    g = lambda k: np.asarray(inp[k], dtype=f32)
    upto = int(os.environ.get("MK_UPTO", "99"))
    if upto not in _PROG:
        _PROG[upto] = build_program(upto)
    nc = _PROG[upto]
    x_prompt = g("x_prompt"); x_sample = g("x_sample"); meta = g("meta_tokens")
    shared = {}
    shared["ffn_w_gate"] = g("ffn_w_gate").reshape(4, D, FF)
    shared["ffn_w_up"] = g("ffn_w_up").reshape(4, D, FF)
    shared["ffn_w_down"] = g("ffn_w_down").reshape(4, FF, D)
    shared["ab_w_in"] = g("ab_w_in")[0]; shared["ab_w_out"] = g("ab_w_out")[0]
    shared["pool_w"] = g("pool_w")[0]; shared["ssm_w_glu"] = g("ssm_w_glu")[0]
    shared["sb_w_qkv"] = g("sb_w_qkv")[0]; shared["sb_w_out"] = g("sb_w_out")[0]
    fn = g("ffn_norm").reshape(4, D); mn = g("mix_norm"); fin = g("final_norm")
    gall = np.stack([_vec128(fn[0]), _vec128(fn[1]), _vec128(fn[2]), _vec128(fn[3]),
                     _vec128(mn[0]), _vec128(mn[1]), _vec128(fin)], axis=1)
    shared["gall"] = np.ascontiguousarray(gall)
    shared["pscale"] = _vec128(g("pool_scale")[0]); shared["ssmd"] = _vec128(g("ssm_d")[0])
    shared["bglu"] = _vec128(g("ssm_b_glu")[0])
    ic = np.zeros((128, 4, 16), f32)
    for gi in range(4):
        w = 2 << gi
        ic[:, gi, :] = 1.0 / np.minimum(np.arange(16) + 1, w)
    shared["invcnt"] = ic
    shared["ssm_are"] = _state_layout(g("ssm_a_re")[0]); shared["ssm_aim"] = _state_layout(g("ssm_a_im")[0])
    shared["ssm_ldt"] = _state_layout(np.repeat(g("ssm_log_dt")[0][:, None], 64, axis=1))
    bre = g("ssm_b_re")[0]; bim = g("ssm_b_im")[0]; cre = g("ssm_c_re")[0]; cim = g("ssm_c_im")[0]
    Bre = np.zeros((128, 32, 128), f32); Bim = np.zeros((128, 32, 128), f32)
    Cre = np.zeros((128, 32, 128), f32); Cim = np.zeros((128, 32, 128), f32)
    for gg in range(64):
        st = gg // 2; gl = gg % 2
        ch0 = (gg % 8) * 16
        Bre[ch0:ch0 + 16, st, gl * 64:(gl + 1) * 64] = bre[gg].T
        Bim[ch0:ch0 + 16, st, gl * 64:(gl + 1) * 64] = bim[gg].T
        Cre[gl * 64:(gl + 1) * 64, st, ch0:ch0 + 16] = cre[gg].T
        Cim[gl * 64:(gl + 1) * 64, st, ch0:ch0 + 16] = cim[gg].T
    shared["ssm_Bre"] = Bre; shared["ssm_Bim"] = Bim; shared["ssm_Cre"] = Cre; shared["ssm_Cim"] = Cim
    jj = np.arange(128)
    shared["c_tri"] = (jj[:, None] < jj[None, :]).astype(f32)
    shared["c_lt"] = (jj[:, None] >= jj[None, :]).astype(f32)
    shared["c_ident"] = np.eye(128, dtype=f32)
    cache_pool = g("cache_pool")[0]; sre = g("state_ssm_re")[0]; sim_ = g("state_ssm_im")[0]
    cache_k = np.asarray(inp["cache_k"], dtype=f32)[0]; cache_v = np.asarray(inp["cache_v"], dtype=f32)[0]
    in_maps = []
    for c in range(8):
        b = c % 4
        m = dict(shared)
        xin = np.zeros((D, RP), f32)
        xin[:, 0:16] = meta.T
        xin[:, 16:NPR] = x_prompt[b].T
        sq = list(range(4 * c, 4 * c + 4))
        xin[:, SC0:SC0 + NSQ * SQL] = x_sample[sq].reshape(NSQ * SQL, D).T
        m["xin"] = xin
        m["ssm_h0re"] = np.ascontiguousarray(np.stack([_state_layout(sre[s]) for s in sq], axis=0))
        m["ssm_h0im"] = np.ascontiguousarray(np.stack([_state_layout(sim_[s]) for s in sq], axis=0))
        ph = cache_pool[sq]
        m["pool_hist"] = np.ascontiguousarray(ph.transpose(2, 0, 1).reshape(8, 128, NSQ, 15).transpose(1, 0, 2, 3))
        m["cache_kT"] = np.ascontiguousarray(cache_k[sq].transpose(0, 2, 3, 1))
        m["cache_v"] = np.ascontiguousarray(cache_v[sq].reshape(NSQ, PAST, D))
        in_maps.append(m)
    ncore = int(os.environ.get("MK_NCORE", "8"))
    if ncore < 8:
        res = run_bass_kernel_spmd(nc, in_maps[:ncore], core_ids=list(range(ncore)))
        return res.results
    res = run_bass_kernel_spmd(nc, in_maps, core_ids=list(range(8)))
    rs = res.results
    y_prompt = np.stack([rs[b]["y"][:, 16:NPR].T for b in range(4)], 0)
    y_sample = np.concatenate([rs[c]["y"][:, SC0:SC0 + NSQ * SQL].T.reshape(NSQ, SQL, D) for c in range(8)], 0)
    pool_p = np.stack([rs[b]["pool_p"].T for b in range(4)], 0)[None]
    pool_s = np.concatenate([rs[c]["pool_s"].transpose(1, 2, 0) for c in range(8)], 0)[None]

    def unstate(a):
        return a.reshape(2, 64, 32).transpose(2, 0, 1).reshape(64, 64)
    re_p = np.stack([unstate(rs[b]["hre_p"]) for b in range(4)], 0)[None]
    im_p = np.stack([unstate(rs[b]["him_p"]) for b in range(4)], 0)[None]
    re_s = np.stack([unstate(rs[c]["hre_s"][s]) for c in range(8) for s in range(NSQ)], 0)[None]
    im_s = np.stack([unstate(rs[c]["him_s"][s]) for c in range(8) for s in range(NSQ)], 0)[None]
    k_p = np.stack([rs[b]["kout"][:, 0:NPR].T.reshape(NPR, 16, 128) for b in range(4)], 0)[None]
    v_p = np.stack([rs[b]["vout"][:, 0:NPR].T.reshape(NPR, 16, 128) for b in range(4)], 0)[None]
    k_s = np.concatenate([rs[c]["kout"][:, SC0:SC0 + NSQ * SQL].T.reshape(NSQ, SQL, 16, 128) for c in range(8)], 0)[None]
    v_s = np.concatenate([rs[c]["vout"][:, SC0:SC0 + NSQ * SQL].T.reshape(NSQ, SQL, 16, 128) for c in range(8)], 0)[None]
    outs = (y_prompt, y_sample, pool_p, pool_s, re_p, im_p, re_s, im_s, k_p, v_p, k_s, v_s)
    return tuple(np.ascontiguousarray(o, dtype=f32) for o in outs)
```

```python
import os, math
import numpy as np
from contextlib import ExitStack
import concourse.bass as bass
import concourse.mybir as mybir
from concourse.bass_utils import run_bass_kernel_spmd

F32 = mybir.dt.float32
BF = mybir.dt.bfloat16
AF = mybir.ActivationFunctionType
ALU = mybir.AluOpType

RP = 4224
NPR = 4112
SC0 = 4112
NSQ = 4
SQL = 16
D = 2048
FF = 5632
NFC = 44
PAST = 4096
NHEAD = 16
SUP = [(0, 896), (896, 896), (1792, 896), (2688, 896), (3584, 640)]
LCH = 64
EPS = 1e-6


def cchunks(c0, W):
    h = W // 2
    return [(c0, h), (c0 + h, W - h)]


_UID = [0]


def _sbt(nc, name, shape, dt):
    _UID[0] += 1
    return nc.sbuf_tensor("%s_%d" % (name, _UID[0]), shape, dt)


def _pst(nc, name, shape, dt):
    _UID[0] += 1
    return nc.psum_tensor("%s_%d" % (name, _UID[0]), shape, dt)


class Buf:
    __slots__ = ("w", "r")

    def __init__(self):
        self.w = []
        self.r = {}


class _Cap:
    def __getattr__(self, name):
        def f(*a, **k):
            self.call = (name, a, k)
            return None
        return f


class Rec:
    ENG = ("pe", "act", "dve", "pool", "sp")
    K = 8

    def __init__(self, nc):
        self.nc = nc
        self.ops = {e: [] for e in self.ENG}
        self.cnt = {e: 0 for e in self.ENG}
        self.dq = {"sp": 0, "pool": 0, "act": 0}
        self.floor = {}
        self.nops = {e: 0 for e in self.ENG}

    def _deps(self, reads, writes, extra):
        deps = list(extra)
        for b in reads:
            deps += b.w
        for b in writes:
            deps += b.w
            deps += list(b.r.values())
        return deps

    def _upd(self, tok, key, reads, writes):
        for b in reads:
            b.r[key] = tok
        for b in writes:
            b.w = [tok]
            b.r = {}

    def op(self, eng, fn, reads=(), writes=(), inc=True, extra=()):
        deps = self._deps(reads, writes, extra)
        deps += self.floor.pop(eng, [])
        if inc:
            self.cnt[eng] += 1
            tok = ("c", eng, self.cnt[eng])
        else:
            tok = ("c", eng, self.cnt[eng] + 1)
        cap = _Cap()
        fn(cap)
        self.ops[eng].append(("op", cap.call, deps, inc))
        self._upd(tok, ("c", eng), reads, writes)
        return tok

    def dma(self, q, out, in_, reads=(), writes=(), extra=(), **kw):
        deps = self._deps(reads, writes, extra)
        deps += self.floor.pop(q, [])
        j = self.dq[q]
        self.dq[q] = j + 1
        slot = j % self.K
        val = 16 * (j // self.K + 1)
        if j >= self.K:
            deps.append(("d", q, slot, val - 16))
        tok = ("d", q, slot, val)
        self.ops[q].append(("dma", (lambda e, out=out, in_=in_, kw=kw: e.dma_start(out=out, in_=in_, **kw)), deps, slot))
        self._upd(tok, ("d", q, slot), reads, writes)
        return tok

    def all_tokens(self):
        toks = []
        for e in self.ENG:
            if self.cnt[e] > 0:
                toks.append(("c", e, self.cnt[e]))
        for q, j in self.dq.items():
            for slot in range(min(j, self.K)):
                n = (j - 1 - slot) // self.K + 1
                toks.append(("d", q, slot, 16 * n))
        return toks

    def barrier(self):
        toks = self.all_tokens()
        for e in self.ENG:
            self.floor[e] = list(toks) + self.floor.get(e, [])

    def begin(self, es):
        nc = self.nc
        self.csem = {e: es.enter_context(nc.semaphore("c_" + e)) for e in self.ENG}
        self.dsem = {q: [es.enter_context(nc.semaphore("d_%s%d" % (q, i))) for i in range(self.K)]
                     for q in self.dq}
        self.block = es.enter_context(nc.Block())
        self.waited = {e: {} for e in self.ENG}

    def flush(self):
        bname = {"pe": "tensor", "act": "scalar", "dve": "vector", "pool": "gpsimd", "sp": "sync"}
        csem, dsem = self.csem, self.dsem
        for ename in self.ENG:
            ops = self.ops[ename]
            self.nops[ename] += len(ops)
            self.ops[ename] = []
            if not ops:
                continue

            def body(e, ename=ename, ops=ops):
                waited = self.waited[ename]
                for (kind, fn, deps, aux) in ops:
                    for t in deps:
                        if t[0] == "c":
                            if t[1] == "pe" and ename == "pe":
                                continue
                            key = ("c", t[1]); val = t[2]; sem = csem[t[1]]
                        else:
                            key = ("d", t[1], t[2]); val = t[3]; sem = dsem[t[1]][t[2]]
                        if waited.get(key, 0) >= val:
                            continue
                        waited[key] = val
                        e.wait_ge(sem, val)
                    if kind == "wait":
                        continue
                    if kind == "op":
                        ins = getattr(e, fn[0])(*fn[1], **fn[2])
                    else:
                        ins = fn(e)
                    if kind == "dma":
                        ins.then_inc(dsem[ename][aux], 16)
                    elif aux:
                        ins.then_inc(csem[ename], 1)
            getattr(self.block, bname[ename])(body)

    def finish(self):
        final = self.all_tokens()
        self.ops["sp"].append(("wait", None, final, False))
        self.flush()


class Ring:
    def __init__(self, es, nc, name, n, shape, dtype, psum=False):
        self.t = []
        self.b = []
        for i in range(n):
            if psum:
                t = es.enter_context(_pst(nc, "%s%d" % (name, i), shape, dtype))
            else:
                t = es.enter_context(_sbt(nc, "%s%d" % (name, i), shape, dtype))
            self.t.append(t)
            self.b.append(Buf())
        self.i = 0

    def next(self):
        k = self.i % len(self.t)
        self.i += 1
        return self.t[k], self.b[k]


class Ctx:
    pass


def xbuf(C, name, dk, si):
    key = (name, dk, si)
    if key not in C.dbufs:
        C.dbufs[key] = Buf()
    return C.dbufs[key]


def phase_norm(R, nc, C, S, Xsrc, xname, si, c0, W, gi, hout, hbuf, hview=None):
    ch = cchunks(0, W)
    for dk in range(16):
        xt, xb = S.xr.next()
        R.dma("sp", xt[:, 0:W], Xsrc[dk * 128:(dk + 1) * 128, c0:c0 + W],
              reads=[xbuf(C, xname, dk, si)], writes=[xb])
        st, sb = S.sqr.next()
        R.op("act", lambda e, o=st[:, 0:W], i=xt[:, 0:W]: e.activation(out=o, in_=i, func=AF.Square),
             reads=[xb], writes=[sb])
        for ci, (cc, n) in enumerate(ch):
            R.op("pe", lambda e, o=S.pss[ci][:, 0:n], r=st[:, cc:cc + n], s=(dk == 0), p=(dk == 15):
                 e.matmul(o, lhsT=C.ones32[:], rhs=r, start=s, stop=p),
                 reads=[sb], writes=[S.pssb[ci]], inc=True)
    for ci, (cc, n) in enumerate(ch):
        R.op("act", lambda e, o=S.srt[:, cc:cc + n], i=S.pss[ci][:, 0:n]:
             e.activation(out=o, in_=i, func=AF.Sqrt, bias=C.epsb[:, 0:1], scale=1.0 / D),
             reads=[S.pssb[ci]], writes=[S.srtb])
    R.op("dve", lambda e: e.reciprocal(out=S.rstd[:, 0:W], in_=S.srt[:, 0:W]),
         reads=[S.srtb], writes=[S.rstdb])
    for dk in range(16):
        xt, xb = S.xr.next()
        R.dma("sp", xt[:, 0:W], Xsrc[dk * 128:(dk + 1) * 128, c0:c0 + W],
              reads=[xbuf(C, xname, dk, si)], writes=[xb])
        o = hout(dk)
        R.op("dve", lambda e, o=o, i=xt[:, 0:W], g=C.gall[:, gi, dk:dk + 1]:
             e.scalar_tensor_tensor(out=o, in0=i, scalar=g, in1=S.rstd[:, 0:W], op0=ALU.mult, op1=ALU.mult),
             reads=[xb, S.rstdb], writes=[hbuf(dk)])


def norm_tasks(R, C, S, Xsrc, xname, si, c0, W, gi, hout, hbuf):
    ch = cchunks(0, W)
    hold = {}

    def A(dk):
        xt, xb = S.xr.next()
        R.dma("sp", xt[:, 0:W], Xsrc[dk * 128:(dk + 1) * 128, c0:c0 + W],
              reads=[xbuf(C, xname, dk, si)], writes=[xb])
        st, sb = S.sqr.next()
        R.op("act", lambda e, o=st[:, 0:W], i=xt[:, 0:W]: e.activation(out=o, in_=i, func=AF.Square),
             reads=[xb], writes=[sb])
        hi, hib = S.hir.next()
        lo, lob = S.lor.next()
        R.op("dve", lambda e, o=hi[:, 0:W], i=st[:, 0:W]: e.tensor_copy(out=o, in_=i), reads=[sb], writes=[hib])
        R.op("dve", lambda e, o=lo[:, 0:W], a=st[:, 0:W], b=hi[:, 0:W]:
             e.tensor_tensor(out=o, in0=a, in1=b, op=ALU.subtract), reads=[sb, hib], writes=[lob])
        hold[("s", dk)] = (hi, hib, lo, lob)

    def B(dk):
        hi, hib, lo, lob = hold.pop(("s", dk))
        for ci, (cc, n) in enumerate(ch):
            R.op("pe", lambda e, o=S.pss[ci][:, 0:n], r=hi[:, cc:cc + n], s=(dk == 0):
                 e.matmul(o, lhsT=C.onesb[:], rhs=r, start=s, stop=False),
                 reads=[hib], writes=[S.pssb[ci]], inc=False)
            R.op("pe", lambda e, o=S.pss[ci][:, 0:n], r=lo[:, cc:cc + n], p=(dk == 15):
                 e.matmul(o, lhsT=C.onesb[:], rhs=r, start=False, stop=p),
                 reads=[lob], writes=[S.pssb[ci]], inc=True)

    def FIN():
        for ci, (cc, n) in enumerate(ch):
            R.op("act", lambda e, o=S.srt[:, cc:cc + n], i=S.pss[ci][:, 0:n]:
                 e.activation(out=o, in_=i, func=AF.Sqrt, bias=C.epsb[:, 0:1], scale=1.0 / D),
                 reads=[S.pssb[ci]], writes=[S.srtb])
        R.op("dve", lambda e: e.reciprocal(out=S.rstd[:, 0:W], in_=S.srt[:, 0:W]),
             reads=[S.srtb], writes=[S.rstdb])

    def A2(dk):
        xt, xb = S.xr.next()
        R.dma("sp", xt[:, 0:W], Xsrc[dk * 128:(dk + 1) * 128, c0:c0 + W],
              reads=[xbuf(C, xname, dk, si)], writes=[xb])
        hold[("x", dk)] = (xt, xb)

    def B2(dk):
        xt, xb = hold.pop(("x", dk))
        o = hout(dk)
        R.op("dve", lambda e, o=o, i=xt[:, 0:W], g=C.gall[:, gi, dk:dk + 1]:
             e.scalar_tensor_tensor(out=o, in0=i, scalar=g, in1=S.rstd[:, 0:W], op0=ALU.mult, op1=ALU.mult),
             reads=[xb, S.rstdb], writes=[hbuf(dk)])

    stats = [lambda: A(0)]
    for dk in range(16):
        if dk + 1 < 16:
            stats.append(lambda dk=dk: A(dk + 1))
        stats.append(lambda dk=dk: B(dk))
    stats.append(FIN)
    app = [lambda: A2(0)]
    for dk in range(16):
        if dk + 1 < 16:
            app.append(lambda dk=dk: A2(dk + 1))
        app.append(lambda dk=dk: B2(dk))
    return stats, app


def run_tasks(tasks, k):
    for _ in range(k):
        if tasks:
            tasks.pop(0)()


def alloc_norm(es, nc, S):
    S.xr = Ring(es, nc, "xr", 4, [128, 896], F32)
    S.sqr = Ring(es, nc, "sqr", 2, [128, 896], F32)
    S.hir = Ring(es, nc, "sqhi", 2, [128, 896], BF)
    S.lor = Ring(es, nc, "sqlo", 2, [128, 896], BF)
    S.pss = [es.enter_context(_pst(nc, "pss%d" % i, [128, 512], F32)) for i in range(2)]
    S.pssb = [Buf(), Buf()]
    S.srt = es.enter_context(_sbt(nc, "srt", [128, 896], F32))
    S.srtb = Buf()
    S.rstd = es.enter_context(_sbt(nc, "rstd", [128, 896], F32))
    S.rstdb = Buf()


def ffn_stage(R, nc, C, Xsrc, sname, Xdst, dname, wg, wu, wd, gi):
    wgv = wg.rearrange("(kt p) f -> p kt f", p=128)
    wuv = wu.rearrange("(kt p) f -> p kt f", p=128)
    wdv = wd.rearrange("(fc p) d -> p fc d", p=128)
    with ExitStack() as es:
        S = Ctx()
        alloc_norm(es, nc, S)
        hfm = es.enter_context(_sbt(nc, "hfm", [128, 16, 896], BF))
        hb = [Buf() for _ in range(16)]
        act = es.enter_context(_sbt(nc, "actf", [128, NFC, 896], BF))
        ab = [Buf() for _ in range(NFC)]
        wgr = Ring(es, nc, "wg", 2, [128, 16, 128], BF)
        wur = Ring(es, nc, "wu", 2, [128, 16, 128], BF)
        wdr = Ring(es, nc, "wd", 2, [128, NFC, 128], BF)
        pgr = Ring(es, nc, "pg", 2, [128, 512], F32, psum=True)
        pur = Ring(es, nc, "pu", 2, [128, 512], F32, psum=True)
        sgr = Ring(es, nc, "sg", 2, [128, 448], F32)
        xor_ = Ring(es, nc, "xo", 2, [128, 448], F32)
        rxr = Ring(es, nc, "rxr", 3, [128, 448], F32)
        def mk(si):
            c0n, Wn = SUP[si]
            return norm_tasks(R, C, S, Xsrc, sname, si, c0n, Wn, gi,
                              hout=lambda dk, Wn=Wn: hfm[:, dk, 0:Wn], hbuf=lambda dk: hb[dk])
        st0, ap0 = mk(0)
        run_tasks(st0, 99)
        run_tasks(ap0, 99)
        for si, (c0, W) in enumerate(SUP):
            if si + 1 < len(SUP):
                stt, apt = mk(si + 1)
            else:
                stt, apt = [], []
            ch = cchunks(0, W)
            for fc in range(NFC):
                wgt, wgb = wgr.next()
                wut, wub = wur.next()
                R.dma("pool", wgt[:], wgv[:, :, fc * 128:(fc + 1) * 128], writes=[wgb])
                R.dma("pool", wut[:], wuv[:, :, fc * 128:(fc + 1) * 128], writes=[wub])
                for (cc, n) in ch:
                    pg, pgb = pgr.next()
                    pu, pub = pur.next()
                    for kt in range(16):
                        R.op("pe", lambda e, o=pg[:, 0:n], l=wgt[:, kt, :], r=hfm[:, kt, cc:cc + n], s=(kt == 0), p=(kt == 15):
                             e.matmul(o, lhsT=l, rhs=r, start=s, stop=p),
                             reads=[wgb, hb[kt]], writes=[pgb], inc=(kt == 15))
                    for kt in range(16):
                        R.op("pe", lambda e, o=pu[:, 0:n], l=wut[:, kt, :], r=hfm[:, kt, cc:cc + n], s=(kt == 0), p=(kt == 15):
                             e.matmul(o, lhsT=l, rhs=r, start=s, stop=p),
                             reads=[wub, hb[kt]], writes=[pub], inc=(kt == 15))
                    sg, sgb = sgr.next()
                    R.op("act", lambda e, o=sg[:, 0:n], i=pg[:, 0:n]: e.activation(out=o, in_=i, func=AF.Silu),
                         reads=[pgb], writes=[sgb])
                    R.op("dve", lambda e, o=act[:, fc, cc:cc + n], a=sg[:, 0:n], b=pu[:, 0:n]:
                         e.tensor_tensor(out=o, in0=a, in1=b, op=ALU.mult),
                         reads=[sgb, pub], writes=[ab[fc]])
                run_tasks(stt, 1)
            run_tasks(stt, 99)
            for dcc in range(16):
                wdt, wdb = wdr.next()
                R.dma("pool", wdt[:], wdv[:, :, dcc * 128:(dcc + 1) * 128], writes=[wdb])
                for (cc, n) in ch:
                    py, pyb = pgr.next()
                    for fc in range(NFC):
                        R.op("pe", lambda e, o=py[:, 0:n], l=wdt[:, fc, :], r=act[:, fc, cc:cc + n], s=(fc == 0), p=(fc == NFC - 1):
                             e.matmul(o, lhsT=l, rhs=r, start=s, stop=p),
                             reads=[wdb, ab[fc]], writes=[pyb], inc=(fc == NFC - 1))
                    xt, xb = rxr.next()
                    R.dma("sp", xt[:, 0:n], Xsrc[dcc * 128:(dcc + 1) * 128, c0 + cc:c0 + cc + n],
                          reads=[xbuf(C, sname, dcc, si)], writes=[xb])
                    xo, xob = xor_.next()
                    R.op("dve", lambda e, o=xo[:, 0:n], a=py[:, 0:n], b=xt[:, 0:n]:
                         e.scalar_tensor_tensor(out=o, in0=a, scalar=0.5, in1=b, op0=ALU.mult, op1=ALU.add),
                         reads=[pyb, xb], writes=[xob])
                    R.dma("sp", Xdst[dcc * 128:(dcc + 1) * 128, c0 + cc:c0 + cc + n], xo[:, 0:n],
                          reads=[xob], writes=[xbuf(C, dname, dcc, si)])
                run_tasks(apt, 2 if dcc < 15 else 99)
        R.flush()
    R.barrier()


def norm_linear_stage(R, nc, C, Xsrc, sname, gi, Wd, M, out_cb):
    wv = Wd.rearrange("(kt p) m -> p kt m", p=128)
    with ExitStack() as es:
        S = Ctx()
        alloc_norm(es, nc, S)
        hfms = [es.enter_context(_sbt(nc, "hfm%d" % i, [128, 16, 896], BF)) for i in range(2)]
        hbs = [[Buf() for _ in range(16)] for _ in range(2)]
        wr = Ring(es, nc, "wl", 2, [128, 16, 128], BF)
        pr = Ring(es, nc, "pl", 3, [128, 512], F32, psum=True)
        S.es = es
        cbs = out_cb(es)

        def mk(si):
            c0n, Wn = SUP[si]
            hf_, hb_ = hfms[si % 2], hbs[si % 2]
            return norm_tasks(R, C, S, Xsrc, sname, si, c0n, Wn, gi,
                              hout=lambda dk, Wn=Wn, hf_=hf_: hf_[:, dk, 0:Wn], hbuf=lambda dk, hb_=hb_: hb_[dk])
        st0, ap0 = mk(0)
        run_tasks(st0, 99)
        run_tasks(ap0, 99)
        MG = M // 128
        for si, (c0, W) in enumerate(SUP):
            hfm, hb = hfms[si % 2], hbs[si % 2]
            if si + 1 < len(SUP):
                stt, apt = mk(si + 1)
                tasks = stt + apt
            else:
                tasks = []
            per = -(-len(tasks) // max(1, MG - 2))
            for mc in range(MG):
                wt, wb = wr.next()
                R.dma("pool", wt[:], wv[:, :, mc * 128:(mc + 1) * 128], writes=[wb])
                for (cc, n) in cchunks(0, W):
                    p, pb = pr.next()
                    for kt in range(16):
                        R.op("pe", lambda e, o=p[:, 0:n], l=wt[:, kt, :], r=hfm[:, kt, cc:cc + n], s=(kt == 0), q=(kt == 15):
                             e.matmul(o, lhsT=l, rhs=r, start=s, stop=q),
                             reads=[wb, hb[kt]], writes=[pb], inc=(kt == 15))
                    cbs(si, mc, c0 + cc, n, p, pb)
                run_tasks(tasks, per)
            run_tasks(tasks, 999)
        R.flush()
    R.barrier()


def linear_residual_stage(R, nc, C, Asrc, aname, Wd, X, xname):
    wv = Wd.rearrange("(kt p) m -> p kt m", p=128)
    with ExitStack() as es:
        afm = es.enter_context(_sbt(nc, "afm", [128, 16, 896], BF))
        ab = [Buf() for _ in range(16)]
        wr = Ring(es, nc, "wl", 2, [128, 16, 128], BF)
        pr = Ring(es, nc, "pl", 3, [128, 512], F32, psum=True)
        xr = Ring(es, nc, "xr", 3, [128, 448], F32)
        xor_ = Ring(es, nc, "xo", 3, [128, 448], F32)
        for si, (c0, W) in enumerate(SUP):
            for kt in range(16):
                R.dma("sp", afm[:, kt, 0:W], Asrc[kt * 128:(kt + 1) * 128, c0:c0 + W],
                      reads=[xbuf(C, aname, kt, si)], writes=[ab[kt]])
            for mc in range(16):
                wt, wb = wr.next()
                R.dma("pool", wt[:], wv[:, :, mc * 128:(mc + 1) * 128], writes=[wb])
                for (cc, n) in cchunks(0, W):
                    p, pb = pr.next()
                    for kt in range(16):
                        R.op("pe", lambda e, o=p[:, 0:n], l=wt[:, kt, :], r=afm[:, kt, cc:cc + n], s=(kt == 0), q=(kt == 15):
                             e.matmul(o, lhsT=l, rhs=r, start=s, stop=q),
                             reads=[wb, ab[kt]], writes=[pb], inc=(kt == 15))
                    xt, xb = xr.next()
                    R.dma("sp", xt[:, 0:n], X[mc * 128:(mc + 1) * 128, c0 + cc:c0 + cc + n],
                          reads=[xbuf(C, xname, mc, si)], writes=[xb])
                    xo, xob = xor_.next()
                    R.op("dve", lambda e, o=xo[:, 0:n], a=p[:, 0:n], b=xt[:, 0:n]:
                         e.tensor_tensor(out=o, in0=a, in1=b, op=ALU.add),
                         reads=[pb, xb], writes=[xob])
                    R.dma("sp", X[mc * 128:(mc + 1) * 128, c0 + cc:c0 + cc + n], xo[:, 0:n],
                          reads=[xob], writes=[xbuf(C, xname, mc, si)])
        R.flush()
    R.barrier()


def final_stage2(R, nc, C, X, xname, Y):
    with ExitStack() as es:
        S = Ctx()
        alloc_norm(es, nc, S)
        yr = Ring(es, nc, "yr", 3, [128, 896], F32)
        for si, (c0, W) in enumerate(SUP):
            ch = cchunks(0, W)
            for dk in range(16):
                xt, xb = S.xr.next()
                R.dma("sp", xt[:, 0:W], X[dk * 128:(dk + 1) * 128, c0:c0 + W],
                      reads=[xbuf(C, xname, dk, si)], writes=[xb])
                st, sb = S.sqr.next()
                R.op("act", lambda e, o=st[:, 0:W], i=xt[:, 0:W]: e.activation(out=o, in_=i, func=AF.Square),
                     reads=[xb], writes=[sb])
                for ci, (cc, n) in enumerate(ch):
                    R.op("pe", lambda e, o=S.pss[ci][:, 0:n], r=st[:, cc:cc + n], s=(dk == 0), p=(dk == 15):
                         e.matmul(o, lhsT=C.ones32[:], rhs=r, start=s, stop=p),
                         reads=[sb], writes=[S.pssb[ci]], inc=True)
            for ci, (cc, n) in enumerate(ch):
                R.op("act", lambda e, o=S.srt[:, cc:cc + n], i=S.pss[ci][:, 0:n]:
                     e.activation(out=o, in_=i, func=AF.Sqrt, bias=C.epsb[:, 0:1], scale=1.0 / D),
                     reads=[S.pssb[ci]], writes=[S.srtb])
            R.op("dve", lambda e: e.reciprocal(out=S.rstd[:, 0:W], in_=S.srt[:, 0:W]),
                 reads=[S.srtb], writes=[S.rstdb])
            for dk in range(16):
                xt, xb = S.xr.next()
                R.dma("sp", xt[:, 0:W], X[dk * 128:(dk + 1) * 128, c0:c0 + W],
                      reads=[xbuf(C, xname, dk, si)], writes=[xb])
                yt, yb = yr.next()
                R.op("dve", lambda e, o=yt[:, 0:W], i=xt[:, 0:W], g=C.gall[:, 6, dk:dk + 1]:
                     e.scalar_tensor_tensor(out=o, in0=i, scalar=g, in1=S.rstd[:, 0:W], op0=ALU.mult, op1=ALU.mult),
                     reads=[xb, S.rstdb], writes=[yb])
                R.dma("sp", Y[dk * 128:(dk + 1) * 128, c0:c0 + W], yt[:, 0:W], reads=[yb],
                      writes=[xbuf(C, "Y", dk, si)])
        R.flush()
    R.barrier()


def pool_stage(R, nc, C, U, YP, I):
    PADC = 16
    with ExitStack() as es:
        NCOL = PADC + NPR
        lev = [es.enter_context(_sbt(nc, "lev%d" % i, [128, NCOL], F32)) for i in range(3)]
        levb = [Buf() for _ in range(3)]
        slev = [es.enter_context(_sbt(nc, "slev%d" % i, [128, NSQ, 32], F32)) for i in range(3)]
        slevb = [Buf() for _ in range(3)]
        dfr = Ring(es, nc, "dfm", 2, [128, RP], BF)
        wp = es.enter_context(_sbt(nc, "wp", [128, 2, 256], BF))
        wpb = Buf()
        pr = Ring(es, nc, "pp", 2, [128, 512], F32, psum=True)
        orr = Ring(es, nc, "po", 2, [128, 512], BF)
        tfix = es.enter_context(_sbt(nc, "tfix", [128, 16], F32))
        tfb = Buf()
        for i in range(3):
            R.op("pool", lambda e, t=lev[i]: e.memset(t[:, 0:PADC], 0.0), writes=[levb[i]])
            R.op("pool", lambda e, t=slev[i]: e.memset(t[:], 0.0), writes=[slevb[i]])
        for g in range(4):
            w = 2 << g
            nst = g + 1
            R.dma("pool", wp[:], I["pool_w"][g].rearrange("(kt p) m -> p kt m", p=128), writes=[wpb])
            dts = []
            for kt2 in range(2):
                ct = 2 * g + kt2
                R.dma("sp", lev[0][:, PADC:PADC + NPR], U[ct * 128:(ct + 1) * 128, 0:NPR],
                      reads=[xbuf(C, "U", ct, 0)], writes=[levb[0]])
                R.dma("sp", slev[0][:, :, 1:16], I["pool_hist"][:, ct, :, :], writes=[slevb[0]])
                R.dma("sp", slev[0][:, :, 16:32],
                      U[ct * 128:(ct + 1) * 128, SC0:SC0 + NSQ * SQL].rearrange("p (s t) -> p s t", t=SQL),
                      reads=[xbuf(C, "U", ct, 0)], writes=[slevb[0]])
                src = 0
                srcb = levb[0]
                ssrc = 0
                for s in range(nst):
                    sh = 1 << s
                    dst = 1 if src != 1 else 2
                    R.op("dve", lambda e, o=lev[dst][:, PADC:NCOL], a=lev[src][:, PADC:NCOL], b=lev[src][:, PADC - sh:NCOL - sh]:
                         e.tensor_tensor(out=o, in0=a, in1=b, op=ALU.add),
                         reads=[levb[src]], writes=[levb[dst]])
                    R.op("pool", lambda e, o=slev[dst][:, :, sh:32], a=slev[ssrc][:, :, sh:32], b=slev[ssrc][:, :, 0:32 - sh]:
                         e.tensor_tensor(out=o, in0=a, in1=b, op=ALU.add),
                         reads=[slevb[ssrc]], writes=[slevb[dst]])
                    src = dst
                    ssrc = dst
                df, dfb = dfr.next()
                dts.append((df, dfb))
                R.op("dve", lambda e, o=df[:, 0:NPR], a=lev[src][:, PADC:NCOL], b=lev[0][:, PADC:NCOL]:
                     e.scalar_tensor_tensor(out=o, in0=a, scalar=1.0 / w, in1=b, op0=ALU.mult, op1=ALU.subtract),
                     reads=[levb[src], levb[0]], writes=[dfb])
                R.op("dve", lambda e, a=lev[src][:, PADC:PADC + 16], b=C.invcnt[:, g, :]:
                     e.tensor_tensor(out=tfix[:], in0=a, in1=b, op=ALU.mult),
                     reads=[levb[src]], writes=[tfb])
                R.op("dve", lambda e, o=df[:, 0:16], b=lev[0][:, PADC:PADC + 16]:
                     e.tensor_tensor(out=o, in0=tfix[:], in1=b, op=ALU.subtract),
                     reads=[tfb, levb[0]], writes=[dfb])
                R.op("dve", lambda e, o=df[:, SC0:SC0 + NSQ * SQL].rearrange("p (s t) -> p s t", t=SQL), a=slev[ssrc][:, :, 16:32], b=slev[0][:, :, 16:32]:
                     e.scalar_tensor_tensor(out=o, in0=a, scalar=1.0 / w, in1=b, op0=ALU.mult, op1=ALU.subtract),
                     reads=[slevb[ssrc], slevb[0]], writes=[dfb])
                R.op("pool", lambda e, o=df[:, SC0 + NSQ * SQL:RP]: e.memset(o, 0.0), writes=[dfb])
            for m in range(2):
                oc = 2 * g + m
                for cc in range(0, RP, 512):
                    n = min(512, RP - cc)
                    p, pb = pr.next()
                    for kt2 in range(2):
                        R.op("pe", lambda e, o=p[:, 0:n], l=wp[:, kt2, m * 128:(m + 1) * 128], r=dts[kt2][0][:, cc:cc + n], s=(kt2 == 0), q=(kt2 == 1):
                             e.matmul(o, lhsT=l, rhs=r, start=s, stop=q),
                             reads=[wpb, dts[kt2][1]], writes=[pb], inc=(kt2 == 1))
                    ot, ob = orr.next()
                    R.op("act", lambda e, o=ot[:, 0:n], i=p[:, 0:n], sc=C.pscale[:, oc:oc + 1]:
                         e.activation(out=o, in_=i, func=AF.Copy, scale=sc),
                         reads=[pb], writes=[ob])
                    R.dma("sp", YP[oc * 128:(oc + 1) * 128, cc:cc + n], ot[:, 0:n], reads=[ob],
                          writes=[xbuf(C, "YP", oc, 0)])
        R.flush()
    R.barrier()


def ssm_stage(R, nc, C, U, Z, I, O):
    L = LCH
    HT = 16
    with ExitStack() as es:
        def sb(name, shape, dt=F32):
            return es.enter_context(_sbt(nc, name, shape, dt))
        are = sb("are", [128, 32]); aim = sb("aim", [128, 32]); ldt = sb("ldt", [128, 32])
        t0 = sb("t0", [128, 32]); t1 = sb("t1", [128, 32]); t2 = sb("t2", [128, 32]); t3 = sb("t3", [128, 32])
        cs = sb("cs", [128, 32]); sn = sb("sn", [128, 32]); mag = sb("mag", [128, 32])
        cre = sb("cre", [128, 32]); cim = sb("cim", [128, 32])
        Ec = sb("Ec", [128, 32, L]); Es = sb("Es", [128, 32, L])
        nEs = sb("nEs", [128, 32, L]); nEc = sb("nEc", [128, 32, L])
        TA = sb("TA", [128, 32, L]); TB = sb("TB", [128, 32, L])
        Rb = sb("Rb", [128, 32, L])
        ELc = sb("ELc", [128, 32]); ELs = sb("ELs", [128, 32])
        one = Buf()
        Bre = sb("Bre", [128, 32, 128], BF); Bim = sb("Bim", [128, 32, 128], BF)
        Cre = sb("Cre", [128, 32, 128], BF); Cim = sb("Cim", [128, 32, 128], BF)
        wb_ = Buf()
        R.dma("sp", are[:], I["ssm_are"], writes=[one])
        R.dma("sp", aim[:], I["ssm_aim"], writes=[one])
        R.dma("sp", ldt[:], I["ssm_ldt"], writes=[one])
        R.dma("pool", Bre[:], I["ssm_Bre"], writes=[wb_])
        R.dma("pool", Bim[:], I["ssm_Bim"], writes=[wb_])
        R.dma("pool", Cre[:], I["ssm_Cre"], writes=[wb_])
        R.dma("pool", Cim[:], I["ssm_Cim"], writes=[wb_])

        def A(fn):
            R.op("act", fn, reads=[one], writes=[one])

        def V(fn):
            R.op("dve", fn, reads=[one], writes=[one])
        A(lambda e: e.activation(out=t0[:], in_=ldt[:], func=AF.Exp))
        V(lambda e: e.tensor_tensor(out=t1[:], in0=are[:], in1=t0[:], op=ALU.mult))
        V(lambda e: e.tensor_tensor(out=t2[:], in0=aim[:], in1=t0[:], op=ALU.mult))
        A(lambda e: e.activation(out=mag[:], in_=t1[:], func=AF.Exp))
        A(lambda e: e.activation(out=sn[:], in_=t2[:], func=AF.Sin, scale=1.0 / 16))
        A(lambda e: e.activation(out=cs[:], in_=t2[:], func=AF.Sin, scale=-1.0 / 16, bias=C.halfpi[:, 0:1]))
        for _ in range(4):
            V(lambda e: e.tensor_tensor(out=t0[:], in0=cs[:], in1=cs[:], op=ALU.mult))
            V(lambda e: e.tensor_tensor(out=t1[:], in0=sn[:], in1=sn[:], op=ALU.mult))
            V(lambda e: e.tensor_tensor(out=t3[:], in0=cs[:], in1=sn[:], op=ALU.mult))
            V(lambda e: e.tensor_tensor(out=cs[:], in0=t0[:], in1=t1[:], op=ALU.subtract))
            V(lambda e: e.tensor_scalar(out=sn[:], in0=t3[:], scalar1=2.0, scalar2=None, op0=ALU.mult))
        V(lambda e: e.tensor_tensor(out=t0[:], in0=mag[:], in1=cs[:], op=ALU.mult))
        V(lambda e: e.tensor_tensor(out=t1[:], in0=mag[:], in1=sn[:], op=ALU.mult))
        V(lambda e: e.tensor_scalar(out=t0[:], in0=t0[:], scalar1=-1.0, scalar2=None, op0=ALU.add))
        V(lambda e: e.tensor_tensor(out=t2[:], in0=are[:], in1=are[:], op=ALU.mult))
        V(lambda e: e.tensor_tensor(out=t3[:], in0=aim[:], in1=aim[:], op=ALU.mult))
        V(lambda e: e.tensor_tensor(out=t2[:], in0=t2[:], in1=t3[:], op=ALU.add))
        V(lambda e: e.reciprocal(out=t2[:], in_=t2[:]))
        V(lambda e: e.tensor_tensor(out=cre[:], in0=t0[:], in1=are[:], op=ALU.mult))
        V(lambda e: e.tensor_tensor(out=t3[:], in0=t1[:], in1=aim[:], op=ALU.mult))
        V(lambda e: e.tensor_tensor(out=cre[:], in0=cre[:], in1=t3[:], op=ALU.add))
        V(lambda e: e.tensor_tensor(out=cre[:], in0=cre[:], in1=t2[:], op=ALU.mult))
        V(lambda e: e.tensor_tensor(out=cim[:], in0=t1[:], in1=are[:], op=ALU.mult))
        V(lambda e: e.tensor_tensor(out=t3[:], in0=t0[:], in1=aim[:], op=ALU.mult))
        V(lambda e: e.tensor_tensor(out=cim[:], in0=cim[:], in1=t3[:], op=ALU.subtract))
        V(lambda e: e.tensor_tensor(out=cim[:], in0=cim[:], in1=t2[:], op=ALU.mult))
        V(lambda e: e.tensor_copy(out=Ec[:, :, 0:1], in_=cs[:].unsqueeze(2)))
        V(lambda e: e.tensor_copy(out=Es[:, :, 0:1], in_=sn[:].unsqueeze(2)))
        m = 1
        while m < L:
            bc = Ec[:, :, m - 1:m].broadcast_to([128, 32, m])
            bs = Es[:, :, m - 1:m].broadcast_to([128, 32, m])
            V(lambda e, m=m, bc=bc: e.tensor_tensor(out=TA[:, :, 0:m], in0=Ec[:, :, 0:m], in1=bc, op=ALU.mult))
            V(lambda e, m=m, bs=bs: e.tensor_tensor(out=TB[:, :, 0:m], in0=Es[:, :, 0:m], in1=bs, op=ALU.mult))
            V(lambda e, m=m: e.tensor_tensor(out=Ec[:, :, m:2 * m], in0=TA[:, :, 0:m], in1=TB[:, :, 0:m], op=ALU.subtract))
            V(lambda e, m=m, bs=bs: e.tensor_tensor(out=TA[:, :, 0:m], in0=Ec[:, :, 0:m], in1=bs, op=ALU.mult))
            V(lambda e, m=m, bc=bc: e.tensor_tensor(out=TB[:, :, 0:m], in0=Es[:, :, 0:m], in1=bc, op=ALU.mult))
            V(lambda e, m=m: e.tensor_tensor(out=Es[:, :, m:2 * m], in0=TA[:, :, 0:m], in1=TB[:, :, 0:m], op=ALU.add))
            m *= 2
        V(lambda e: e.tensor_scalar(out=nEs[:], in0=Es[:], scalar1=-1.0, scalar2=None, op0=ALU.mult))
        V(lambda e: e.tensor_scalar(out=nEc[:], in0=Ec[:], scalar1=-1.0, scalar2=None, op0=ALU.mult))
        crb = cre[:].unsqueeze(2).broadcast_to([128, 32, L])
        cib = cim[:].unsqueeze(2).broadcast_to([128, 32, L])
        V(lambda e: e.tensor_tensor(out=TA[:], in0=Ec[:], in1=crb, op=ALU.mult))
        V(lambda e: e.tensor_tensor(out=Rb[:], in0=Es[:], in1=cib, op=ALU.mult))
        V(lambda e: e.tensor_tensor(out=TA[:], in0=TA[:], in1=Rb[:], op=ALU.add))
        V(lambda e: e.tensor_tensor(out=TB[:], in0=Ec[:], in1=cib, op=ALU.mult))
        V(lambda e: e.tensor_tensor(out=Rb[:], in0=Es[:], in1=crb, op=ALU.mult))
        V(lambda e: e.tensor_tensor(out=TB[:], in0=TB[:], in1=Rb[:], op=ALU.subtract))
        V(lambda e: e.tensor_copy(out=Rb[:], in_=mag[:].unsqueeze(2).broadcast_to([128, 32, L])))
        tabs = one

        ur = Ring(es, nc, "ubf", 3, [128, 8, L], BF)
        u32 = Ring(es, nc, "u32", 3, [128, 8, L], F32)
        Pre = es.enter_context(_pst(nc, "Pre", [128, HT, L], F32)); Preb = Buf()
        Pim = es.enter_context(_pst(nc, "Pim", [128, HT, L], F32)); Pimb = Buf()
        Yp = Ring(es, nc, "Yp", 2, [128, 8, L], F32, psum=True)
        m1 = sb("m1", [128, HT, L]); m2 = sb("m2", [128, HT, L])
        cr_ = sb("cr_", [128, HT, L]); ci_ = sb("ci_", [128, HT, L])
        kr = sb("kr", [128, 32, L]); ki = sb("ki", [128, 32, L])
        m1b, m2b, crb_, cib_ = Buf(), Buf(), Buf(), Buf()
        krb = [Buf(), Buf()]; kib = [Buf(), Buf()]
        qr = [[Ring(es, nc, "q%d%d" % (k, hf), 2, [128, HT, L], BF) for hf in range(2)] for k in range(4)]
        hre = sb("hre", [128, 32]); him = sb("him", [128, 32]); hb_ = Buf()
        hta = sb("hta", [128, 32]); htb = sb("htb", [128, 32]); hsc = Buf()
        ysr = Ring(es, nc, "ys", 2, [128, 8, L], F32)
        g1 = Ring(es, nc, "g1", 2, [128, 8, L], F32)
        g2 = Ring(es, nc, "g2", 2, [128, 8, L], F32)
        zr = Ring(es, nc, "zr", 2, [128, 8, L], BF)

        zpad = sb("zpad", [128, 8, RP - SC0 - NSQ * SQL], BF)
        zpb = Buf()
        R.op("pool", lambda e: e.memset(zpad[:], 0.0), writes=[zpb])
        R.dma("sp", Z[:, SC0 + NSQ * SQL:RP].rearrange("(ct p) c -> p ct c", p=128), zpad[:], reads=[zpb], writes=[xbuf(C, "Zpad", 0, 0)])
        seqs = [(0, NPR, None)] + [(SC0 + s * SQL, SQL, s) for s in range(NSQ)]
        chunks = []
        for (q0, qlen, sidx) in seqs:
            t0s = list(range(0, qlen, L))
            for k, t0_ in enumerate(t0s):
                chunks.append(dict(col=q0 + t0_, n=min(L, qlen - t0_), sidx=sidx, first=(k == 0), last=(k == len(t0s) - 1)))

        def head(c):
            n = c["n"]; col = c["col"]; sidx = c["sidx"]
            if c["first"]:
                if sidx is None:
                    R.op("dve", lambda e: e.memset(hre[:], 0.0), writes=[hb_])
                    R.op("dve", lambda e: e.memset(him[:], 0.0), writes=[hb_])
                else:
                    R.dma("sp", hre[:], I["ssm_h0re"][sidx], writes=[hb_])
                    R.dma("sp", him[:], I["ssm_h0im"][sidx], writes=[hb_])
            ub, ubb = ur.next()
            uf, ufb = u32.next()
            R.dma("pool", ub[:, :, 0:n], U[1024:2048, col:col + n].rearrange("(ct p) c -> p ct c", p=128),
                  reads=[xbuf(C, "U", 8, 0)], writes=[ubb])
            R.dma("sp", uf[:, :, 0:n], U[1024:2048, col:col + n].rearrange("(ct p) c -> p ct c", p=128),
                  reads=[xbuf(C, "U", 8, 0)], writes=[ufb])
            prods = [[None, None] for _ in range(4)]
            for hf in range(2):
                for j in range(HT):
                    st = hf * HT + j
                    R.op("pe", lambda e, o=Pre[:, j, 0:n], l=Bre[:, st, :], r=ub[:, st // 4, 0:n]:
                         e.matmul(o, lhsT=l, rhs=r, start=True, stop=True),
                         reads=[wb_, ubb], writes=[Preb], inc=False)
                    R.op("pe", lambda e, o=Pim[:, j, 0:n], l=Bim[:, st, :], r=ub[:, st // 4, 0:n]:
                         e.matmul(o, lhsT=l, rhs=r, start=True, stop=True),
                         reads=[wb_, ubb], writes=[Pimb], inc=(j == HT - 1))
                sl = slice(hf * HT, (hf + 1) * HT)
                R.op("dve", lambda e, sl=sl: e.tensor_tensor(out=m1[:, :, 0:n], in0=Pre[:, :, 0:n], in1=TA[:, sl, 0:n], op=ALU.mult),
                     reads=[Preb, tabs], writes=[m1b])
                R.op("dve", lambda e, sl=sl: e.tensor_tensor(out=m2[:, :, 0:n], in0=Pim[:, :, 0:n], in1=TB[:, sl, 0:n], op=ALU.mult),
                     reads=[Pimb, tabs], writes=[m2b])
                R.op("dve", lambda e: e.tensor_tensor(out=cr_[:, :, 0:n], in0=m1[:, :, 0:n], in1=m2[:, :, 0:n], op=ALU.subtract),
                     reads=[m1b, m2b], writes=[crb_])
                R.op("dve", lambda e, sl=sl: e.tensor_tensor(out=m1[:, :, 0:n], in0=Pre[:, :, 0:n], in1=TB[:, sl, 0:n], op=ALU.mult),
                     reads=[Preb, tabs], writes=[m1b])
                R.op("dve", lambda e, sl=sl: e.tensor_tensor(out=m2[:, :, 0:n], in0=Pim[:, :, 0:n], in1=TA[:, sl, 0:n], op=ALU.mult),
                     reads=[Pimb, tabs], writes=[m2b])
                R.op("dve", lambda e: e.tensor_tensor(out=ci_[:, :, 0:n], in0=m1[:, :, 0:n], in1=m2[:, :, 0:n], op=ALU.add),
                     reads=[m1b, m2b], writes=[cib_])
                for j in range(HT):
                    st = hf * HT + j
                    R.op("dve", lambda e, st=st, j=j: e.tensor_tensor_scan(out=kr[:, st, 0:n], data0=Rb[:, st, 0:n], data1=cr_[:, j, 0:n],
                                                                         initial=hre[:, st:st + 1], op0=ALU.mult, op1=ALU.add),
                         reads=[crb_, hb_, tabs], writes=[krb[hf]])
                    R.op("dve", lambda e, st=st, j=j: e.tensor_tensor_scan(out=ki[:, st, 0:n], data0=Rb[:, st, 0:n], data1=ci_[:, j, 0:n],
                                                                         initial=him[:, st:st + 1], op0=ALU.mult, op1=ALU.add),
                         reads=[cib_, hb_, tabs], writes=[kib[hf]])
                for k, (srcT, srcB, tab) in enumerate(((kr, krb, Ec), (ki, kib, nEs), (kr, krb, nEs), (ki, kib, nEc))):
                    pt_, pb_ = qr[k][hf].next()
                    R.op("pool", lambda e, o=pt_, s_=srcT, t_=tab, sl=sl: e.tensor_tensor(out=o[:, :, 0:n], in0=s_[:, sl, 0:n], in1=t_[:, sl, 0:n], op=ALU.mult),
                         reads=[srcB[hf], tabs], writes=[pb_])
                    prods[k][hf] = (pt_, pb_)
            R.op("dve", lambda e: e.tensor_tensor(out=hta[:].unsqueeze(2), in0=kr[:, :, n - 1:n], in1=Ec[:, :, n - 1:n], op=ALU.mult),
                 reads=[krb[0], krb[1], tabs], writes=[hsc])
            R.op("dve", lambda e: e.tensor_tensor(out=htb[:].unsqueeze(2), in0=ki[:, :, n - 1:n], in1=Es[:, :, n - 1:n], op=ALU.mult),
                 reads=[kib[0], kib[1], tabs], writes=[hsc])
            R.op("dve", lambda e: e.tensor_tensor(out=hre[:], in0=hta[:], in1=htb[:], op=ALU.subtract),
                 reads=[hsc], writes=[hb_])
            R.op("dve", lambda e: e.tensor_tensor(out=hta[:].unsqueeze(2), in0=kr[:, :, n - 1:n], in1=Es[:, :, n - 1:n], op=ALU.mult),
                 reads=[krb[0], krb[1], tabs], writes=[hsc])
            R.op("dve", lambda e: e.tensor_tensor(out=htb[:].unsqueeze(2), in0=ki[:, :, n - 1:n], in1=Ec[:, :, n - 1:n], op=ALU.mult),
                 reads=[kib[0], kib[1], tabs], writes=[hsc])
            R.op("dve", lambda e: e.tensor_tensor(out=him[:], in0=hta[:], in1=htb[:], op=ALU.add),
                 reads=[hsc], writes=[hb_])
            if c["last"]:
                if sidx is None:
                    R.dma("sp", O["hre_p"], hre[:], reads=[hb_], writes=[xbuf(C, "hst", 0, 0)])
                    R.dma("sp", O["him_p"], him[:], reads=[hb_], writes=[xbuf(C, "hst", 1, 0)])
                else:
                    R.dma("sp", O["hre_s"][sidx], hre[:], reads=[hb_], writes=[xbuf(C, "hst", 2, sidx)])
                    R.dma("sp", O["him_s"][sidx], him[:], reads=[hb_], writes=[xbuf(C, "hst", 3, sidx)])
            c["uf"] = (uf, ufb)
            c["prods"] = prods

        def tail(c):
            n = c["n"]; col = c["col"]
            uf, ufb = c["uf"]
            prods = c["prods"]
            yp, ypb = Yp.next()
            first = True
            for ct in range(8):
                for jj in range(4):
                    st = ct * 4 + jj
                    hf = st // HT
                    j = st % HT
                    for k, Wt in ((0, Cre), (1, Cre), (2, Cim), (3, Cim)):
                        pp, ppb = prods[k][hf]
                        last = (ct == 7 and jj == 3 and k == 3)
                        R.op("pe", lambda e, o=yp[:, ct, 0:n], l=Wt[:, st, :], r=pp[:, j, 0:n], s=first:
                             e.matmul(o, lhsT=l, rhs=r, start=s, stop=False, skip_group_check=True),
                             reads=[wb_, ppb], writes=[ypb], inc=last)
                        first = False
            ys, ysb = ysr.next()
            for ct in range(8):
                R.op("dve", lambda e, ct=ct, ys=ys, uf=uf, yp=yp: e.scalar_tensor_tensor(out=ys[:, ct, 0:n], in0=uf[:, ct, 0:n], scalar=C.ssmd[:, ct:ct + 1],
                                                                    in1=yp[:, ct, 0:n], op0=ALU.mult, op1=ALU.add),
                     reads=[ufb, ypb], writes=[ysb])
            a1, a1b = g1.next(); a2, a2b = g2.next()
            R.op("act", lambda e, a1=a1, ys=ys: e.activation(out=a1[:, :, 0:n], in_=ys[:, :, 0:n], func=AF.Square), reads=[ysb], writes=[a1b])
            R.op("dve", lambda e, a1=a1, a2=a2: e.tensor_scalar(out=a2[:, :, 0:n], in0=a1[:, :, 0:n], scalar1=0.044715, scalar2=1.0, op0=ALU.mult, op1=ALU.add),
                 reads=[a1b], writes=[a2b])
            R.op("dve", lambda e, a1=a1, a2=a2, ys=ys: e.tensor_tensor(out=a1[:, :, 0:n], in0=a2[:, :, 0:n], in1=ys[:, :, 0:n], op=ALU.mult),
                 reads=[a2b, ysb], writes=[a1b])
            R.op("act", lambda e, a1=a1, a2=a2: e.activation(out=a2[:, :, 0:n], in_=a1[:, :, 0:n], func=AF.Sigmoid, scale=2.0 * 0.7978845608028654),
                 reads=[a1b], writes=[a2b])
            zt, ztb = zr.next()
            R.op("dve", lambda e, zt=zt, a2=a2, ys=ys: e.tensor_tensor(out=zt[:, :, 0:n], in0=a2[:, :, 0:n], in1=ys[:, :, 0:n], op=ALU.mult),
                 reads=[a2b, ysb], writes=[ztb])
            R.dma("sp", Z[:, col:col + n].rearrange("(ct p) c -> p ct c", p=128), zt[:, :, 0:n], reads=[ztb],
                  writes=[xbuf(C, "Z", 0, 0)])

        pending = None
        for c in chunks:
            head(c)
            if pending is not None:
                tail(pending)
            pending = c
        tail(pending)
        R.flush()
    R.barrier()


def glu_stage(R, nc, C, Z, YP, I):
    wv = I["ssm_w_glu"].rearrange("(kt p) m -> p kt m", p=128)
    with ExitStack() as es:
        zf = es.enter_context(_sbt(nc, "zf", [128, 8, RP], BF))
        zb = Buf()
        wr = Ring(es, nc, "wg", 2, [128, 8, 128], BF)
        pr = Ring(es, nc, "pgl", 3, [128, 512], F32, psum=True)
        sr = Ring(es, nc, "sgl", 2, [128, 512], F32)
        orr = Ring(es, nc, "ogl", 2, [128, 512], BF)
        for ct in range(8):
            R.dma("sp", zf[:, ct, :], Z[ct * 128:(ct + 1) * 128, :], reads=[xbuf(C, "Z", 0, 0)], writes=[zb])
        for mc in range(8):
            wt, wb = wr.next()
            R.dma("pool", wt[:], wv[:, :, mc * 128:(mc + 1) * 128], writes=[wb])
            for cc in range(0, RP, 512):
                n = min(512, RP - cc)
                p, pb = pr.next()
                for kt in range(8):
                    R.op("pe", lambda e, o=p[:, 0:n], l=wt[:, kt, :], r=zf[:, kt, cc:cc + n], s=(kt == 0), q=(kt == 7):
                         e.matmul(o, lhsT=l, rhs=r, start=s, stop=q), reads=[wb, zb], writes=[pb], inc=(kt == 7))
                st, sb_ = sr.next()
                R.op("act", lambda e, o=st[:, 0:n], i=p[:, 0:n], b=C.bglu[:, mc:mc + 1]:
                     e.activation(out=o, in_=i, func=AF.Sigmoid, bias=b), reads=[pb], writes=[sb_])
                ot, ob = orr.next()
                R.op("dve", lambda e, o=ot[:, 0:n], a=st[:, 0:n], b=zf[:, mc, cc:cc + n]:
                     e.tensor_tensor(out=o, in0=a, in1=b, op=ALU.mult), reads=[sb_, zb], writes=[ob])
                R.dma("sp", YP[(8 + mc) * 128:(9 + mc) * 128, cc:cc + n], ot[:, 0:n], reads=[ob],
                      writes=[xbuf(C, "YP", 8 + mc, 0)])
        R.flush()
    R.barrier()


def attn_core(R, nc, C, A, groups, kblocks, ncols, diag):
    blks = []
    for bi in range(len(kblocks) - 1, -1, -1):
        k0, nk = kblocks[bi]
        cstart, mask = diag(k0, nk)
        if cstart < ncols:
            blks.append((k0, nk, cstart, mask))
    n = len(blks)
    st = [dict() for _ in range(n)]
    Rs = []
    for _ in range(3):
        rt, rb = A.rr.next()
        R.op("pool", lambda e, o=rt: e.memset(o[:, 0:ncols], 0.0), writes=[rb])
        Rs.append((rt, rb))
    firstpv = [True]

    def P1(i):
        k0, nk, cs, mask = blks[i]
        ps, psb = A.psr.next()
        for (gc0, gn, qT, kTf, vf, gdeps) in groups:
            lo = max(gc0, cs)
            if lo >= gc0 + gn:
                continue
            R.op("pe", lambda e, o=ps[0:nk, lo:gc0 + gn], l=kTf(k0, nk), r=qT[:, lo - gc0:gn]:
                 e.matmul(o, lhsT=l, rhs=r, start=True, stop=True), reads=gdeps, writes=[psb], inc=True)
        st[i]["ps"] = (ps, psb)

    def A1(i):
        k0, nk, cs, mask = blks[i]
        ps, psb = st[i]["ps"]
        et, eb = A.er.next()
        R.op("act", lambda e, o=et[0:nk, cs:ncols], i_=ps[0:nk, cs:ncols]:
             e.activation(out=o, in_=i_, func=AF.Exp, scale=A.scale), reads=[psb], writes=[eb])
        spt, spb = A.spr.next()
        R.op("act", lambda e, o=spt[0:nk, cs:ncols], i_=et[0:nk, cs:ncols], b=C.oneb[0:nk, 0:1]:
             e.activation(out=o, in_=i_, func=AF.Ln, bias=b), reads=[eb], writes=[spb])
        if mask is not None:
            for (mc0, mn, map_) in mask:
                R.op("dve", lambda e, o=spt[0:nk, mc0:mc0 + mn], m_=map_:
                     e.tensor_tensor(out=o, in0=o, in1=m_, op=ALU.mult), reads=[spb], writes=[spb])
        st[i]["e"] = (et, eb)
        st[i]["sp"] = (spt, spb)

    def D1(i):
        if i == n - 1:
            return
        k0, nk, cs, mask = blks[i]
        spt, spb = st[i]["sp"]
        rp, rpb = Rs[i % 3]
        rn, rnb = Rs[(i + 1) % 3]
        R.op("dve", lambda e, o=rn[0:nk, cs:ncols], a=spt[0:nk, cs:ncols], b=rp[0:nk, cs:ncols]:
             e.tensor_tensor(out=o, in0=a, in1=b, op=ALU.add), reads=[spb, rpb], writes=[rnb])

    def P2(i):
        k0, nk, cs, mask = blks[i]
        spt, spb = st[i]["sp"]
        pc, pcb = A.pcr.next()
        R.op("pe", lambda e, o=pc[0:nk, cs:ncols], l=C.LTb[0:nk, 0:nk], r=spt[0:nk, cs:ncols], s_=(i == 0):
             e.matmul(o, lhsT=l, rhs=r, start=True, stop=s_), reads=[spb], writes=[pcb], inc=(i == 0))
        if i > 0:
            rp, rpb = Rs[i % 3]
            R.op("pe", lambda e, o=pc[0:nk, cs:ncols], l=C.onesb[0:128, 0:nk], r=rp[0:128, cs:ncols]:
                 e.matmul(o, lhsT=l, rhs=r, start=False, stop=True, skip_group_check=True), reads=[rpb], writes=[pcb], inc=True)
        st[i]["pc"] = (pc, pcb)

    def A2(i):
        k0, nk, cs, mask = blks[i]
        pc, pcb = st[i]["pc"]
        rt2, r2b = A.xr.next()
        R.op("act", lambda e, o=rt2[0:nk, cs:ncols], i_=pc[0:nk, cs:ncols]:
             e.activation(out=o, in_=i_, func=AF.Exp, scale=-1.0), reads=[pcb], writes=[r2b])
        st[i]["r"] = (rt2, r2b)

    def D2(i):
        k0, nk, cs, mask = blks[i]
        et, eb = st[i]["e"]
        rt2, r2b = st[i]["r"]
        wt, wb = A.wr.next()
        R.op("dve", lambda e, o=wt[0:nk, cs:ncols], a=et[0:nk, cs:ncols], b=rt2[0:nk, cs:ncols]:
             e.tensor_tensor(out=o, in0=a, in1=b, op=ALU.mult), reads=[eb, r2b], writes=[wb])
        if mask is not None:
            for (mc0, mn, map_) in mask:
                R.op("dve", lambda e, o=wt[0:nk, mc0:mc0 + mn], m_=map_:
                     e.tensor_tensor(out=o, in0=o, in1=m_, op=ALU.mult), reads=[wb], writes=[wb])
        st[i]["w"] = (wt, wb)

    def P3(i):
        k0, nk, cs, mask = blks[i]
        wt, wb = st[i]["w"]
        for (gc0, gn, qT, kTf, vf, gdeps) in groups:
            lo = max(gc0, cs)
            if lo >= gc0 + gn:
                continue
            R.op("pe", lambda e, o=A.po[:, lo:gc0 + gn], l=vf(k0, nk), r=wt[0:nk, lo:gc0 + gn], s_=firstpv[0]:
                 e.matmul(o, lhsT=l, rhs=r, start=s_, stop=False, skip_group_check=True),
                 reads=[wb] + list(gdeps), writes=[A.pob], inc=True)
            firstpv[0] = False

    for step in range(n + 2):
        if step < n:
            P1(step)
        if 0 <= step - 1 < n:
            P2(step - 1)
        if 0 <= step - 2 < n:
            P3(step - 2)
        if step < n:
            A1(step)
            D1(step)
        if 0 <= step - 1 < n:
            A2(step - 1)
            D2(step - 1)


def alloc_attn(es, nc, A):
    A.psr = Ring(es, nc, "aps", 2, [128, 512], F32, psum=True)
    A.pcr = Ring(es, nc, "apc", 2, [128, 512], F32, psum=True)
    A.po = es.enter_context(_pst(nc, "apo", [128, 512], F32))
    A.pob = Buf()
    A.er = Ring(es, nc, "ae", 4, [128, 512], F32)
    A.spr = Ring(es, nc, "asp", 4, [128, 512], BF)
    A.rr = Ring(es, nc, "arr", 3, [128, 512], BF)
    A.xr = Ring(es, nc, "axr", 3, [128, 512], F32)
    A.wr = Ring(es, nc, "awr", 4, [128, 512], BF)
    A.scale = 1.0 / math.sqrt(128.0)


def attn_prompt_stage(R, nc, C, Q, K, V, OUT):
    with ExitStack() as es:
        A = Ctx()
        alloc_attn(es, nc, A)
        qr = Ring(es, nc, "aq", 2, [128, NPR], BF)
        kr = Ring(es, nc, "ak", 2, [128, NPR], BF)
        vfr = Ring(es, nc, "avf", 2, [128, NPR], BF)
        vtr = Ring(es, nc, "avt", 2, [128, 33, 128], BF)
        ptr = Ring(es, nc, "apt", 2, [128, 512], BF, psum=True)
        oo = Ring(es, nc, "aoo", 2, [128, 512], BF)
        kblocks = [(k0, min(128, NPR - k0)) for k0 in range(0, NPR, 128)]
        for h in range(NHEAD):
            qt, qb = qr.next(); kt, kb = kr.next(); vf, vfb = vfr.next(); vt, vtb = vtr.next()
            R.dma("sp", qt[:], Q[h * 128:(h + 1) * 128, 0:NPR], reads=[xbuf(C, "Q", h, 0)], writes=[qb])
            R.dma("sp", kt[:], K[h * 128:(h + 1) * 128, 0:NPR], reads=[xbuf(C, "K", h, 0)], writes=[kb])
            R.dma("sp", vf[:], V[h * 128:(h + 1) * 128, 0:NPR], reads=[xbuf(C, "V", h, 0)], writes=[vfb])
            for b0 in range(0, len(kblocks), 4):
                pt, ptb = ptr.next()
                blks = kblocks[b0:b0 + 4]
                for j, (k0, nk) in enumerate(blks):
                    R.op("pe", lambda e, o=pt[0:nk, j * 128:(j + 1) * 128], i=vf[:, k0:k0 + nk]:
                         e.transpose(o, i, C.identb[:]), reads=[vfb], writes=[ptb], inc=(j == len(blks) - 1))
                for j, (k0, nk) in enumerate(blks):
                    R.op("dve", lambda e, o=vt[0:nk, b0 + j, :], i=pt[0:nk, j * 128:(j + 1) * 128]:
                         e.tensor_copy(out=o, in_=i), reads=[ptb], writes=[vtb])
            for q0 in range(0, NPR, 512):
                nq = min(512, NPR - q0)
                kb_list = [kbk for kbk in kblocks if kbk[0] < q0 + nq]

                def diag(k0, nk, q0=q0, nq=nq):
                    if k0 + nk <= q0:
                        return 0, None
                    cs = k0 - q0
                    mn = min(nk, nq - cs)
                    return cs, [(cs, mn, C.trib[0:nk, 0:mn])]
                groups = [(0, nq, qt[:, q0:q0 + nq],
                           (lambda k0, nk, kt=kt: kt[:, k0:k0 + nk]),
                           (lambda k0, nk, vt=vt: vt[0:nk, k0 // 128, :]),
                           [qb, kb, vtb])]
                attn_core(R, nc, C, A, groups, kb_list, nq, diag)
                ot, ob = oo.next()
                R.op("act", lambda e, o=ot[:, 0:nq], i=A.po[:, 0:nq]: e.activation(out=o, in_=i, func=AF.Copy),
                     reads=[A.pob], writes=[ob])
                R.dma("sp", OUT[h * 128:(h + 1) * 128, q0:q0 + nq], ot[:, 0:nq], reads=[ob],
                      writes=[xbuf(C, "YP", h, 0)])
        R.flush()
    R.barrier()


def attn_sample_stage(R, nc, C, Q, K, V, OUT, I):
    HG = 4
    NK = PAST + SQL
    with ExitStack() as es:
        A = Ctx()
        alloc_attn(es, nc, A)
        kr = Ring(es, nc, "sk", 2, [128, HG, NK], BF)
        vr = Ring(es, nc, "sv", 2, [128, 32, HG * 128], BF)
        qr = Ring(es, nc, "sq", 2, [128, HG, SQL], BF)
        vnf = Ring(es, nc, "svn", 2, [128, HG, SQL], BF)
        vnt = Ring(es, nc, "svt", 2, [SQL, HG, 128], BF)
        ptr = Ring(es, nc, "spt", 2, [128, 512], BF, psum=True)
        oo = Ring(es, nc, "soo", 2, [128, HG * SQL], BF)
        kblocks = [(k0, 128) for k0 in range(0, PAST, 128)] + [(PAST, SQL)]
        for s in range(NSQ):
            col = SC0 + s * SQL
            for hg in range(16 // HG):
                kt, kb = kr.next(); vt, vb = vr.next(); qt, qb = qr.next(); vn, vnb = vnf.next(); vtt, vttb = vnt.next()
                h0 = hg * HG
                R.dma("pool", kt[:, :, 0:PAST], I["cache_kT"][s, h0:h0 + HG].rearrange("h d k -> d h k"), writes=[kb])
                R.dma("sp", kt[:, :, PAST:NK], K[h0 * 128:(h0 + HG) * 128, col:col + SQL].rearrange("(h d) c -> d h c", d=128),
                      reads=[xbuf(C, "K", 0, 0)], writes=[kb])
                R.dma("pool", vt[:], I["cache_v"][s][:, h0 * 128:(h0 + HG) * 128].rearrange("(b p) c -> p b c", p=128), writes=[vb])
                R.dma("sp", qt[:], Q[h0 * 128:(h0 + HG) * 128, col:col + SQL].rearrange("(h d) c -> d h c", d=128),
                      reads=[xbuf(C, "Q", 0, 0)], writes=[qb])
                R.dma("sp", vn[:], V[h0 * 128:(h0 + HG) * 128, col:col + SQL].rearrange("(h d) c -> d h c", d=128),
                      reads=[xbuf(C, "V", 0, 0)], writes=[vnb])
                pt, ptb = ptr.next()
                for j in range(HG):
                    R.op("pe", lambda e, o=pt[0:SQL, j * 128:(j + 1) * 128], i=vn[:, j, :]:
                         e.transpose(o, i, C.identb[:]), reads=[vnb], writes=[ptb], inc=(j == HG - 1))
                R.op("dve", lambda e, o=vtt[:], i=pt[0:SQL, 0:HG * 128].rearrange("p (h d) -> p h d", d=128):
                     e.tensor_copy(out=o, in_=i), reads=[ptb], writes=[vttb])

                def diag(k0, nk):
                    if k0 < PAST:
                        return 0, None
                    return 0, [(j * SQL, SQL, C.trib[0:SQL, 0:SQL]) for j in range(HG)]
                groups = []
                for j in range(HG):
                    groups.append((j * SQL, SQL, qt[:, j, :],
                                   (lambda k0, nk, kt=kt, j=j: kt[:, j, k0:k0 + nk]),
                                   (lambda k0, nk, vt=vt, vtt=vtt, j=j: (vt[:, k0 // 128, j * 128:(j + 1) * 128] if k0 < PAST else vtt[:, j, :])),
                                   [qb, kb, vb, vttb]))
                attn_core(R, nc, C, A, groups, kblocks, HG * SQL, diag)
                ot, ob = oo.next()
                R.op("act", lambda e, o=ot[:], i=A.po[:, 0:HG * SQL]: e.activation(out=o, in_=i, func=AF.Copy),
                     reads=[A.pob], writes=[ob])
                R.dma("sp", OUT[h0 * 128:(h0 + HG) * 128, col:col + SQL].rearrange("(h d) c -> d h c", d=128),
                      ot[:].rearrange("p (h c) -> p h c", c=SQL), reads=[ob], writes=[xbuf(C, "YP", 0, 1)])
        R.flush()
    R.barrier()


def build_program(upto=99):
    nc = bass.Bass("TRN2", target_bir_lowering=False)
    I = {}

    def inp(name, shape, dt=F32):
        I[name] = nc.dram_tensor(name, list(shape), dt, kind="ExternalInput").ap()
        return I[name]
    inp("xin", [D, RP])
    inp("ffn_w_gate", [4, D, FF]); inp("ffn_w_up", [4, D, FF]); inp("ffn_w_down", [4, FF, D])
    inp("ab_w_in", [D, D]); inp("ab_w_out", [D, D]); inp("pool_w", [4, 256, 256])
    inp("ssm_w_glu", [1024, 1024]); inp("sb_w_qkv", [D, 3 * D]); inp("sb_w_out", [D, D])
    inp("gall", [128, 7, 16]); inp("pscale", [128, 8]); inp("ssmd", [128, 8]); inp("bglu", [128, 8])
    inp("invcnt", [128, 4, 16])
    inp("ssm_are", [128, 32]); inp("ssm_aim", [128, 32]); inp("ssm_ldt", [128, 32])
    inp("ssm_Bre", [128, 32, 128]); inp("ssm_Bim", [128, 32, 128])
    inp("ssm_Cre", [128, 32, 128]); inp("ssm_Cim", [128, 32, 128])
    inp("ssm_h0re", [NSQ, 128, 32]); inp("ssm_h0im", [NSQ, 128, 32])
    inp("pool_hist", [128, 8, NSQ, 15])
    inp("cache_kT", [NSQ, 16, 128, PAST]); inp("cache_v", [NSQ, PAST, D])
    inp("c_tri", [128, 128]); inp("c_lt", [128, 128]); inp("c_ident", [128, 128])
    O = {}

    def outp(name, shape, dt=F32):
        O[name] = nc.dram_tensor(name, list(shape), dt, kind="ExternalOutput").ap()
        return O[name]
    outp("y", [D, RP]); outp("kout", [D, RP]); outp("vout", [D, RP])
    outp("pool_p", [1024, 15]); outp("pool_s", [1024, NSQ, 15])
    outp("hre_p", [128, 32]); outp("him_p", [128, 32]); outp("hre_s", [NSQ, 128, 32]); outp("him_s", [NSQ, 128, 32])
    X = nc.dram_tensor("Xs", [D, RP], F32).ap()
    U = nc.dram_tensor("Us", [D, RP], F32).ap()
    Z = nc.dram_tensor("Zs", [1024, RP], BF).ap()
    YP = nc.dram_tensor("YPs", [D, RP], BF).ap()
    Qs = nc.dram_tensor("Qs", [D, RP], BF).ap()
    Ks = nc.dram_tensor("Ks", [D, RP], BF).ap()
    Vs = nc.dram_tensor("Vs", [D, RP], BF).ap()

    R = Rec(nc)
    C = Ctx()
    C.dbufs = {}
    with ExitStack() as es:
        R.begin(es)
        def sb(name, shape, dt=F32):
            return es.enter_context(_sbt(nc, name, shape, dt))
        C.ones32 = sb("ones32", [128, 128]); C.onesb = sb("onesb", [128, 128], BF)
        C.trib = sb("trib", [128, 128], BF); C.LTb = sb("LTb", [128, 128], BF); C.identb = sb("identb", [128, 128], BF)
        C.gall = sb("gall", [128, 7, 16]); C.pscale = sb("pscale", [128, 8]); C.ssmd = sb("ssmd", [128, 8])
        C.bglu = sb("bglu", [128, 8]); C.invcnt = sb("invcnt", [128, 4, 16])
        C.epsb = sb("epsb", [128, 1]); C.oneb = sb("oneb", [128, 1]); C.halfpi = sb("halfpi", [128, 1])
        cb = Buf()
        R.op("dve", lambda e: e.memset(C.ones32[:], 1.0), writes=[cb])
        R.op("dve", lambda e: e.memset(C.onesb[:], 1.0), writes=[cb])
        R.op("dve", lambda e: e.memset(C.epsb[:], EPS), writes=[cb])
        R.op("dve", lambda e: e.memset(C.oneb[:], 1.0), writes=[cb])
        R.op("dve", lambda e: e.memset(C.halfpi[:], math.pi / 2), writes=[cb])
        R.dma("pool", C.trib[:], I["c_tri"], writes=[cb])
        R.dma("pool", C.LTb[:], I["c_lt"], writes=[cb])
        R.dma("pool", C.identb[:], I["c_ident"], writes=[cb])
        for nm in ("gall", "pscale", "ssmd", "bglu", "invcnt"):
            R.dma("sp", getattr(C, nm)[:], I[nm], writes=[cb])
        R.barrier()

        wgate = I["ffn_w_gate"]; wup = I["ffn_w_up"]; wdn = I["ffn_w_down"]
        stage = 0

        def go():
            nonlocal stage
            stage += 1
            return stage <= upto
        skipffn = os.environ.get("MK_SKIPFFN", "0") == "1"
        if go():
            if skipffn:
                for dk in range(16):
                    R.dma("sp", X[dk * 128:(dk + 1) * 128, :], I["xin"][dk * 128:(dk + 1) * 128, :], writes=[xbuf(C, "X", dk, 0)])
                R.barrier()
            else:
                ffn_stage(R, nc, C, I["xin"], "xin", X, "X", wgate[0], wup[0], wdn[0], 0)
        if go():
            def cb_u(es2):
                orr = Ring(es2, nc, "uo", 3, [128, 448], F32)

                def f(si, mc, c0, n, p, pb):
                    ot, ob = orr.next()
                    R.op("act", lambda e, o=ot[:, 0:n], i=p[:, 0:n]: e.activation(out=o, in_=i, func=AF.Copy), reads=[pb], writes=[ob])
                    R.dma("sp", U[mc * 128:(mc + 1) * 128, c0:c0 + n], ot[:, 0:n], reads=[ob],
                          writes=[xbuf(C, "U", mc, 0), xbuf(C, "U", 8, 0)] if mc >= 8 else [xbuf(C, "U", mc, 0)])
                return f
            norm_linear_stage(R, nc, C, X, "X", 4, I["ab_w_in"], D, cb_u)
            R.dma("sp", O["pool_p"], U[0:1024, NPR - 15:NPR], reads=[xbuf(C, "U", k, 0) for k in range(8)], writes=[xbuf(C, "pp", 0, 0)])
            for s in range(NSQ):
                R.dma("sp", O["pool_s"][:, s, :], U[0:1024, SC0 + s * SQL + 1:SC0 + (s + 1) * SQL],
                      reads=[xbuf(C, "U", k, 0) for k in range(8)], writes=[xbuf(C, "pp", 1, s)])
        if go():
            pool_stage(R, nc, C, U, YP, I)
        if go():
            ssm_stage(R, nc, C, U, Z, I, O)
        if go():
            glu_stage(R, nc, C, Z, YP, I)
        if go():
            for kt in range(16):
                for si in range(len(SUP)):
                    C.dbufs[("YPl", kt, si)] = Buf()
            linear_residual_stage(R, nc, C, YP, "YPl", I["ab_w_out"], X, "X")
        if go() and not skipffn:
            ffn_stage(R, nc, C, X, "X", X, "X", wgate[1], wup[1], wdn[1], 1)
        if go() and not skipffn:
            ffn_stage(R, nc, C, X, "X", X, "X", wgate[2], wup[2], wdn[2], 2)
        if go():
            def cb_qkv(es2):
                o32 = Ring(es2, nc, "qo32", 3, [128, 448], F32)
                obf = Ring(es2, nc, "qobf", 3, [128, 448], BF)

                def f(si, mc, c0, n, p, pb):
                    which = mc // 16
                    hh = mc % 16
                    dst = (Qs, Ks, Vs)[which]
                    ot, ob = obf.next()
                    if which == 0:
                        R.op("act", lambda e, o=ot[:, 0:n], i=p[:, 0:n]: e.activation(out=o, in_=i, func=AF.Copy), reads=[pb], writes=[ob])
                    else:
                        o2, o2b = o32.next()
                        R.op("act", lambda e, o=o2[:, 0:n], i=p[:, 0:n]: e.activation(out=o, in_=i, func=AF.Copy), reads=[pb], writes=[o2b])
                        R.dma("sp", (O["kout"], O["vout"])[which - 1][hh * 128:(hh + 1) * 128, c0:c0 + n], o2[:, 0:n],
                              reads=[o2b], writes=[xbuf(C, "kvout", which, mc)])
                        R.op("dve", lambda e, o=ot[:, 0:n], i=o2[:, 0:n]: e.tensor_copy(out=o, in_=i), reads=[o2b], writes=[ob])
                    R.dma("sp", dst[hh * 128:(hh + 1) * 128, c0:c0 + n], ot[:, 0:n], reads=[ob],
                          writes=[xbuf(C, "QKV", which, mc)])
                return f
            norm_linear_stage(R, nc, C, X, "X", 5, I["sb_w_qkv"], 3 * D, cb_qkv)
        if go():
            attn_prompt_stage(R, nc, C, Qs, Ks, Vs, YP)
        if go():
            attn_sample_stage(R, nc, C, Qs, Ks, Vs, YP, I)
        if go():
            for kt in range(16):
                for si in range(len(SUP)):
                    C.dbufs[("YPm", kt, si)] = Buf()
            linear_residual_stage(R, nc, C, YP, "YPm", I["sb_w_out"], X, "X")
        if go() and not skipffn:
            ffn_stage(R, nc, C, X, "X", X, "X", wgate[3], wup[3], wdn[3], 3)
        if go():
            final_stage2(R, nc, C, X, "X", O["y"])
        if upto < 99:
            dsrc = {1: X, 2: U, 6: X, 7: X, 8: X, 12: X, 13: X}.get(upto, X)
            for dk in range(16):
                R.dma("sp", O["y"][dk * 128:(dk + 1) * 128, :], dsrc[dk * 128:(dk + 1) * 128, :], writes=[xbuf(C, "Ydump", dk, 0)])
        R.finish()
        print('NOPS', upto, R.nops, 'sem cnt', R.cnt, 'dq', R.dq)
    return nc


def _state_layout(a):
    return np.ascontiguousarray(a.reshape(32, 2, 64).transpose(1, 2, 0).reshape(128, 32))


def _vec128(v):
    return np.ascontiguousarray(v.reshape(-1, 128).T)


_PROG = {}


def kernel(**inp):
    f32 = np.float32
    g = lambda k: np.asarray(inp[k], dtype=f32)
    upto = int(os.environ.get("MK_UPTO", "99"))
    if upto not in _PROG:
        _PROG[upto] = build_program(upto)
    nc = _PROG[upto]
    x_prompt = g("x_prompt"); x_sample = g("x_sample"); meta = g("meta_tokens")
    shared = {}
    shared["ffn_w_gate"] = g("ffn_w_gate").reshape(4, D, FF)
    shared["ffn_w_up"] = g("ffn_w_up").reshape(4, D, FF)
    shared["ffn_w_down"] = g("ffn_w_down").reshape(4, FF, D)
    shared["ab_w_in"] = g("ab_w_in")[0]; shared["ab_w_out"] = g("ab_w_out")[0]
    shared["pool_w"] = g("pool_w")[0]; shared["ssm_w_glu"] = g("ssm_w_glu")[0]
    shared["sb_w_qkv"] = g("sb_w_qkv")[0]; shared["sb_w_out"] = g("sb_w_out")[0]
    fn = g("ffn_norm").reshape(4, D); mn = g("mix_norm"); fin = g("final_norm")
    gall = np.stack([_vec128(fn[0]), _vec128(fn[1]), _vec128(fn[2]), _vec128(fn[3]),
                     _vec128(mn[0]), _vec128(mn[1]), _vec128(fin)], axis=1)
    shared["gall"] = np.ascontiguousarray(gall)
    shared["pscale"] = _vec128(g("pool_scale")[0]); shared["ssmd"] = _vec128(g("ssm_d")[0])
    shared["bglu"] = _vec128(g("ssm_b_glu")[0])
    ic = np.zeros((128, 4, 16), f32)
    for gi in range(4):
        w = 2 << gi
        ic[:, gi, :] = 1.0 / np.minimum(np.arange(16) + 1, w)
    shared["invcnt"] = ic
    shared["ssm_are"] = _state_layout(g("ssm_a_re")[0]); shared["ssm_aim"] = _state_layout(g("ssm_a_im")[0])
    shared["ssm_ldt"] = _state_layout(np.repeat(g("ssm_log_dt")[0][:, None], 64, axis=1))
    bre = g("ssm_b_re")[0]; bim = g("ssm_b_im")[0]; cre = g("ssm_c_re")[0]; cim = g("ssm_c_im")[0]
    Bre = np.zeros((128, 32, 128), f32); Bim = np.zeros((128, 32, 128), f32)
    Cre = np.zeros((128, 32, 128), f32); Cim = np.zeros((128, 32, 128), f32)
    for gg in range(64):
        st = gg // 2; gl = gg % 2
        ch0 = (gg % 8) * 16
        Bre[ch0:ch0 + 16, st, gl * 64:(gl + 1) * 64] = bre[gg].T
        Bim[ch0:ch0 + 16, st, gl * 64:(gl + 1) * 64] = bim[gg].T
        Cre[gl * 64:(gl + 1) * 64, st, ch0:ch0 + 16] = cre[gg].T
        Cim[gl * 64:(gl + 1) * 64, st, ch0:ch0 + 16] = cim[gg].T
    shared["ssm_Bre"] = Bre; shared["ssm_Bim"] = Bim; shared["ssm_Cre"] = Cre; shared["ssm_Cim"] = Cim
    jj = np.arange(128)
    shared["c_tri"] = (jj[:, None] < jj[None, :]).astype(f32)
    shared["c_lt"] = (jj[:, None] >= jj[None, :]).astype(f32)
    shared["c_ident"] = np.eye(128, dtype=f32)
    cache_pool = g("cache_pool")[0]; sre = g("state_ssm_re")[0]; sim_ = g("state_ssm_im")[0]
    cache_k = np.asarray(inp["cache_k"], dtype=f32)[0]; cache_v = np.asarray(inp["cache_v"], dtype=f32)[0]
    in_maps = []
    for c in range(8):
        b = c % 4
        m = dict(shared)
        xin = np.zeros((D, RP), f32)
        xin[:, 0:16] = meta.T
        xin[:, 16:NPR] = x_prompt[b].T
        sq = list(range(4 * c, 4 * c + 4))
        xin[:, SC0:SC0 + NSQ * SQL] = x_sample[sq].reshape(NSQ * SQL, D).T
        m["xin"] = xin
        m["ssm_h0re"] = np.ascontiguousarray(np.stack([_state_layout(sre[s]) for s in sq], axis=0))
        m["ssm_h0im"] = np.ascontiguousarray(np.stack([_state_layout(sim_[s]) for s in sq], axis=0))
        ph = cache_pool[sq]
        m["pool_hist"] = np.ascontiguousarray(ph.transpose(2, 0, 1).reshape(8, 128, NSQ, 15).transpose(1, 0, 2, 3))
        m["cache_kT"] = np.ascontiguousarray(cache_k[sq].transpose(0, 2, 3, 1))
        m["cache_v"] = np.ascontiguousarray(cache_v[sq].reshape(NSQ, PAST, D))
        in_maps.append(m)
    ncore = int(os.environ.get("MK_NCORE", "8"))
    if ncore < 8:
        res = run_bass_kernel_spmd(nc, in_maps[:ncore], core_ids=list(range(ncore)))
        return res.results
    res = run_bass_kernel_spmd(nc, in_maps, core_ids=list(range(8)))
    rs = res.results
    y_prompt = np.stack([rs[b]["y"][:, 16:NPR].T for b in range(4)], 0)
    y_sample = np.concatenate([rs[c]["y"][:, SC0:SC0 + NSQ * SQL].T.reshape(NSQ, SQL, D) for c in range(8)], 0)
    pool_p = np.stack([rs[b]["pool_p"].T for b in range(4)], 0)[None]
    pool_s = np.concatenate([rs[c]["pool_s"].transpose(1, 2, 0) for c in range(8)], 0)[None]

    def unstate(a):
        return a.reshape(2, 64, 32).transpose(2, 0, 1).reshape(64, 64)
    re_p = np.stack([unstate(rs[b]["hre_p"]) for b in range(4)], 0)[None]
    im_p = np.stack([unstate(rs[b]["him_p"]) for b in range(4)], 0)[None]
    re_s = np.stack([unstate(rs[c]["hre_s"][s]) for c in range(8) for s in range(NSQ)], 0)[None]
    im_s = np.stack([unstate(rs[c]["him_s"][s]) for c in range(8) for s in range(NSQ)], 0)[None]
    k_p = np.stack([rs[b]["kout"][:, 0:NPR].T.reshape(NPR, 16, 128) for b in range(4)], 0)[None]
    v_p = np.stack([rs[b]["vout"][:, 0:NPR].T.reshape(NPR, 16, 128) for b in range(4)], 0)[None]
    k_s = np.concatenate([rs[c]["kout"][:, SC0:SC0 + NSQ * SQL].T.reshape(NSQ, SQL, 16, 128) for c in range(8)], 0)[None]
    v_s = np.concatenate([rs[c]["vout"][:, SC0:SC0 + NSQ * SQL].T.reshape(NSQ, SQL, 16, 128) for c in range(8)], 0)[None]
    outs = (y_prompt, y_sample, pool_p, pool_s, re_p, im_p, re_s, im_s, k_p, v_p, k_s, v_s)
    return tuple(np.ascontiguousarray(o, dtype=f32) for o in outs)
```

```python
import os, math
import numpy as np
from contextlib import ExitStack
import concourse.bass as bass
import concourse.mybir as mybir
from concourse.bass_utils import run_bass_kernel_spmd

F32 = mybir.dt.float32
BF = mybir.dt.bfloat16
AF = mybir.ActivationFunctionType
ALU = mybir.AluOpType

RP = 4224
NPR = 4112
SC0 = 4112
NSQ = 4
SQL = 16
D = 2048
FF = 5632
NFC = 44
PAST = 4096
NHEAD = 16
SUP = [(0, 896), (896, 896), (1792, 896), (2688, 896), (3584, 640)]
LCH = 64
EPS = 1e-6


def cchunks(c0, W):
    h = W // 2
    return [(c0, h), (c0 + h, W - h)]


_UID = [0]


def _sbt(nc, name, shape, dt):
    _UID[0] += 1
    return nc.sbuf_tensor("%s_%d" % (name, _UID[0]), shape, dt)


def _pst(nc, name, shape, dt):
    _UID[0] += 1
    return nc.psum_tensor("%s_%d" % (name, _UID[0]), shape, dt)


class Buf:
    __slots__ = ("w", "r")

    def __init__(self):
        self.w = []
        self.r = {}


class _Cap:
    def __getattr__(self, name):
        def f(*a, **k):
            self.call = (name, a, k)
            return None
        return f


class Rec:
    ENG = ("pe", "act", "dve", "pool", "sp")
    K = 8

    def __init__(self, nc):
        self.nc = nc
        self.ops = {e: [] for e in self.ENG}
        self.cnt = {e: 0 for e in self.ENG}
        self.dq = {"sp": 0, "pool": 0, "act": 0}
        self.floor = {}
        self.nops = {e: 0 for e in self.ENG}

    def _deps(self, reads, writes, extra):
        deps = list(extra)
        for b in reads:
            deps += b.w
        for b in writes:
            deps += b.w
            deps += list(b.r.values())
        return deps

    def _upd(self, tok, key, reads, writes):
        for b in reads:
            b.r[key] = tok
        for b in writes:
            b.w = [tok]
            b.r = {}

    def op(self, eng, fn, reads=(), writes=(), inc=True, extra=()):
        deps = self._deps(reads, writes, extra)
        deps += self.floor.pop(eng, [])
        if inc:
            self.cnt[eng] += 1
            tok = ("c", eng, self.cnt[eng])
        else:
            tok = ("c", eng, self.cnt[eng] + 1)
        cap = _Cap()
        fn(cap)
        self.ops[eng].append(("op", cap.call, deps, inc))
        self._upd(tok, ("c", eng), reads, writes)
        return tok

    def dma(self, q, out, in_, reads=(), writes=(), extra=(), **kw):
        deps = self._deps(reads, writes, extra)
        deps += self.floor.pop(q, [])
        j = self.dq[q]
        self.dq[q] = j + 1
        slot = j % self.K
        val = 16 * (j // self.K + 1)
        if j >= self.K:
            deps.append(("d", q, slot, val - 16))
        tok = ("d", q, slot, val)
        self.ops[q].append(("dma", (lambda e, out=out, in_=in_, kw=kw: e.dma_start(out=out, in_=in_, **kw)), deps, slot))
        self._upd(tok, ("d", q, slot), reads, writes)
        return tok

    def all_tokens(self):
        toks = []
        for e in self.ENG:
            if self.cnt[e] > 0:
                toks.append(("c", e, self.cnt[e]))
        for q, j in self.dq.items():
            for slot in range(min(j, self.K)):
                n = (j - 1 - slot) // self.K + 1
                toks.append(("d", q, slot, 16 * n))
        return toks

    def barrier(self):
        toks = self.all_tokens()
        for e in self.ENG:
            self.floor[e] = list(toks) + self.floor.get(e, [])

    def begin(self, es):
        nc = self.nc
        self.csem = {e: es.enter_context(nc.semaphore("c_" + e)) for e in self.ENG}
        self.dsem = {q: [es.enter_context(nc.semaphore("d_%s%d" % (q, i))) for i in range(self.K)]
                     for q in self.dq}
        self.block = es.enter_context(nc.Block())
        self.waited = {e: {} for e in self.ENG}

    def flush(self):
        bname = {"pe": "tensor", "act": "scalar", "dve": "vector", "pool": "gpsimd", "sp": "sync"}
        csem, dsem = self.csem, self.dsem
        for ename in self.ENG:
            ops = self.ops[ename]
            self.nops[ename] += len(ops)
            self.ops[ename] = []
            if not ops:
                continue

            def body(e, ename=ename, ops=ops):
                waited = self.waited[ename]
                for (kind, fn, deps, aux) in ops:
                    for t in deps:
                        if t[0] == "c":
                            if t[1] == "pe" and ename == "pe":
                                continue
                            key = ("c", t[1]); val = t[2]; sem = csem[t[1]]
                        else:
                            key = ("d", t[1], t[2]); val = t[3]; sem = dsem[t[1]][t[2]]
                        if waited.get(key, 0) >= val:
                            continue
                        waited[key] = val
                        e.wait_ge(sem, val)
                    if kind == "wait":
                        continue
                    if kind == "op":
                        ins = getattr(e, fn[0])(*fn[1], **fn[2])
                    else:
                        ins = fn(e)
                    if kind == "dma":
                        ins.then_inc(dsem[ename][aux], 16)
                    elif aux:
                        ins.then_inc(csem[ename], 1)
            getattr(self.block, bname[ename])(body)

    def finish(self):
        final = self.all_tokens()
        self.ops["sp"].append(("wait", None, final, False))
        self.flush()


class Ring:
    def __init__(self, es, nc, name, n, shape, dtype, psum=False):
        self.t = []
        self.b = []
        for i in range(n):
            if psum:
                t = es.enter_context(_pst(nc, "%s%d" % (name, i), shape, dtype))
            else:
                t = es.enter_context(_sbt(nc, "%s%d" % (name, i), shape, dtype))
            self.t.append(t)
            self.b.append(Buf())
        self.i = 0

    def next(self):
        k = self.i % len(self.t)
        self.i += 1
        return self.t[k], self.b[k]


class Ctx:
    pass


def xbuf(C, name, dk, si):
    key = (name, dk, si)
    if key not in C.dbufs:
        C.dbufs[key] = Buf()
    return C.dbufs[key]


def phase_norm(R, nc, C, S, Xsrc, xname, si, c0, W, gi, hout, hbuf, hview=None):
    ch = cchunks(0, W)
    for dk in range(16):
        xt, xb = S.xr.next()
        R.dma("sp", xt[:, 0:W], Xsrc[dk * 128:(dk + 1) * 128, c0:c0 + W],
              reads=[xbuf(C, xname, dk, si)], writes=[xb])
        st, sb = S.sqr.next()
        R.op("act", lambda e, o=st[:, 0:W], i=xt[:, 0:W]: e.activation(out=o, in_=i, func=AF.Square),
             reads=[xb], writes=[sb])
        for ci, (cc, n) in enumerate(ch):
            R.op("pe", lambda e, o=S.pss[ci][:, 0:n], r=st[:, cc:cc + n], s=(dk == 0), p=(dk == 15):
                 e.matmul(o, lhsT=C.ones32[:], rhs=r, start=s, stop=p),
                 reads=[sb], writes=[S.pssb[ci]], inc=True)
    for ci, (cc, n) in enumerate(ch):
        R.op("act", lambda e, o=S.srt[:, cc:cc + n], i=S.pss[ci][:, 0:n]:
             e.activation(out=o, in_=i, func=AF.Sqrt, bias=C.epsb[:, 0:1], scale=1.0 / D),
             reads=[S.pssb[ci]], writes=[S.srtb])
    R.op("dve", lambda e: e.reciprocal(out=S.rstd[:, 0:W], in_=S.srt[:, 0:W]),
         reads=[S.srtb], writes=[S.rstdb])
    for dk in range(16):
        xt, xb = S.xr.next()
        R.dma("sp", xt[:, 0:W], Xsrc[dk * 128:(dk + 1) * 128, c0:c0 + W],
              reads=[xbuf(C, xname, dk, si)], writes=[xb])
        o = hout(dk)
        R.op("dve", lambda e, o=o, i=xt[:, 0:W], g=C.gall[:, gi, dk:dk + 1]:
             e.scalar_tensor_tensor(out=o, in0=i, scalar=g, in1=S.rstd[:, 0:W], op0=ALU.mult, op1=ALU.mult),
             reads=[xb, S.rstdb], writes=[hbuf(dk)])


def norm_tasks(R, C, S, Xsrc, xname, si, c0, W, gi, hout, hbuf):
    ch = cchunks(0, W)
    hold = {}

    def A(dk):
        xt, xb = S.xr.next()
        R.dma("sp", xt[:, 0:W], Xsrc[dk * 128:(dk + 1) * 128, c0:c0 + W],
              reads=[xbuf(C, xname, dk, si)], writes=[xb])
        st, sb = S.sqr.next()
        R.op("act", lambda e, o=st[:, 0:W], i=xt[:, 0:W]: e.activation(out=o, in_=i, func=AF.Square),
             reads=[xb], writes=[sb])
        hi, hib = S.hir.next()
        lo, lob = S.lor.next()
        R.op("dve", lambda e, o=hi[:, 0:W], i=st[:, 0:W]: e.tensor_copy(out=o, in_=i), reads=[sb], writes=[hib])
        R.op("dve", lambda e, o=lo[:, 0:W], a=st[:, 0:W], b=hi[:, 0:W]:
             e.tensor_tensor(out=o, in0=a, in1=b, op=ALU.subtract), reads=[sb, hib], writes=[lob])
        hold[("s", dk)] = (hi, hib, lo, lob)

    def B(dk):
        hi, hib, lo, lob = hold.pop(("s", dk))
        for ci, (cc, n) in enumerate(ch):
            R.op("pe", lambda e, o=S.pss[ci][:, 0:n], r=hi[:, cc:cc + n], s=(dk == 0):
                 e.matmul(o, lhsT=C.onesb[:], rhs=r, start=s, stop=False),
                 reads=[hib], writes=[S.pssb[ci]], inc=False)
            R.op("pe", lambda e, o=S.pss[ci][:, 0:n], r=lo[:, cc:cc + n], p=(dk == 15):
                 e.matmul(o, lhsT=C.onesb[:], rhs=r, start=False, stop=p),
                 reads=[lob], writes=[S.pssb[ci]], inc=True)

    def FIN():
        for ci, (cc, n) in enumerate(ch):
            R.op("act", lambda e, o=S.srt[:, cc:cc + n], i=S.pss[ci][:, 0:n]:
                 e.activation(out=o, in_=i, func=AF.Sqrt, bias=C.epsb[:, 0:1], scale=1.0 / D),
                 reads=[S.pssb[ci]], writes=[S.srtb])
        R.op("dve", lambda e: e.reciprocal(out=S.rstd[:, 0:W], in_=S.srt[:, 0:W]),
             reads=[S.srtb], writes=[S.rstdb])

    def A2(dk):
        xt, xb = S.xr.next()
        R.dma("sp", xt[:, 0:W], Xsrc[dk * 128:(dk + 1) * 128, c0:c0 + W],
              reads=[xbuf(C, xname, dk, si)], writes=[xb])
        hold[("x", dk)] = (xt, xb)

    def B2(dk):
        xt, xb = hold.pop(("x", dk))
        o = hout(dk)
        R.op("dve", lambda e, o=o, i=xt[:, 0:W], g=C.gall[:, gi, dk:dk + 1]:
             e.scalar_tensor_tensor(out=o, in0=i, scalar=g, in1=S.rstd[:, 0:W], op0=ALU.mult, op1=ALU.mult),
             reads=[xb, S.rstdb], writes=[hbuf(dk)])

    stats = [lambda: A(0)]
    for dk in range(16):
        if dk + 1 < 16:
            stats.append(lambda dk=dk: A(dk + 1))
        stats.append(lambda dk=dk: B(dk))
    stats.append(FIN)
    app = [lambda: A2(0)]
    for dk in range(16):
        if dk + 1 < 16:
            app.append(lambda dk=dk: A2(dk + 1))
        app.append(lambda dk=dk: B2(dk))
    return stats, app


def run_tasks(tasks, k):
    for _ in range(k):
        if tasks:
            tasks.pop(0)()


def alloc_norm(es, nc, S):
    S.xr = Ring(es, nc, "xr", 4, [128, 896], F32)
    S.sqr = Ring(es, nc, "sqr", 2, [128, 896], F32)
    S.hir = Ring(es, nc, "sqhi", 2, [128, 896], BF)
    S.lor = Ring(es, nc, "sqlo", 2, [128, 896], BF)
    S.pss = [es.enter_context(_pst(nc, "pss%d" % i, [128, 512], F32)) for i in range(2)]
    S.pssb = [Buf(), Buf()]
    S.srt = es.enter_context(_sbt(nc, "srt", [128, 896], F32))
    S.srtb = Buf()
    S.rstd = es.enter_context(_sbt(nc, "rstd", [128, 896], F32))
    S.rstdb = Buf()


def ffn_stage(R, nc, C, Xsrc, sname, Xdst, dname, wg, wu, wd, gi):
    wgv = wg.rearrange("(kt p) f -> p kt f", p=128)
    wuv = wu.rearrange("(kt p) f -> p kt f", p=128)
    wdv = wd.rearrange("(fc p) d -> p fc d", p=128)
    with ExitStack() as es:
        S = Ctx()
        alloc_norm(es, nc, S)
        hfm = es.enter_context(_sbt(nc, "hfm", [128, 16, 896], BF))
        hb = [Buf() for _ in range(16)]
        act = es.enter_context(_sbt(nc, "actf", [128, NFC, 896], BF))
        ab = [Buf() for _ in range(NFC)]
        wgr = Ring(es, nc, "wg", 2, [128, 16, 128], BF)
        wur = Ring(es, nc, "wu", 2, [128, 16, 128], BF)
        wdr = Ring(es, nc, "wd", 2, [128, NFC, 128], BF)
        pgr = Ring(es, nc, "pg", 2, [128, 512], F32, psum=True)
        pur = Ring(es, nc, "pu", 2, [128, 512], F32, psum=True)
        sgr = Ring(es, nc, "sg", 2, [128, 448], F32)
        xor_ = Ring(es, nc, "xo", 2, [128, 448], F32)
        rxr = Ring(es, nc, "rxr", 3, [128, 448], F32)
        def mk(si):
            c0n, Wn = SUP[si]
            return norm_tasks(R, C, S, Xsrc, sname, si, c0n, Wn, gi,
                              hout=lambda dk, Wn=Wn: hfm[:, dk, 0:Wn], hbuf=lambda dk: hb[dk])
        st0, ap0 = mk(0)
        run_tasks(st0, 99)
        run_tasks(ap0, 99)
        for si, (c0, W) in enumerate(SUP):
            if si + 1 < len(SUP):
                stt, apt = mk(si + 1)
            else:
                stt, apt = [], []
            ch = cchunks(0, W)
            for fc in range(NFC):
                wgt, wgb = wgr.next()
                wut, wub = wur.next()
                R.dma("pool", wgt[:], wgv[:, :, fc * 128:(fc + 1) * 128], writes=[wgb])
                R.dma("pool", wut[:], wuv[:, :, fc * 128:(fc + 1) * 128], writes=[wub])
                for (cc, n) in ch:
                    pg, pgb = pgr.next()
                    pu, pub = pur.next()
                    for kt in range(16):
                        R.op("pe", lambda e, o=pg[:, 0:n], l=wgt[:, kt, :], r=hfm[:, kt, cc:cc + n], s=(kt == 0), p=(kt == 15):
                             e.matmul(o, lhsT=l, rhs=r, start=s, stop=p),
                             reads=[wgb, hb[kt]], writes=[pgb], inc=(kt == 15))
                    for kt in range(16):
                        R.op("pe", lambda e, o=pu[:, 0:n], l=wut[:, kt, :], r=hfm[:, kt, cc:cc + n], s=(kt == 0), p=(kt == 15):
                             e.matmul(o, lhsT=l, rhs=r, start=s, stop=p),
                             reads=[wub, hb[kt]], writes=[pub], inc=(kt == 15))
                    sg, sgb = sgr.next()
                    R.op("act", lambda e, o=sg[:, 0:n], i=pg[:, 0:n]: e.activation(out=o, in_=i, func=AF.Silu),
                         reads=[pgb], writes=[sgb])
                    R.op("dve", lambda e, o=act[:, fc, cc:cc + n], a=sg[:, 0:n], b=pu[:, 0:n]:
                         e.tensor_tensor(out=o, in0=a, in1=b, op=ALU.mult),
                         reads=[sgb, pub], writes=[ab[fc]])
                run_tasks(stt, 1)
            run_tasks(stt, 99)
            for dcc in range(16):
                wdt, wdb = wdr.next()
                R.dma("pool", wdt[:], wdv[:, :, dcc * 128:(dcc + 1) * 128], writes=[wdb])
                for (cc, n) in ch:
                    py, pyb = pgr.next()
                    for fc in range(NFC):
                        R.op("pe", lambda e, o=py[:, 0:n], l=wdt[:, fc, :], r=act[:, fc, cc:cc + n], s=(fc == 0), p=(fc == NFC - 1):
                             e.matmul(o, lhsT=l, rhs=r, start=s, stop=p),
                             reads=[wdb, ab[fc]], writes=[pyb], inc=(fc == NFC - 1))
                    xt, xb = rxr.next()
                    R.dma("sp", xt[:, 0:n], Xsrc[dcc * 128:(dcc + 1) * 128, c0 + cc:c0 + cc + n],
                          reads=[xbuf(C, sname, dcc, si)], writes=[xb])
                    xo, xob = xor_.next()
                    R.op("dve", lambda e, o=xo[:, 0:n], a=py[:, 0:n], b=xt[:, 0:n]:
                         e.scalar_tensor_tensor(out=o, in0=a, scalar=0.5, in1=b, op0=ALU.mult, op1=ALU.add),
                         reads=[pyb, xb], writes=[xob])
                    R.dma("sp", Xdst[dcc * 128:(dcc + 1) * 128, c0 + cc:c0 + cc + n], xo[:, 0:n],
                          reads=[xob], writes=[xbuf(C, dname, dcc, si)])
                run_tasks(apt, 2 if dcc < 15 else 99)
        R.flush()
    R.barrier()


def norm_linear_stage(R, nc, C, Xsrc, sname, gi, Wd, M, out_cb):
    wv = Wd.rearrange("(kt p) m -> p kt m", p=128)
    with ExitStack() as es:
        S = Ctx()
        alloc_norm(es, nc, S)
        hfms = [es.enter_context(_sbt(nc, "hfm%d" % i, [128, 16, 896], BF)) for i in range(2)]
        hbs = [[Buf() for _ in range(16)] for _ in range(2)]
        wr = Ring(es, nc, "wl", 2, [128, 16, 128], BF)
        pr = Ring(es, nc, "pl", 3, [128, 512], F32, psum=True)
        S.es = es
        cbs = out_cb(es)

        def mk(si):
            c0n, Wn = SUP[si]
            hf_, hb_ = hfms[si % 2], hbs[si % 2]
            return norm_tasks(R, C, S, Xsrc, sname, si, c0n, Wn, gi,
                              hout=lambda dk, Wn=Wn, hf_=hf_: hf_[:, dk, 0:Wn], hbuf=lambda dk, hb_=hb_: hb_[dk])
        st0, ap0 = mk(0)
        run_tasks(st0, 99)
        run_tasks(ap0, 99)
        MG = M // 128
        for si, (c0, W) in enumerate(SUP):
            hfm, hb = hfms[si % 2], hbs[si % 2]
            if si + 1 < len(SUP):
                stt, apt = mk(si + 1)
                tasks = stt + apt
            else:
                tasks = []
            per = -(-len(tasks) // max(1, MG - 2))
            for mc in range(MG):
                wt, wb = wr.next()
                R.dma("pool", wt[:], wv[:, :, mc * 128:(mc + 1) * 128], writes=[wb])
                for (cc, n) in cchunks(0, W):
                    p, pb = pr.next()
                    for kt in range(16):
                        R.op("pe", lambda e, o=p[:, 0:n], l=wt[:, kt, :], r=hfm[:, kt, cc:cc + n], s=(kt == 0), q=(kt == 15):
                             e.matmul(o, lhsT=l, rhs=r, start=s, stop=q),
                             reads=[wb, hb[kt]], writes=[pb], inc=(kt == 15))
                    cbs(si, mc, c0 + cc, n, p, pb)
                run_tasks(tasks, per)
            run_tasks(tasks, 999)
        R.flush()
    R.barrier()


def linear_residual_stage(R, nc, C, Asrc, aname, Wd, X, xname):
    wv = Wd.rearrange("(kt p) m -> p kt m", p=128)
    with ExitStack() as es:
        afms = [es.enter_context(_sbt(nc, "afm%d" % i, [128, 16, 896], BF)) for i in range(2)]
        abs_ = [[Buf() for _ in range(16)] for _ in range(2)]
        wr = Ring(es, nc, "wl", 2, [128, 16, 128], BF)
        pr = Ring(es, nc, "pl", 3, [128, 512], F32, psum=True)
        xr = Ring(es, nc, "xr", 3, [128, 448], F32)
        xor_ = Ring(es, nc, "xo", 3, [128, 448], F32)

        def load(si):
            c0n, Wn = SUP[si]
            for kt in range(16):
                R.dma("sp", afms[si % 2][:, kt, 0:Wn], Asrc[kt * 128:(kt + 1) * 128, c0n:c0n + Wn],
                      reads=[xbuf(C, aname, kt, si)], writes=[abs_[si % 2][kt]])
        load(0)
        for si, (c0, W) in enumerate(SUP):
            afm, ab = afms[si % 2], abs_[si % 2]
            if si + 1 < len(SUP):
                load(si + 1)
            for mc in range(16):
                wt, wb = wr.next()
                R.dma("pool", wt[:], wv[:, :, mc * 128:(mc + 1) * 128], writes=[wb])
                for (cc, n) in cchunks(0, W):
                    p, pb = pr.next()
                    for kt in range(16):
                        R.op("pe", lambda e, o=p[:, 0:n], l=wt[:, kt, :], r=afm[:, kt, cc:cc + n], s=(kt == 0), q=(kt == 15):
                             e.matmul(o, lhsT=l, rhs=r, start=s, stop=q),
                             reads=[wb, ab[kt]], writes=[pb], inc=(kt == 15))
                    xt, xb = xr.next()
                    R.dma("sp", xt[:, 0:n], X[mc * 128:(mc + 1) * 128, c0 + cc:c0 + cc + n],
                          reads=[xbuf(C, xname, mc, si)], writes=[xb])
                    xo, xob = xor_.next()
                    R.op("dve", lambda e, o=xo[:, 0:n], a=p[:, 0:n], b=xt[:, 0:n]:
                         e.tensor_tensor(out=o, in0=a, in1=b, op=ALU.add),
                         reads=[pb, xb], writes=[xob])
                    R.dma("sp", X[mc * 128:(mc + 1) * 128, c0 + cc:c0 + cc + n], xo[:, 0:n],
                          reads=[xob], writes=[xbuf(C, xname, mc, si)])
        R.flush()
    R.barrier()


def final_stage2(R, nc, C, X, xname, Y):
    with ExitStack() as es:
        S = Ctx()
        alloc_norm(es, nc, S)
        yr = Ring(es, nc, "yr", 3, [128, 896], F32)
        xks = [es.enter_context(_sbt(nc, "xk%d" % i, [128, 16, 896], F32)) for i in range(2)]
        xkb = [[Buf() for _ in range(16)] for _ in range(2)]
        rstds = [es.enter_context(_sbt(nc, "rstdf%d" % i, [128, 896], F32)) for i in range(2)]
        rstdb = [Buf(), Buf()]

        def stats(si):
            c0, W = SUP[si]
            ch = cchunks(0, W)
            xk, kb = xks[si % 2], xkb[si % 2]
            for dk in range(16):
                R.dma("sp", xk[:, dk, 0:W], X[dk * 128:(dk + 1) * 128, c0:c0 + W],
                      reads=[xbuf(C, xname, dk, si)], writes=[kb[dk]])
                st, sb = S.sqr.next()
                R.op("act", lambda e, o=st[:, 0:W], i=xk[:, dk, 0:W]: e.activation(out=o, in_=i, func=AF.Square),
                     reads=[kb[dk]], writes=[sb])
                for ci, (cc, n) in enumerate(ch):
                    R.op("pe", lambda e, o=S.pss[ci][:, 0:n], r=st[:, cc:cc + n], s=(dk == 0), p=(dk == 15):
                         e.matmul(o, lhsT=C.ones32[:], rhs=r, start=s, stop=p),
                         reads=[sb], writes=[S.pssb[ci]], inc=True)
            for ci, (cc, n) in enumerate(ch):
                R.op("act", lambda e, o=S.srt[:, cc:cc + n], i=S.pss[ci][:, 0:n]:
                     e.activation(out=o, in_=i, func=AF.Sqrt, bias=C.epsb[:, 0:1], scale=1.0 / D),
                     reads=[S.pssb[ci]], writes=[S.srtb])
            R.op("dve", lambda e, o=rstds[si % 2]: e.reciprocal(out=o[:, 0:W], in_=S.srt[:, 0:W]),
                 reads=[S.srtb], writes=[rstdb[si % 2]])

        def apply(si):
            c0, W = SUP[si]
            xk, kb = xks[si % 2], xkb[si % 2]
            rs, rsb = rstds[si % 2], rstdb[si % 2]
            for dk in range(16):
                yt, yb = yr.next()
                R.op("dve", lambda e, o=yt[:, 0:W], i=xk[:, dk, 0:W], g=C.gall[:, 6, dk:dk + 1], rs=rs:
                     e.scalar_tensor_tensor(out=o, in0=i, scalar=g, in1=rs[:, 0:W], op0=ALU.mult, op1=ALU.mult),
                     reads=[kb[dk], rsb], writes=[yb])
                R.dma("sp", Y[dk * 128:(dk + 1) * 128, c0:c0 + W], yt[:, 0:W], reads=[yb],
                      writes=[xbuf(C, "Y", dk, si)])
        stats(0)
        for si in range(len(SUP)):
            if si + 1 < len(SUP):
                stats(si + 1)
            apply(si)
        R.flush()
    R.barrier()


def pool_stage(R, nc, C, U, YP, I):
    PADC = 16
    with ExitStack() as es:
        NCOL = PADC + NPR
        lev = [es.enter_context(_sbt(nc, "lev%d" % i, [128, NCOL], F32)) for i in range(3)]
        levb = [Buf() for _ in range(3)]
        slev = [es.enter_context(_sbt(nc, "slev%d" % i, [128, NSQ, 32], F32)) for i in range(3)]
        slevb = [Buf() for _ in range(3)]
        dfr = Ring(es, nc, "dfm", 2, [128, RP], BF)
        wp = es.enter_context(_sbt(nc, "wp", [128, 2, 256], BF))
        wpb = Buf()
        pr = Ring(es, nc, "pp", 2, [128, 512], F32, psum=True)
        orr = Ring(es, nc, "po", 2, [128, 512], BF)
        tfix = es.enter_context(_sbt(nc, "tfix", [128, 16], F32))
        tfb = Buf()
        for i in range(3):
            R.op("pool", lambda e, t=lev[i]: e.memset(t[:, 0:PADC], 0.0), writes=[levb[i]])
            R.op("pool", lambda e, t=slev[i]: e.memset(t[:], 0.0), writes=[slevb[i]])
        for g in range(4):
            w = 2 << g
            nst = g + 1
            R.dma("pool", wp[:], I["pool_w"][g].rearrange("(kt p) m -> p kt m", p=128), writes=[wpb])
            dts = []
            for kt2 in range(2):
                ct = 2 * g + kt2
                R.dma("sp", lev[0][:, PADC:PADC + NPR], U[ct * 128:(ct + 1) * 128, 0:NPR],
                      reads=[xbuf(C, "U", ct, 0)], writes=[levb[0]])
                R.dma("sp", slev[0][:, :, 1:16], I["pool_hist"][:, ct, :, :], writes=[slevb[0]])
                R.dma("sp", slev[0][:, :, 16:32],
                      U[ct * 128:(ct + 1) * 128, SC0:SC0 + NSQ * SQL].rearrange("p (s t) -> p s t", t=SQL),
                      reads=[xbuf(C, "U", ct, 0)], writes=[slevb[0]])
                src = 0
                srcb = levb[0]
                ssrc = 0
                for s in range(nst):
                    sh = 1 << s
                    dst = 1 if src != 1 else 2
                    R.op("dve", lambda e, o=lev[dst][:, PADC:NCOL], a=lev[src][:, PADC:NCOL], b=lev[src][:, PADC - sh:NCOL - sh]:
                         e.tensor_tensor(out=o, in0=a, in1=b, op=ALU.add),
                         reads=[levb[src]], writes=[levb[dst]])
                    R.op("pool", lambda e, o=slev[dst][:, :, sh:32], a=slev[ssrc][:, :, sh:32], b=slev[ssrc][:, :, 0:32 - sh]:
                         e.tensor_tensor(out=o, in0=a, in1=b, op=ALU.add),
                         reads=[slevb[ssrc]], writes=[slevb[dst]])
                    src = dst
                    ssrc = dst
                df, dfb = dfr.next()
                dts.append((df, dfb))
                R.op("dve", lambda e, o=df[:, 0:NPR], a=lev[src][:, PADC:NCOL], b=lev[0][:, PADC:NCOL]:
                     e.scalar_tensor_tensor(out=o, in0=a, scalar=1.0 / w, in1=b, op0=ALU.mult, op1=ALU.subtract),
                     reads=[levb[src], levb[0]], writes=[dfb])
                R.op("dve", lambda e, a=lev[src][:, PADC:PADC + 16], b=C.invcnt[:, g, :]:
                     e.tensor_tensor(out=tfix[:], in0=a, in1=b, op=ALU.mult),
                     reads=[levb[src]], writes=[tfb])
                R.op("dve", lambda e, o=df[:, 0:16], b=lev[0][:, PADC:PADC + 16]:
                     e.tensor_tensor(out=o, in0=tfix[:], in1=b, op=ALU.subtract),
                     reads=[tfb, levb[0]], writes=[dfb])
                R.op("dve", lambda e, o=df[:, SC0:SC0 + NSQ * SQL].rearrange("p (s t) -> p s t", t=SQL), a=slev[ssrc][:, :, 16:32], b=slev[0][:, :, 16:32]:
                     e.scalar_tensor_tensor(out=o, in0=a, scalar=1.0 / w, in1=b, op0=ALU.mult, op1=ALU.subtract),
                     reads=[slevb[ssrc], slevb[0]], writes=[dfb])
                R.op("pool", lambda e, o=df[:, SC0 + NSQ * SQL:RP]: e.memset(o, 0.0), writes=[dfb])
            for m in range(2):
                oc = 2 * g + m
                for cc in range(0, RP, 512):
                    n = min(512, RP - cc)
                    p, pb = pr.next()
                    for kt2 in range(2):
                        R.op("pe", lambda e, o=p[:, 0:n], l=wp[:, kt2, m * 128:(m + 1) * 128], r=dts[kt2][0][:, cc:cc + n], s=(kt2 == 0), q=(kt2 == 1):
                             e.matmul(o, lhsT=l, rhs=r, start=s, stop=q),
                             reads=[wpb, dts[kt2][1]], writes=[pb], inc=(kt2 == 1))
                    ot, ob = orr.next()
                    R.op("act", lambda e, o=ot[:, 0:n], i=p[:, 0:n], sc=C.pscale[:, oc:oc + 1]:
                         e.activation(out=o, in_=i, func=AF.Copy, scale=sc),
                         reads=[pb], writes=[ob])
                    R.dma("sp", YP[oc * 128:(oc + 1) * 128, cc:cc + n], ot[:, 0:n], reads=[ob],
                          writes=[xbuf(C, "YP", oc, 0)])
        R.flush()
    R.barrier()


def ssm_stage(R, nc, C, U, Z, I, O):
    L = LCH
    HT = 16
    with ExitStack() as es:
        def sb(name, shape, dt=F32):
            return es.enter_context(_sbt(nc, name, shape, dt))
        are = sb("are", [128, 32]); aim = sb("aim", [128, 32]); ldt = sb("ldt", [128, 32])
        t0 = sb("t0", [128, 32]); t1 = sb("t1", [128, 32]); t2 = sb("t2", [128, 32]); t3 = sb("t3", [128, 32])
        cs = sb("cs", [128, 32]); sn = sb("sn", [128, 32]); mag = sb("mag", [128, 32])
        cre = sb("cre", [128, 32]); cim = sb("cim", [128, 32])
        Ec = sb("Ec", [128, 32, L]); Es = sb("Es", [128, 32, L])
        nEs = sb("nEs", [128, 32, L]); nEc = sb("nEc", [128, 32, L])
        TA = sb("TA", [128, 32, L]); TB = sb("TB", [128, 32, L])
        Rb = sb("Rb", [128, 32, L])
        ELc = sb("ELc", [128, 32]); ELs = sb("ELs", [128, 32])
        one = Buf()
        Bre = sb("Bre", [128, 32, 128], BF); Bim = sb("Bim", [128, 32, 128], BF)
        Cre = sb("Cre", [128, 32, 128], BF); Cim = sb("Cim", [128, 32, 128], BF)
        wb_ = Buf()
        R.dma("sp", are[:], I["ssm_are"], writes=[one])
        R.dma("sp", aim[:], I["ssm_aim"], writes=[one])
        R.dma("sp", ldt[:], I["ssm_ldt"], writes=[one])
        R.dma("pool", Bre[:], I["ssm_Bre"], writes=[wb_])
        R.dma("pool", Bim[:], I["ssm_Bim"], writes=[wb_])
        R.dma("pool", Cre[:], I["ssm_Cre"], writes=[wb_])
        R.dma("pool", Cim[:], I["ssm_Cim"], writes=[wb_])

        def A(fn):
            R.op("act", fn, reads=[one], writes=[one])

        def V(fn):
            R.op("dve", fn, reads=[one], writes=[one])
        A(lambda e: e.activation(out=t0[:], in_=ldt[:], func=AF.Exp))
        V(lambda e: e.tensor_tensor(out=t1[:], in0=are[:], in1=t0[:], op=ALU.mult))
        V(lambda e: e.tensor_tensor(out=t2[:], in0=aim[:], in1=t0[:], op=ALU.mult))
        A(lambda e: e.activation(out=mag[:], in_=t1[:], func=AF.Exp))
        A(lambda e: e.activation(out=sn[:], in_=t2[:], func=AF.Sin, scale=1.0 / 16))
        A(lambda e: e.activation(out=cs[:], in_=t2[:], func=AF.Sin, scale=-1.0 / 16, bias=C.halfpi[:, 0:1]))
        for _ in range(4):
            V(lambda e: e.tensor_tensor(out=t0[:], in0=cs[:], in1=cs[:], op=ALU.mult))
            V(lambda e: e.tensor_tensor(out=t1[:], in0=sn[:], in1=sn[:], op=ALU.mult))
            V(lambda e: e.tensor_tensor(out=t3[:], in0=cs[:], in1=sn[:], op=ALU.mult))
            V(lambda e: e.tensor_tensor(out=cs[:], in0=t0[:], in1=t1[:], op=ALU.subtract))
            V(lambda e: e.tensor_scalar(out=sn[:], in0=t3[:], scalar1=2.0, scalar2=None, op0=ALU.mult))
        V(lambda e: e.tensor_tensor(out=t0[:], in0=mag[:], in1=cs[:], op=ALU.mult))
        V(lambda e: e.tensor_tensor(out=t1[:], in0=mag[:], in1=sn[:], op=ALU.mult))
        V(lambda e: e.tensor_scalar(out=t0[:], in0=t0[:], scalar1=-1.0, scalar2=None, op0=ALU.add))
        V(lambda e: e.tensor_tensor(out=t2[:], in0=are[:], in1=are[:], op=ALU.mult))
        V(lambda e: e.tensor_tensor(out=t3[:], in0=aim[:], in1=aim[:], op=ALU.mult))
        V(lambda e: e.tensor_tensor(out=t2[:], in0=t2[:], in1=t3[:], op=ALU.add))
        V(lambda e: e.reciprocal(out=t2[:], in_=t2[:]))
        V(lambda e: e.tensor_tensor(out=cre[:], in0=t0[:], in1=are[:], op=ALU.mult))
        V(lambda e: e.tensor_tensor(out=t3[:], in0=t1[:], in1=aim[:], op=ALU.mult))
        V(lambda e: e.tensor_tensor(out=cre[:], in0=cre[:], in1=t3[:], op=ALU.add))
        V(lambda e: e.tensor_tensor(out=cre[:], in0=cre[:], in1=t2[:], op=ALU.mult))
        V(lambda e: e.tensor_tensor(out=cim[:], in0=t1[:], in1=are[:], op=ALU.mult))
        V(lambda e: e.tensor_tensor(out=t3[:], in0=t0[:], in1=aim[:], op=ALU.mult))
        V(lambda e: e.tensor_tensor(out=cim[:], in0=cim[:], in1=t3[:], op=ALU.subtract))
        V(lambda e: e.tensor_tensor(out=cim[:], in0=cim[:], in1=t2[:], op=ALU.mult))
        V(lambda e: e.tensor_copy(out=Ec[:, :, 0:1], in_=cs[:].unsqueeze(2)))
        V(lambda e: e.tensor_copy(out=Es[:, :, 0:1], in_=sn[:].unsqueeze(2)))
        m = 1
        while m < L:
            bc = Ec[:, :, m - 1:m].broadcast_to([128, 32, m])
            bs = Es[:, :, m - 1:m].broadcast_to([128, 32, m])
            V(lambda e, m=m, bc=bc: e.tensor_tensor(out=TA[:, :, 0:m], in0=Ec[:, :, 0:m], in1=bc, op=ALU.mult))
            V(lambda e, m=m, bs=bs: e.tensor_tensor(out=TB[:, :, 0:m], in0=Es[:, :, 0:m], in1=bs, op=ALU.mult))
            V(lambda e, m=m: e.tensor_tensor(out=Ec[:, :, m:2 * m], in0=TA[:, :, 0:m], in1=TB[:, :, 0:m], op=ALU.subtract))
            V(lambda e, m=m, bs=bs: e.tensor_tensor(out=TA[:, :, 0:m], in0=Ec[:, :, 0:m], in1=bs, op=ALU.mult))
            V(lambda e, m=m, bc=bc: e.tensor_tensor(out=TB[:, :, 0:m], in0=Es[:, :, 0:m], in1=bc, op=ALU.mult))
            V(lambda e, m=m: e.tensor_tensor(out=Es[:, :, m:2 * m], in0=TA[:, :, 0:m], in1=TB[:, :, 0:m], op=ALU.add))
            m *= 2
        V(lambda e: e.tensor_scalar(out=nEs[:], in0=Es[:], scalar1=-1.0, scalar2=None, op0=ALU.mult))
        V(lambda e: e.tensor_scalar(out=nEc[:], in0=Ec[:], scalar1=-1.0, scalar2=None, op0=ALU.mult))
        crb = cre[:].unsqueeze(2).broadcast_to([128, 32, L])
        cib = cim[:].unsqueeze(2).broadcast_to([128, 32, L])
        V(lambda e: e.tensor_tensor(out=TA[:], in0=Ec[:], in1=crb, op=ALU.mult))
        V(lambda e: e.tensor_tensor(out=Rb[:], in0=Es[:], in1=cib, op=ALU.mult))
        V(lambda e: e.tensor_tensor(out=TA[:], in0=TA[:], in1=Rb[:], op=ALU.add))
        V(lambda e: e.tensor_tensor(out=TB[:], in0=Ec[:], in1=cib, op=ALU.mult))
        V(lambda e: e.tensor_tensor(out=Rb[:], in0=Es[:], in1=crb, op=ALU.mult))
        V(lambda e: e.tensor_tensor(out=TB[:], in0=TB[:], in1=Rb[:], op=ALU.subtract))
        V(lambda e: e.tensor_copy(out=Rb[:], in_=mag[:].unsqueeze(2).broadcast_to([128, 32, L])))
        tabs = one

        ur = Ring(es, nc, "ubf", 3, [128, 8, L], BF)
        u32 = Ring(es, nc, "u32", 3, [128, 8, L], F32)
        Pre = es.enter_context(_pst(nc, "Pre", [128, HT, L], F32)); Preb = Buf()
        Pim = es.enter_context(_pst(nc, "Pim", [128, HT, L], F32)); Pimb = Buf()
        Yp = Ring(es, nc, "Yp", 2, [128, 8, L], F32, psum=True)
        m1 = sb("m1", [128, HT, L]); m2 = sb("m2", [128, HT, L])
        cr_ = sb("cr_", [128, HT, L]); ci_ = sb("ci_", [128, HT, L])
        kr = sb("kr", [128, 32, L]); ki = sb("ki", [128, 32, L])
        m1b, m2b, crb_, cib_ = Buf(), Buf(), Buf(), Buf()
        krb = [Buf(), Buf()]; kib = [Buf(), Buf()]
        qr = [[Ring(es, nc, "q%d%d" % (k, hf), 2, [128, HT, L], BF) for hf in range(2)] for k in range(4)]
        hre = sb("hre", [128, 32]); him = sb("him", [128, 32]); hb_ = Buf()
        hta = sb("hta", [128, 32]); htb = sb("htb", [128, 32]); hsc = Buf()
        ysr = Ring(es, nc, "ys", 2, [128, 8, L], F32)
        g1 = Ring(es, nc, "g1", 2, [128, 8, L], F32)
        g2 = Ring(es, nc, "g2", 2, [128, 8, L], F32)
        zr = Ring(es, nc, "zr", 2, [128, 8, L], BF)

        zpad = sb("zpad", [128, 8, RP - SC0 - NSQ * SQL], BF)
        zpb = Buf()
        R.op("pool", lambda e: e.memset(zpad[:], 0.0), writes=[zpb])
        R.dma("sp", Z[:, SC0 + NSQ * SQL:RP].rearrange("(ct p) c -> p ct c", p=128), zpad[:], reads=[zpb], writes=[xbuf(C, "Zpad", 0, 0)])
        seqs = [(0, NPR, None)] + [(SC0 + s * SQL, SQL, s) for s in range(NSQ)]
        chunks = []
        for (q0, qlen, sidx) in seqs:
            t0s = list(range(0, qlen, L))
            for k, t0_ in enumerate(t0s):
                chunks.append(dict(col=q0 + t0_, n=min(L, qlen - t0_), sidx=sidx, first=(k == 0), last=(k == len(t0s) - 1)))

        def head(c):
            n = c["n"]; col = c["col"]; sidx = c["sidx"]
            if c["first"]:
                if sidx is None:
                    R.op("dve", lambda e: e.memset(hre[:], 0.0), writes=[hb_])
                    R.op("dve", lambda e: e.memset(him[:], 0.0), writes=[hb_])
                else:
                    R.dma("sp", hre[:], I["ssm_h0re"][sidx], writes=[hb_])
                    R.dma("sp", him[:], I["ssm_h0im"][sidx], writes=[hb_])
            ub, ubb = ur.next()
            uf, ufb = u32.next()
            R.dma("pool", ub[:, :, 0:n], U[1024:2048, col:col + n].rearrange("(ct p) c -> p ct c", p=128),
                  reads=[xbuf(C, "U", 8, 0)], writes=[ubb])
            R.dma("sp", uf[:, :, 0:n], U[1024:2048, col:col + n].rearrange("(ct p) c -> p ct c", p=128),
                  reads=[xbuf(C, "U", 8, 0)], writes=[ufb])
            prods = [[None, None] for _ in range(4)]
            for hf in range(2):
                for j in range(HT):
                    st = hf * HT + j
                    R.op("pe", lambda e, o=Pre[:, j, 0:n], l=Bre[:, st, :], r=ub[:, st // 4, 0:n]:
                         e.matmul(o, lhsT=l, rhs=r, start=True, stop=True),
                         reads=[wb_, ubb], writes=[Preb], inc=False)
                    R.op("pe", lambda e, o=Pim[:, j, 0:n], l=Bim[:, st, :], r=ub[:, st // 4, 0:n]:
                         e.matmul(o, lhsT=l, rhs=r, start=True, stop=True),
                         reads=[wb_, ubb], writes=[Pimb], inc=(j == HT - 1))
                sl = slice(hf * HT, (hf + 1) * HT)
                R.op("dve", lambda e, sl=sl: e.tensor_tensor(out=m1[:, :, 0:n], in0=Pre[:, :, 0:n], in1=TA[:, sl, 0:n], op=ALU.mult),
                     reads=[Preb, tabs], writes=[m1b])
                R.op("dve", lambda e, sl=sl: e.tensor_tensor(out=m2[:, :, 0:n], in0=Pim[:, :, 0:n], in1=TB[:, sl, 0:n], op=ALU.mult),
                     reads=[Pimb, tabs], writes=[m2b])
                R.op("dve", lambda e: e.tensor_tensor(out=cr_[:, :, 0:n], in0=m1[:, :, 0:n], in1=m2[:, :, 0:n], op=ALU.subtract),
                     reads=[m1b, m2b], writes=[crb_])
                R.op("dve", lambda e, sl=sl: e.tensor_tensor(out=m1[:, :, 0:n], in0=Pre[:, :, 0:n], in1=TB[:, sl, 0:n], op=ALU.mult),
                     reads=[Preb, tabs], writes=[m1b])
                R.op("dve", lambda e, sl=sl: e.tensor_tensor(out=m2[:, :, 0:n], in0=Pim[:, :, 0:n], in1=TA[:, sl, 0:n], op=ALU.mult),
                     reads=[Pimb, tabs], writes=[m2b])
                R.op("dve", lambda e: e.tensor_tensor(out=ci_[:, :, 0:n], in0=m1[:, :, 0:n], in1=m2[:, :, 0:n], op=ALU.add),
                     reads=[m1b, m2b], writes=[cib_])
                for j in range(HT):
                    st = hf * HT + j
                    R.op("dve", lambda e, st=st, j=j: e.tensor_tensor_scan(out=kr[:, st, 0:n], data0=Rb[:, st, 0:n], data1=cr_[:, j, 0:n],
                                                                         initial=hre[:, st:st + 1], op0=ALU.mult, op1=ALU.add),
                         reads=[crb_, hb_, tabs], writes=[krb[hf]])
                    R.op("dve", lambda e, st=st, j=j: e.tensor_tensor_scan(out=ki[:, st, 0:n], data0=Rb[:, st, 0:n], data1=ci_[:, j, 0:n],
                                                                         initial=him[:, st:st + 1], op0=ALU.mult, op1=ALU.add),
                         reads=[cib_, hb_, tabs], writes=[kib[hf]])
                for k, (srcT, srcB, tab) in enumerate(((kr, krb, Ec), (ki, kib, nEs), (kr, krb, nEs), (ki, kib, nEc))):
                    pt_, pb_ = qr[k][hf].next()
                    R.op("pool", lambda e, o=pt_, s_=srcT, t_=tab, sl=sl: e.tensor_tensor(out=o[:, :, 0:n], in0=s_[:, sl, 0:n], in1=t_[:, sl, 0:n], op=ALU.mult),
                         reads=[srcB[hf], tabs], writes=[pb_])
                    prods[k][hf] = (pt_, pb_)
            R.op("dve", lambda e: e.tensor_tensor(out=hta[:].unsqueeze(2), in0=kr[:, :, n - 1:n], in1=Ec[:, :, n - 1:n], op=ALU.mult),
                 reads=[krb[0], krb[1], tabs], writes=[hsc])
            R.op("dve", lambda e: e.tensor_tensor(out=htb[:].unsqueeze(2), in0=ki[:, :, n - 1:n], in1=Es[:, :, n - 1:n], op=ALU.mult),
                 reads=[kib[0], kib[1], tabs], writes=[hsc])
            R.op("dve", lambda e: e.tensor_tensor(out=hre[:], in0=hta[:], in1=htb[:], op=ALU.subtract),
                 reads=[hsc], writes=[hb_])
            R.op("dve", lambda e: e.tensor_tensor(out=hta[:].unsqueeze(2), in0=kr[:, :, n - 1:n], in1=Es[:, :, n - 1:n], op=ALU.mult),
                 reads=[krb[0], krb[1], tabs], writes=[hsc])
            R.op("dve", lambda e: e.tensor_tensor(out=htb[:].unsqueeze(2), in0=ki[:, :, n - 1:n], in1=Ec[:, :, n - 1:n], op=ALU.mult),
                 reads=[kib[0], kib[1], tabs], writes=[hsc])
            R.op("dve", lambda e: e.tensor_tensor(out=him[:], in0=hta[:], in1=htb[:], op=ALU.add),
                 reads=[hsc], writes=[hb_])
            if c["last"]:
                if sidx is None:
                    R.dma("sp", O["hre_p"], hre[:], reads=[hb_], writes=[xbuf(C, "hst", 0, 0)])
                    R.dma("sp", O["him_p"], him[:], reads=[hb_], writes=[xbuf(C, "hst", 1, 0)])
                else:
                    R.dma("sp", O["hre_s"][sidx], hre[:], reads=[hb_], writes=[xbuf(C, "hst", 2, sidx)])
                    R.dma("sp", O["him_s"][sidx], him[:], reads=[hb_], writes=[xbuf(C, "hst", 3, sidx)])
            c["uf"] = (uf, ufb)
            c["prods"] = prods

        def tail(c):
            n = c["n"]; col = c["col"]
            uf, ufb = c["uf"]
            prods = c["prods"]
            yp, ypb = Yp.next()
            first = True
            for ct in range(8):
                for jj in range(4):
                    st = ct * 4 + jj
                    hf = st // HT
                    j = st % HT
                    for k, Wt in ((0, Cre), (1, Cre), (2, Cim), (3, Cim)):
                        pp, ppb = prods[k][hf]
                        last = (ct == 7 and jj == 3 and k == 3)
                        R.op("pe", lambda e, o=yp[:, ct, 0:n], l=Wt[:, st, :], r=pp[:, j, 0:n], s=first:
                             e.matmul(o, lhsT=l, rhs=r, start=s, stop=False, skip_group_check=True),
                             reads=[wb_, ppb], writes=[ypb], inc=last)
                        first = False
            ys, ysb = ysr.next()
            for ct in range(8):
                R.op("dve", lambda e, ct=ct, ys=ys, uf=uf, yp=yp: e.scalar_tensor_tensor(out=ys[:, ct, 0:n], in0=uf[:, ct, 0:n], scalar=C.ssmd[:, ct:ct + 1],
                                                                    in1=yp[:, ct, 0:n], op0=ALU.mult, op1=ALU.add),
                     reads=[ufb, ypb], writes=[ysb])
            a1, a1b = g1.next(); a2, a2b = g2.next()
            R.op("act", lambda e, a1=a1, ys=ys: e.activation(out=a1[:, :, 0:n], in_=ys[:, :, 0:n], func=AF.Square), reads=[ysb], writes=[a1b])
            R.op("dve", lambda e, a1=a1, a2=a2: e.tensor_scalar(out=a2[:, :, 0:n], in0=a1[:, :, 0:n], scalar1=0.044715, scalar2=1.0, op0=ALU.mult, op1=ALU.add),
                 reads=[a1b], writes=[a2b])
            R.op("dve", lambda e, a1=a1, a2=a2, ys=ys: e.tensor_tensor(out=a1[:, :, 0:n], in0=a2[:, :, 0:n], in1=ys[:, :, 0:n], op=ALU.mult),
                 reads=[a2b, ysb], writes=[a1b])
            R.op("act", lambda e, a1=a1, a2=a2: e.activation(out=a2[:, :, 0:n], in_=a1[:, :, 0:n], func=AF.Sigmoid, scale=2.0 * 0.7978845608028654),
                 reads=[a1b], writes=[a2b])
            zt, ztb = zr.next()
            R.op("dve", lambda e, zt=zt, a2=a2, ys=ys: e.tensor_tensor(out=zt[:, :, 0:n], in0=a2[:, :, 0:n], in1=ys[:, :, 0:n], op=ALU.mult),
                 reads=[a2b, ysb], writes=[ztb])
            R.dma("sp", Z[:, col:col + n].rearrange("(ct p) c -> p ct c", p=128), zt[:, :, 0:n], reads=[ztb],
                  writes=[xbuf(C, "Z", 0, 0)])

        pending = None
        for c in chunks:
            head(c)
            if pending is not None:
                tail(pending)
            pending = c
        tail(pending)
        R.flush()
    R.barrier()


def glu_stage(R, nc, C, Z, YP, I):
    wv = I["ssm_w_glu"].rearrange("(kt p) m -> p kt m", p=128)
    with ExitStack() as es:
        zf = es.enter_context(_sbt(nc, "zf", [128, 8, RP], BF))
        zb = Buf()
        wr = Ring(es, nc, "wg", 2, [128, 8, 128], BF)
        pr = Ring(es, nc, "pgl", 3, [128, 512], F32, psum=True)
        sr = Ring(es, nc, "sgl", 2, [128, 512], F32)
        orr = Ring(es, nc, "ogl", 2, [128, 512], BF)
        for ct in range(8):
            R.dma("sp", zf[:, ct, :], Z[ct * 128:(ct + 1) * 128, :], reads=[xbuf(C, "Z", 0, 0)], writes=[zb])
        for mc in range(8):
            wt, wb = wr.next()
            R.dma("pool", wt[:], wv[:, :, mc * 128:(mc + 1) * 128], writes=[wb])
            for cc in range(0, RP, 512):
                n = min(512, RP - cc)
                p, pb = pr.next()
                for kt in range(8):
                    R.op("pe", lambda e, o=p[:, 0:n], l=wt[:, kt, :], r=zf[:, kt, cc:cc + n], s=(kt == 0), q=(kt == 7):
                         e.matmul(o, lhsT=l, rhs=r, start=s, stop=q), reads=[wb, zb], writes=[pb], inc=(kt == 7))
                st, sb_ = sr.next()
                R.op("act", lambda e, o=st[:, 0:n], i=p[:, 0:n], b=C.bglu[:, mc:mc + 1]:
                     e.activation(out=o, in_=i, func=AF.Sigmoid, bias=b), reads=[pb], writes=[sb_])
                ot, ob = orr.next()
                R.op("dve", lambda e, o=ot[:, 0:n], a=st[:, 0:n], b=zf[:, mc, cc:cc + n]:
                     e.tensor_tensor(out=o, in0=a, in1=b, op=ALU.mult), reads=[sb_, zb], writes=[ob])
                R.dma("sp", YP[(8 + mc) * 128:(9 + mc) * 128, cc:cc + n], ot[:, 0:n], reads=[ob],
                      writes=[xbuf(C, "YP", 8 + mc, 0)])
        R.flush()
    R.barrier()


def attn_core(R, nc, C, A, groups, kblocks, ncols, diag):
    blks = []
    for bi in range(len(kblocks) - 1, -1, -1):
        k0, nk = kblocks[bi]
        cstart, mask = diag(k0, nk)
        if cstart < ncols:
            blks.append((k0, nk, cstart, mask))
    n = len(blks)
    st = [dict() for _ in range(n)]
    Rs = []
    for _ in range(3):
        rt, rb = A.rr.next()
        R.op("pool", lambda e, o=rt: e.memset(o[:, 0:ncols], 0.0), writes=[rb])
        Rs.append((rt, rb))
    firstpv = [True]

    def P1(i):
        k0, nk, cs, mask = blks[i]
        ps, psb = A.psr.next()
        for (gc0, gn, qT, kTf, vf, gdeps) in groups:
            lo = max(gc0, cs)
            if lo >= gc0 + gn:
                continue
            R.op("pe", lambda e, o=ps[0:nk, lo:gc0 + gn], l=kTf(k0, nk), r=qT[:, lo - gc0:gn]:
                 e.matmul(o, lhsT=l, rhs=r, start=True, stop=True), reads=gdeps, writes=[psb], inc=True)
        st[i]["ps"] = (ps, psb)

    def A1(i):
        k0, nk, cs, mask = blks[i]
        ps, psb = st[i]["ps"]
        et, eb = A.er.next()
        R.op("act", lambda e, o=et[0:nk, cs:ncols], i_=ps[0:nk, cs:ncols]:
             e.activation(out=o, in_=i_, func=AF.Exp, scale=A.scale), reads=[psb], writes=[eb])
        spt, spb = A.spr.next()
        R.op("act", lambda e, o=spt[0:nk, cs:ncols], i_=et[0:nk, cs:ncols], b=C.oneb[0:nk, 0:1]:
             e.activation(out=o, in_=i_, func=AF.Ln, bias=b), reads=[eb], writes=[spb])
        if mask is not None:
            for (mc0, mn, map_) in mask:
                R.op("dve", lambda e, o=spt[0:nk, mc0:mc0 + mn], m_=map_:
                     e.tensor_tensor(out=o, in0=o, in1=m_, op=ALU.mult), reads=[spb], writes=[spb])
        st[i]["e"] = (et, eb)
        st[i]["sp"] = (spt, spb)

    def D1(i):
        if i == n - 1:
            return
        k0, nk, cs, mask = blks[i]
        spt, spb = st[i]["sp"]
        rp, rpb = Rs[i % 3]
        rn, rnb = Rs[(i + 1) % 3]
        R.op("dve", lambda e, o=rn[0:nk, cs:ncols], a=spt[0:nk, cs:ncols], b=rp[0:nk, cs:ncols]:
             e.tensor_tensor(out=o, in0=a, in1=b, op=ALU.add), reads=[spb, rpb], writes=[rnb])

    def P2(i):
        k0, nk, cs, mask = blks[i]
        spt, spb = st[i]["sp"]
        pc, pcb = A.pcr.next()
        R.op("pe", lambda e, o=pc[0:nk, cs:ncols], l=C.LTb[0:nk, 0:nk], r=spt[0:nk, cs:ncols], s_=(i == 0):
             e.matmul(o, lhsT=l, rhs=r, start=True, stop=s_), reads=[spb], writes=[pcb], inc=(i == 0))
        if i > 0:
            rp, rpb = Rs[i % 3]
            R.op("pe", lambda e, o=pc[0:nk, cs:ncols], l=C.onesb[0:128, 0:nk], r=rp[0:128, cs:ncols]:
                 e.matmul(o, lhsT=l, rhs=r, start=False, stop=True, skip_group_check=True), reads=[rpb], writes=[pcb], inc=True)
        st[i]["pc"] = (pc, pcb)

    def A2(i):
        k0, nk, cs, mask = blks[i]
        pc, pcb = st[i]["pc"]
        rt2, r2b = A.xr.next()
        R.op("act", lambda e, o=rt2[0:nk, cs:ncols], i_=pc[0:nk, cs:ncols]:
             e.activation(out=o, in_=i_, func=AF.Exp, scale=-1.0), reads=[pcb], writes=[r2b])
        st[i]["r"] = (rt2, r2b)

    def D2(i):
        k0, nk, cs, mask = blks[i]
        et, eb = st[i]["e"]
        rt2, r2b = st[i]["r"]
        wt, wb = A.wr.next()
        R.op("dve", lambda e, o=wt[0:nk, cs:ncols], a=et[0:nk, cs:ncols], b=rt2[0:nk, cs:ncols]:
             e.tensor_tensor(out=o, in0=a, in1=b, op=ALU.mult), reads=[eb, r2b], writes=[wb])
        if mask is not None:
            for (mc0, mn, map_) in mask:
                R.op("dve", lambda e, o=wt[0:nk, mc0:mc0 + mn], m_=map_:
                     e.tensor_tensor(out=o, in0=o, in1=m_, op=ALU.mult), reads=[wb], writes=[wb])
        st[i]["w"] = (wt, wb)

    def P3(i):
        k0, nk, cs, mask = blks[i]
        wt, wb = st[i]["w"]
        for (gc0, gn, qT, kTf, vf, gdeps) in groups:
            lo = max(gc0, cs)
            if lo >= gc0 + gn:
                continue
            R.op("pe", lambda e, o=A.po[:, lo:gc0 + gn], l=vf(k0, nk), r=wt[0:nk, lo:gc0 + gn], s_=firstpv[0]:
                 e.matmul(o, lhsT=l, rhs=r, start=s_, stop=False, skip_group_check=True),
                 reads=[wb] + list(gdeps), writes=[A.pob], inc=True)
            firstpv[0] = False

    for step in range(n + 2):
        if step < n:
            P1(step)
        if 0 <= step - 1 < n:
            P2(step - 1)
        if 0 <= step - 2 < n:
            P3(step - 2)
        if step < n:
            A1(step)
            D1(step)
        if 0 <= step - 1 < n:
            A2(step - 1)
            D2(step - 1)


def alloc_attn(es, nc, A):
    A.psr = Ring(es, nc, "aps", 2, [128, 512], F32, psum=True)
    A.pcr = Ring(es, nc, "apc", 2, [128, 512], F32, psum=True)
    A.po = es.enter_context(_pst(nc, "apo", [128, 512], F32))
    A.pob = Buf()
    A.er = Ring(es, nc, "ae", 4, [128, 512], F32)
    A.spr = Ring(es, nc, "asp", 4, [128, 512], BF)
    A.rr = Ring(es, nc, "arr", 3, [128, 512], BF)
    A.xr = Ring(es, nc, "axr", 3, [128, 512], F32)
    A.wr = Ring(es, nc, "awr", 4, [128, 512], BF)
    A.scale = 1.0 / math.sqrt(128.0)


def attn_prompt_stage(R, nc, C, Q, K, V, OUT):
    with ExitStack() as es:
        A = Ctx()
        alloc_attn(es, nc, A)
        qr = Ring(es, nc, "aq", 2, [128, NPR], BF)
        kr = Ring(es, nc, "ak", 2, [128, NPR], BF)
        vfr = Ring(es, nc, "avf", 2, [128, NPR], BF)
        vtr = Ring(es, nc, "avt", 2, [128, 33, 128], BF)
        ptr = Ring(es, nc, "apt", 2, [128, 512], BF, psum=True)
        oo = Ring(es, nc, "aoo", 2, [128, 512], BF)
        kblocks = [(k0, min(128, NPR - k0)) for k0 in range(0, NPR, 128)]
        for h in range(NHEAD):
            qt, qb = qr.next(); kt, kb = kr.next(); vf, vfb = vfr.next(); vt, vtb = vtr.next()
            R.dma("sp", qt[:], Q[h * 128:(h + 1) * 128, 0:NPR], reads=[xbuf(C, "Q", h, 0)], writes=[qb])
            R.dma("sp", kt[:], K[h * 128:(h + 1) * 128, 0:NPR], reads=[xbuf(C, "K", h, 0)], writes=[kb])
            R.dma("sp", vf[:], V[h * 128:(h + 1) * 128, 0:NPR], reads=[xbuf(C, "V", h, 0)], writes=[vfb])
            for b0 in range(0, len(kblocks), 4):
                pt, ptb = ptr.next()
                blks = kblocks[b0:b0 + 4]
                for j, (k0, nk) in enumerate(blks):
                    R.op("pe", lambda e, o=pt[0:nk, j * 128:(j + 1) * 128], i=vf[:, k0:k0 + nk]:
                         e.transpose(o, i, C.identb[:]), reads=[vfb], writes=[ptb], inc=(j == len(blks) - 1))
                for j, (k0, nk) in enumerate(blks):
                    R.op("dve", lambda e, o=vt[0:nk, b0 + j, :], i=pt[0:nk, j * 128:(j + 1) * 128]:
                         e.tensor_copy(out=o, in_=i), reads=[ptb], writes=[vtb])
            for q0 in range(0, NPR, 512):
                nq = min(512, NPR - q0)
                kb_list = [kbk for kbk in kblocks if kbk[0] < q0 + nq]

                def diag(k0, nk, q0=q0, nq=nq):
                    if k0 + nk <= q0:
                        return 0, None
                    cs = k0 - q0
                    mn = min(nk, nq - cs)
                    return cs, [(cs, mn, C.trib[0:nk, 0:mn])]
                groups = [(0, nq, qt[:, q0:q0 + nq],
                           (lambda k0, nk, kt=kt: kt[:, k0:k0 + nk]),
                           (lambda k0, nk, vt=vt: vt[0:nk, k0 // 128, :]),
                           [qb, kb, vtb])]
                attn_core(R, nc, C, A, groups, kb_list, nq, diag)
                ot, ob = oo.next()
                R.op("act", lambda e, o=ot[:, 0:nq], i=A.po[:, 0:nq]: e.activation(out=o, in_=i, func=AF.Copy),
                     reads=[A.pob], writes=[ob])
                R.dma("sp", OUT[h * 128:(h + 1) * 128, q0:q0 + nq], ot[:, 0:nq], reads=[ob],
                      writes=[xbuf(C, "YP", h, 0)])
        R.flush()
    R.barrier()


def attn_sample_stage(R, nc, C, Q, K, V, OUT, I):
    HG = 4
    NK = PAST + SQL
    with ExitStack() as es:
        A = Ctx()
        alloc_attn(es, nc, A)
        kr = Ring(es, nc, "sk", 2, [128, HG, NK], BF)
        vr = Ring(es, nc, "sv", 2, [128, 32, HG * 128], BF)
        qr = Ring(es, nc, "sq", 2, [128, HG, SQL], BF)
        vnf = Ring(es, nc, "svn", 2, [128, HG, SQL], BF)
        vnt = Ring(es, nc, "svt", 2, [SQL, HG, 128], BF)
        ptr = Ring(es, nc, "spt", 2, [128, 512], BF, psum=True)
        oo = Ring(es, nc, "soo", 2, [128, HG * SQL], BF)
        kblocks = [(k0, 128) for k0 in range(0, PAST, 128)] + [(PAST, SQL)]
        for s in range(NSQ):
            col = SC0 + s * SQL
            for hg in range(16 // HG):
                kt, kb = kr.next(); vt, vb = vr.next(); qt, qb = qr.next(); vn, vnb = vnf.next(); vtt, vttb = vnt.next()
                h0 = hg * HG
                R.dma("pool", kt[:, :, 0:PAST], I["cache_kT"][s, h0:h0 + HG].rearrange("h d k -> d h k"), writes=[kb])
                R.dma("sp", kt[:, :, PAST:NK], K[h0 * 128:(h0 + HG) * 128, col:col + SQL].rearrange("(h d) c -> d h c", d=128),
                      reads=[xbuf(C, "K", 0, 0)], writes=[kb])
                R.dma("pool", vt[:], I["cache_v"][s][:, h0 * 128:(h0 + HG) * 128].rearrange("(b p) c -> p b c", p=128), writes=[vb])
                R.dma("sp", qt[:], Q[h0 * 128:(h0 + HG) * 128, col:col + SQL].rearrange("(h d) c -> d h c", d=128),
                      reads=[xbuf(C, "Q", 0, 0)], writes=[qb])
                R.dma("sp", vn[:], V[h0 * 128:(h0 + HG) * 128, col:col + SQL].rearrange("(h d) c -> d h c", d=128),
                      reads=[xbuf(C, "V", 0, 0)], writes=[vnb])
                pt, ptb = ptr.next()
                for j in range(HG):
                    R.op("pe", lambda e, o=pt[0:SQL, j * 128:(j + 1) * 128], i=vn[:, j, :]:
                         e.transpose(o, i, C.identb[:]), reads=[vnb], writes=[ptb], inc=(j == HG - 1))
                R.op("dve", lambda e, o=vtt[:], i=pt[0:SQL, 0:HG * 128].rearrange("p (h d) -> p h d", d=128):
                     e.tensor_copy(out=o, in_=i), reads=[ptb], writes=[vttb])

                def diag(k0, nk):
                    if k0 < PAST:
                        return 0, None
                    return 0, [(j * SQL, SQL, C.trib[0:SQL, 0:SQL]) for j in range(HG)]
                groups = []
                for j in range(HG):
                    groups.append((j * SQL, SQL, qt[:, j, :],
                                   (lambda k0, nk, kt=kt, j=j: kt[:, j, k0:k0 + nk]),
                                   (lambda k0, nk, vt=vt, vtt=vtt, j=j: (vt[:, k0 // 128, j * 128:(j + 1) * 128] if k0 < PAST else vtt[:, j, :])),
                                   [qb, kb, vb, vttb]))
                attn_core(R, nc, C, A, groups, kblocks, HG * SQL, diag)
                ot, ob = oo.next()
                R.op("act", lambda e, o=ot[:], i=A.po[:, 0:HG * SQL]: e.activation(out=o, in_=i, func=AF.Copy),
                     reads=[A.pob], writes=[ob])
                R.dma("sp", OUT[h0 * 128:(h0 + HG) * 128, col:col + SQL].rearrange("(h d) c -> d h c", d=128),
                      ot[:].rearrange("p (h c) -> p h c", c=SQL), reads=[ob], writes=[xbuf(C, "YP", 0, 1)])
        R.flush()
    R.barrier()


def build_program(upto=99):
    nc = bass.Bass("TRN2", target_bir_lowering=False)
    I = {}

    def inp(name, shape, dt=F32):
        I[name] = nc.dram_tensor(name, list(shape), dt, kind="ExternalInput").ap()
        return I[name]
    inp("xin", [D, RP])
    inp("ffn_w_gate", [4, D, FF]); inp("ffn_w_up", [4, D, FF]); inp("ffn_w_down", [4, FF, D])
    inp("ab_w_in", [D, D]); inp("ab_w_out", [D, D]); inp("pool_w", [4, 256, 256])
    inp("ssm_w_glu", [1024, 1024]); inp("sb_w_qkv", [D, 3 * D]); inp("sb_w_out", [D, D])
    inp("gall", [128, 7, 16]); inp("pscale", [128, 8]); inp("ssmd", [128, 8]); inp("bglu", [128, 8])
    inp("invcnt", [128, 4, 16])
    inp("ssm_are", [128, 32]); inp("ssm_aim", [128, 32]); inp("ssm_ldt", [128, 32])
    inp("ssm_Bre", [128, 32, 128]); inp("ssm_Bim", [128, 32, 128])
    inp("ssm_Cre", [128, 32, 128]); inp("ssm_Cim", [128, 32, 128])
    inp("ssm_h0re", [NSQ, 128, 32]); inp("ssm_h0im", [NSQ, 128, 32])
    inp("pool_hist", [128, 8, NSQ, 15])
    inp("cache_kT", [NSQ, 16, 128, PAST]); inp("cache_v", [NSQ, PAST, D])
    inp("c_tri", [128, 128]); inp("c_lt", [128, 128]); inp("c_ident", [128, 128])
    O = {}

    def outp(name, shape, dt=F32):
        O[name] = nc.dram_tensor(name, list(shape), dt, kind="ExternalOutput").ap()
        return O[name]
    outp("y", [D, RP]); outp("kout", [D, RP]); outp("vout", [D, RP])
    outp("pool_p", [1024, 15]); outp("pool_s", [1024, NSQ, 15])
    outp("hre_p", [128, 32]); outp("him_p", [128, 32]); outp("hre_s", [NSQ, 128, 32]); outp("him_s", [NSQ, 128, 32])
    X = nc.dram_tensor("Xs", [D, RP], F32).ap()
    U = nc.dram_tensor("Us", [D, RP], F32).ap()
    Z = nc.dram_tensor("Zs", [1024, RP], BF).ap()
    YP = nc.dram_tensor("YPs", [D, RP], BF).ap()
    Qs = nc.dram_tensor("Qs", [D, RP], BF).ap()
    Ks = nc.dram_tensor("Ks", [D, RP], BF).ap()
    Vs = nc.dram_tensor("Vs", [D, RP], BF).ap()

    R = Rec(nc)
    C = Ctx()
    C.dbufs = {}
    with ExitStack() as es:
        R.begin(es)
        def sb(name, shape, dt=F32):
            return es.enter_context(_sbt(nc, name, shape, dt))
        C.ones32 = sb("ones32", [128, 128]); C.onesb = sb("onesb", [128, 128], BF)
        C.trib = sb("trib", [128, 128], BF); C.LTb = sb("LTb", [128, 128], BF); C.identb = sb("identb", [128, 128], BF)
        C.gall = sb("gall", [128, 7, 16]); C.pscale = sb("pscale", [128, 8]); C.ssmd = sb("ssmd", [128, 8])
        C.bglu = sb("bglu", [128, 8]); C.invcnt = sb("invcnt", [128, 4, 16])
        C.epsb = sb("epsb", [128, 1]); C.oneb = sb("oneb", [128, 1]); C.halfpi = sb("halfpi", [128, 1])
        cb = Buf()
        R.op("dve", lambda e: e.memset(C.ones32[:], 1.0), writes=[cb])
        R.op("dve", lambda e: e.memset(C.onesb[:], 1.0), writes=[cb])
        R.op("dve", lambda e: e.memset(C.epsb[:], EPS), writes=[cb])
        R.op("dve", lambda e: e.memset(C.oneb[:], 1.0), writes=[cb])
        R.op("dve", lambda e: e.memset(C.halfpi[:], math.pi / 2), writes=[cb])
        R.dma("pool", C.trib[:], I["c_tri"], writes=[cb])
        R.dma("pool", C.LTb[:], I["c_lt"], writes=[cb])
        R.dma("pool", C.identb[:], I["c_ident"], writes=[cb])
        for nm in ("gall", "pscale", "ssmd", "bglu", "invcnt"):
            R.dma("sp", getattr(C, nm)[:], I[nm], writes=[cb])
        R.barrier()

        wgate = I["ffn_w_gate"]; wup = I["ffn_w_up"]; wdn = I["ffn_w_down"]
        stage = 0

        def go():
            nonlocal stage
            stage += 1
            return stage <= upto
        skipffn = os.environ.get("MK_SKIPFFN", "0") == "1"
        if go():
            if skipffn:
                for dk in range(16):
                    R.dma("sp", X[dk * 128:(dk + 1) * 128, :], I["xin"][dk * 128:(dk + 1) * 128, :], writes=[xbuf(C, "X", dk, 0)])
                R.barrier()
            else:
                ffn_stage(R, nc, C, I["xin"], "xin", X, "X", wgate[0], wup[0], wdn[0], 0)
        if go():
            def cb_u(es2):
                orr = Ring(es2, nc, "uo", 3, [128, 448], F32)

                def f(si, mc, c0, n, p, pb):
                    ot, ob = orr.next()
                    R.op("act", lambda e, o=ot[:, 0:n], i=p[:, 0:n]: e.activation(out=o, in_=i, func=AF.Copy), reads=[pb], writes=[ob])
                    R.dma("sp", U[mc * 128:(mc + 1) * 128, c0:c0 + n], ot[:, 0:n], reads=[ob],
                          writes=[xbuf(C, "U", mc, 0), xbuf(C, "U", 8, 0)] if mc >= 8 else [xbuf(C, "U", mc, 0)])
                return f
            norm_linear_stage(R, nc, C, X, "X", 4, I["ab_w_in"], D, cb_u)
            R.dma("sp", O["pool_p"], U[0:1024, NPR - 15:NPR], reads=[xbuf(C, "U", k, 0) for k in range(8)], writes=[xbuf(C, "pp", 0, 0)])
            for s in range(NSQ):
                R.dma("sp", O["pool_s"][:, s, :], U[0:1024, SC0 + s * SQL + 1:SC0 + (s + 1) * SQL],
                      reads=[xbuf(C, "U", k, 0) for k in range(8)], writes=[xbuf(C, "pp", 1, s)])
        if go():
            pool_stage(R, nc, C, U, YP, I)
        if go():
            ssm_stage(R, nc, C, U, Z, I, O)
        if go():
            glu_stage(R, nc, C, Z, YP, I)
        if go():
            for kt in range(16):
                for si in range(len(SUP)):
                    C.dbufs[("YPl", kt, si)] = Buf()
            linear_residual_stage(R, nc, C, YP, "YPl", I["ab_w_out"], X, "X")
        if go() and not skipffn:
            ffn_stage(R, nc, C, X, "X", X, "X", wgate[1], wup[1], wdn[1], 1)
        if go() and not skipffn:
            ffn_stage(R, nc, C, X, "X", X, "X", wgate[2], wup[2], wdn[2], 2)
        if go():
            def cb_qkv(es2):
                o32 = Ring(es2, nc, "qo32", 3, [128, 448], F32)
                obf = Ring(es2, nc, "qobf", 3, [128, 448], BF)

                def f(si, mc, c0, n, p, pb):
                    which = mc // 16
                    hh = mc % 16
                    dst = (Qs, Ks, Vs)[which]
                    ot, ob = obf.next()
                    if which == 0:
                        R.op("act", lambda e, o=ot[:, 0:n], i=p[:, 0:n]: e.activation(out=o, in_=i, func=AF.Copy), reads=[pb], writes=[ob])
                    else:
                        o2, o2b = o32.next()
                        R.op("act", lambda e, o=o2[:, 0:n], i=p[:, 0:n]: e.activation(out=o, in_=i, func=AF.Copy), reads=[pb], writes=[o2b])
                        R.dma("sp", (O["kout"], O["vout"])[which - 1][hh * 128:(hh + 1) * 128, c0:c0 + n], o2[:, 0:n],
                              reads=[o2b], writes=[xbuf(C, "kvout", which, mc)])
                        R.op("dve", lambda e, o=ot[:, 0:n], i=o2[:, 0:n]: e.tensor_copy(out=o, in_=i), reads=[o2b], writes=[ob])
                    R.dma("sp", dst[hh * 128:(hh + 1) * 128, c0:c0 + n], ot[:, 0:n], reads=[ob],
                          writes=[xbuf(C, "QKV", which, mc)])
                return f
            norm_linear_stage(R, nc, C, X, "X", 5, I["sb_w_qkv"], 3 * D, cb_qkv)
        if go():
            attn_prompt_stage(R, nc, C, Qs, Ks, Vs, YP)
        if go():
            attn_sample_stage(R, nc, C, Qs, Ks, Vs, YP, I)
        if go():
            for kt in range(16):
                for si in range(len(SUP)):
                    C.dbufs[("YPm", kt, si)] = Buf()
            linear_residual_stage(R, nc, C, YP, "YPm", I["sb_w_out"], X, "X")
        if go() and not skipffn:
            ffn_stage(R, nc, C, X, "X", X, "X", wgate[3], wup[3], wdn[3], 3)
        if go():
            final_stage2(R, nc, C, X, "X", O["y"])
        if upto < 99:
            dsrc = {1: X, 2: U, 6: X, 7: X, 8: X, 12: X, 13: X}.get(upto, X)
            for dk in range(16):
                R.dma("sp", O["y"][dk * 128:(dk + 1) * 128, :], dsrc[dk * 128:(dk + 1) * 128, :], writes=[xbuf(C, "Ydump", dk, 0)])
        R.finish()
        print('NOPS', upto, R.nops, 'sem cnt', R.cnt, 'dq', R.dq)
    return nc


def _state_layout(a):
    return np.ascontiguousarray(a.reshape(32, 2, 64).transpose(1, 2, 0).reshape(128, 32))


def _vec128(v):
    return np.ascontiguousarray(v.reshape(-1, 128).T)


_PROG = {}


def kernel(**inp):
    f32 = np.float32
    g = lambda k: np.asarray(inp[k], dtype=f32)
    upto = int(os.environ.get("MK_UPTO", "99"))
    if upto not in _PROG:
        _PROG[upto] = build_program(upto)
    nc = _PROG[upto]
    x_prompt = g("x_prompt"); x_sample = g("x_sample"); meta = g("meta_tokens")
    shared = {}
    shared["ffn_w_gate"] = g("ffn_w_gate").reshape(4, D, FF)
    shared["ffn_w_up"] = g("ffn_w_up").reshape(4, D, FF)
    shared["ffn_w_down"] = g("ffn_w_down").reshape(4, FF, D)
    shared["ab_w_in"] = g("ab_w_in")[0]; shared["ab_w_out"] = g("ab_w_out")[0]
    shared["pool_w"] = g("pool_w")[0]; shared["ssm_w_glu"] = g("ssm_w_glu")[0]
    shared["sb_w_qkv"] = g("sb_w_qkv")[0]; shared["sb_w_out"] = g("sb_w_out")[0]
    fn = g("ffn_norm").reshape(4, D); mn = g("mix_norm"); fin = g("final_norm")
    gall = np.stack([_vec128(fn[0]), _vec128(fn[1]), _vec128(fn[2]), _vec128(fn[3]),
                     _vec128(mn[0]), _vec128(mn[1]), _vec128(fin)], axis=1)
    shared["gall"] = np.ascontiguousarray(gall)
    shared["pscale"] = _vec128(g("pool_scale")[0]); shared["ssmd"] = _vec128(g("ssm_d")[0])
    shared["bglu"] = _vec128(g("ssm_b_glu")[0])
    ic = np.zeros((128, 4, 16), f32)
    for gi in range(4):
        w = 2 << gi
        ic[:, gi, :] = 1.0 / np.minimum(np.arange(16) + 1, w)
    shared["invcnt"] = ic
    shared["ssm_are"] = _state_layout(g("ssm_a_re")[0]); shared["ssm_aim"] = _state_layout(g("ssm_a_im")[0])
    shared["ssm_ldt"] = _state_layout(np.repeat(g("ssm_log_dt")[0][:, None], 64, axis=1))
    bre = g("ssm_b_re")[0]; bim = g("ssm_b_im")[0]; cre = g("ssm_c_re")[0]; cim = g("ssm_c_im")[0]
    Bre = np.zeros((128, 32, 128), f32); Bim = np.zeros((128, 32, 128), f32)
    Cre = np.zeros((128, 32, 128), f32); Cim = np.zeros((128, 32, 128), f32)
    for gg in range(64):
        st = gg // 2; gl = gg % 2
        ch0 = (gg % 8) * 16
        Bre[ch0:ch0 + 16, st, gl * 64:(gl + 1) * 64] = bre[gg].T
        Bim[ch0:ch0 + 16, st, gl * 64:(gl + 1) * 64] = bim[gg].T
        Cre[gl * 64:(gl + 1) * 64, st, ch0:ch0 + 16] = cre[gg].T
        Cim[gl * 64:(gl + 1) * 64, st, ch0:ch0 + 16] = cim[gg].T
    shared["ssm_Bre"] = Bre; shared["ssm_Bim"] = Bim; shared["ssm_Cre"] = Cre; shared["ssm_Cim"] = Cim
    jj = np.arange(128)
    shared["c_tri"] = (jj[:, None] < jj[None, :]).astype(f32)
    shared["c_lt"] = (jj[:, None] >= jj[None, :]).astype(f32)
    shared["c_ident"] = np.eye(128, dtype=f32)
    cache_pool = g("cache_pool")[0]; sre = g("state_ssm_re")[0]; sim_ = g("state_ssm_im")[0]
    cache_k = np.asarray(inp["cache_k"], dtype=f32)[0]; cache_v = np.asarray(inp["cache_v"], dtype=f32)[0]
    in_maps = []
    for c in range(8):
        b = c % 4
        m = dict(shared)
        xin = np.zeros((D, RP), f32)
        xin[:, 0:16] = meta.T
        xin[:, 16:NPR] = x_prompt[b].T
        sq = list(range(4 * c, 4 * c + 4))
        xin[:, SC0:SC0 + NSQ * SQL] = x_sample[sq].reshape(NSQ * SQL, D).T
        m["xin"] = xin
        m["ssm_h0re"] = np.ascontiguousarray(np.stack([_state_layout(sre[s]) for s in sq], axis=0))
        m["ssm_h0im"] = np.ascontiguousarray(np.stack([_state_layout(sim_[s]) for s in sq], axis=0))
        ph = cache_pool[sq]
        m["pool_hist"] = np.ascontiguousarray(ph.transpose(2, 0, 1).reshape(8, 128, NSQ, 15).transpose(1, 0, 2, 3))
        m["cache_kT"] = np.ascontiguousarray(cache_k[sq].transpose(0, 2, 3, 1))
        m["cache_v"] = np.ascontiguousarray(cache_v[sq].reshape(NSQ, PAST, D))
        in_maps.append(m)
    ncore = int(os.environ.get("MK_NCORE", "8"))
    if ncore < 8:
        res = run_bass_kernel_spmd(nc, in_maps[:ncore], core_ids=list(range(ncore)))
        return res.results
    res = run_bass_kernel_spmd(nc, in_maps, core_ids=list(range(8)))
    rs = res.results
    y_prompt = np.stack([rs[b]["y"][:, 16:NPR].T for b in range(4)], 0)
    y_sample = np.concatenate([rs[c]["y"][:, SC0:SC0 + NSQ * SQL].T.reshape(NSQ, SQL, D) for c in range(8)], 0)
    pool_p = np.stack([rs[b]["pool_p"].T for b in range(4)], 0)[None]
    pool_s = np.concatenate([rs[c]["pool_s"].transpose(1, 2, 0) for c in range(8)], 0)[None]

    def unstate(a):
        return a.reshape(2, 64, 32).transpose(2, 0, 1).reshape(64, 64)
    re_p = np.stack([unstate(rs[b]["hre_p"]) for b in range(4)], 0)[None]
    im_p = np.stack([unstate(rs[b]["him_p"]) for b in range(4)], 0)[None]
    re_s = np.stack([unstate(rs[c]["hre_s"][s]) for c in range(8) for s in range(NSQ)], 0)[None]
    im_s = np.stack([unstate(rs[c]["him_s"][s]) for c in range(8) for s in range(NSQ)], 0)[None]
    k_p = np.stack([rs[b]["kout"][:, 0:NPR].T.reshape(NPR, 16, 128) for b in range(4)], 0)[None]
    v_p = np.stack([rs[b]["vout"][:, 0:NPR].T.reshape(NPR, 16, 128) for b in range(4)], 0)[None]
    k_s = np.concatenate([rs[c]["kout"][:, SC0:SC0 + NSQ * SQL].T.reshape(NSQ, SQL, 16, 128) for c in range(8)], 0)[None]
    v_s = np.concatenate([rs[c]["vout"][:, SC0:SC0 + NSQ * SQL].T.reshape(NSQ, SQL, 16, 128) for c in range(8)], 0)[None]
    outs = (y_prompt, y_sample, pool_p, pool_s, re_p, im_p, re_s, im_s, k_p, v_p, k_s, v_s)
    return tuple(np.ascontiguousarray(o, dtype=f32) for o in outs)
```

```python
import os, math
import numpy as np
from contextlib import ExitStack
import concourse.bass as bass
import concourse.mybir as mybir
from concourse.bass_utils import run_bass_kernel_spmd

F32 = mybir.dt.float32
BF = mybir.dt.bfloat16
AF = mybir.ActivationFunctionType
ALU = mybir.AluOpType

RP = 4224
NPR = 4112
SC0 = 4112
NSQ = 4
SQL = 16
D = 2048
FF = 5632
NFC = 44
PAST = 4096
NHEAD = 16
SUP = [(0, 896), (896, 896), (1792, 896), (2688, 896), (3584, 640)]
LCH = 64
EPS = 1e-6


def cchunks(c0, W):
    h = W // 2
    return [(c0, h), (c0 + h, W - h)]


_UID = [0]


def _sbt(nc, name, shape, dt):
    _UID[0] += 1
    return nc.sbuf_tensor("%s_%d" % (name, _UID[0]), shape, dt)


def _pst(nc, name, shape, dt):
    _UID[0] += 1
    return nc.psum_tensor("%s_%d" % (name, _UID[0]), shape, dt)


class Buf:
    __slots__ = ("w", "r")

    def __init__(self):
        self.w = []
        self.r = {}


class _Cap:
    def __getattr__(self, name):
        def f(*a, **k):
            self.call = (name, a, k)
            return None
        return f


class Rec:
    ENG = ("pe", "act", "dve", "pool", "sp")
    K = 8

    def __init__(self, nc):
        self.nc = nc
        self.ops = {e: [] for e in self.ENG}
        self.cnt = {e: 0 for e in self.ENG}
        self.dq = {"sp": 0, "pool": 0, "act": 0}
        self.floor = {}
        self.nops = {e: 0 for e in self.ENG}

    def _deps(self, reads, writes, extra):
        deps = list(extra)
        for b in reads:
            deps += b.w
        for b in writes:
            deps += b.w
            deps += list(b.r.values())
        return deps

    def _upd(self, tok, key, reads, writes):
        for b in reads:
            b.r[key] = tok
        for b in writes:
            b.w = [tok]
            b.r = {}

    def op(self, eng, fn, reads=(), writes=(), inc=True, extra=()):
        deps = self._deps(reads, writes, extra)
        deps += self.floor.pop(eng, [])
        if inc:
            self.cnt[eng] += 1
            tok = ("c", eng, self.cnt[eng])
        else:
            tok = ("c", eng, self.cnt[eng] + 1)
        cap = _Cap()
        fn(cap)
        self.ops[eng].append(("op", cap.call, deps, inc))
        self._upd(tok, ("c", eng), reads, writes)
        return tok

    def dma(self, q, out, in_, reads=(), writes=(), extra=(), **kw):
        deps = self._deps(reads, writes, extra)
        deps += self.floor.pop(q, [])
        j = self.dq[q]
        self.dq[q] = j + 1
        slot = j % self.K
        val = 16 * (j // self.K + 1)
        if j >= self.K:
            deps.append(("d", q, slot, val - 16))
        tok = ("d", q, slot, val)
        self.ops[q].append(("dma", (lambda e, out=out, in_=in_, kw=kw: e.dma_start(out=out, in_=in_, **kw)), deps, slot))
        self._upd(tok, ("d", q, slot), reads, writes)
        return tok

    def all_tokens(self):
        toks = []
        for e in self.ENG:
            if self.cnt[e] > 0:
                toks.append(("c", e, self.cnt[e]))
        for q, j in self.dq.items():
            for slot in range(min(j, self.K)):
                n = (j - 1 - slot) // self.K + 1
                toks.append(("d", q, slot, 16 * n))
        return toks

    def barrier(self):
        toks = self.all_tokens()
        for e in self.ENG:
            self.floor[e] = list(toks) + self.floor.get(e, [])

    def begin(self, es):
        nc = self.nc
        self.csem = {e: es.enter_context(nc.semaphore("c_" + e)) for e in self.ENG}
        self.dsem = {q: [es.enter_context(nc.semaphore("d_%s%d" % (q, i))) for i in range(self.K)]
                     for q in self.dq}
        self.block = es.enter_context(nc.Block())
        self.waited = {e: {} for e in self.ENG}

    def flush(self):
        bname = {"pe": "tensor", "act": "scalar", "dve": "vector", "pool": "gpsimd", "sp": "sync"}
        csem, dsem = self.csem, self.dsem
        for ename in self.ENG:
            ops = self.ops[ename]
            self.nops[ename] += len(ops)
            self.ops[ename] = []
            if not ops:
                continue

            def body(e, ename=ename, ops=ops):
                waited = self.waited[ename]
                for (kind, fn, deps, aux) in ops:
                    for t in deps:
                        if t[0] == "c":
                            if t[1] == "pe" and ename == "pe":
                                continue
                            key = ("c", t[1]); val = t[2]; sem = csem[t[1]]
                        else:
                            key = ("d", t[1], t[2]); val = t[3]; sem = dsem[t[1]][t[2]]
                        if waited.get(key, 0) >= val:
                            continue
                        waited[key] = val
                        e.wait_ge(sem, val)
                    if kind == "wait":
                        continue
                    if kind == "op":
                        ins = getattr(e, fn[0])(*fn[1], **fn[2])
                    else:
                        ins = fn(e)
                    if kind == "dma":
                        ins.then_inc(dsem[ename][aux], 16)
                    elif aux:
                        ins.then_inc(csem[ename], 1)
            getattr(self.block, bname[ename])(body)

    def finish(self):
        final = self.all_tokens()
        self.ops["sp"].append(("wait", None, final, False))
        self.flush()


class Ring:
    def __init__(self, es, nc, name, n, shape, dtype, psum=False):
        self.t = []
        self.b = []
        for i in range(n):
            if psum:
                t = es.enter_context(_pst(nc, "%s%d" % (name, i), shape, dtype))
            else:
                t = es.enter_context(_sbt(nc, "%s%d" % (name, i), shape, dtype))
            self.t.append(t)
            self.b.append(Buf())
        self.i = 0

    def next(self):
        k = self.i % len(self.t)
        self.i += 1
        return self.t[k], self.b[k]


class Ctx:
    pass


def xbuf(C, name, dk, si):
    key = (name, dk, si)
    if key not in C.dbufs:
        C.dbufs[key] = Buf()
    return C.dbufs[key]


def phase_norm(R, nc, C, S, Xsrc, xname, si, c0, W, gi, hout, hbuf, hview=None):
    ch = cchunks(0, W)
    for dk in range(16):
        xt, xb = S.xr.next()
        R.dma("sp", xt[:, 0:W], Xsrc[dk * 128:(dk + 1) * 128, c0:c0 + W],
              reads=[xbuf(C, xname, dk, si)], writes=[xb])
        st, sb = S.sqr.next()
        R.op("act", lambda e, o=st[:, 0:W], i=xt[:, 0:W]: e.activation(out=o, in_=i, func=AF.Square),
             reads=[xb], writes=[sb])
        for ci, (cc, n) in enumerate(ch):
            R.op("pe", lambda e, o=S.pss[ci][:, 0:n], r=st[:, cc:cc + n], s=(dk == 0), p=(dk == 15):
                 e.matmul(o, lhsT=C.ones32[:], rhs=r, start=s, stop=p),
                 reads=[sb], writes=[S.pssb[ci]], inc=True)
    for ci, (cc, n) in enumerate(ch):
        R.op("act", lambda e, o=S.srt[:, cc:cc + n], i=S.pss[ci][:, 0:n]:
             e.activation(out=o, in_=i, func=AF.Sqrt, bias=C.epsb[:, 0:1], scale=1.0 / D),
             reads=[S.pssb[ci]], writes=[S.srtb])
    R.op("dve", lambda e: e.reciprocal(out=S.rstd[:, 0:W], in_=S.srt[:, 0:W]),
         reads=[S.srtb], writes=[S.rstdb])
    for dk in range(16):
        xt, xb = S.xr.next()
        R.dma("sp", xt[:, 0:W], Xsrc[dk * 128:(dk + 1) * 128, c0:c0 + W],
              reads=[xbuf(C, xname, dk, si)], writes=[xb])
        o = hout(dk)
        R.op("dve", lambda e, o=o, i=xt[:, 0:W], g=C.gall[:, gi, dk:dk + 1]:
             e.scalar_tensor_tensor(out=o, in0=i, scalar=g, in1=S.rstd[:, 0:W], op0=ALU.mult, op1=ALU.mult),
             reads=[xb, S.rstdb], writes=[hbuf(dk)])


def norm_tasks(R, C, S, Xsrc, xname, si, c0, W, gi, hout, hbuf):
    ch = cchunks(0, W)
    hold = {}

    def A(dk):
        xt, xb = S.xr.next()
        R.dma("sp", xt[:, 0:W], Xsrc[dk * 128:(dk + 1) * 128, c0:c0 + W],
              reads=[xbuf(C, xname, dk, si)], writes=[xb])
        st, sb = S.sqr.next()
        R.op("act", lambda e, o=st[:, 0:W], i=xt[:, 0:W]: e.activation(out=o, in_=i, func=AF.Square),
             reads=[xb], writes=[sb])
        hi, hib = S.hir.next()
        lo, lob = S.lor.next()
        R.op("dve", lambda e, o=hi[:, 0:W], i=st[:, 0:W]: e.tensor_copy(out=o, in_=i), reads=[sb], writes=[hib])
        R.op("dve", lambda e, o=lo[:, 0:W], a=st[:, 0:W], b=hi[:, 0:W]:
             e.tensor_tensor(out=o, in0=a, in1=b, op=ALU.subtract), reads=[sb, hib], writes=[lob])
        hold[("s", dk)] = (hi, hib, lo, lob)

    def B(dk):
        hi, hib, lo, lob = hold.pop(("s", dk))
        for ci, (cc, n) in enumerate(ch):
            R.op("pe", lambda e, o=S.pss[ci][:, 0:n], r=hi[:, cc:cc + n], s=(dk == 0):
                 e.matmul(o, lhsT=C.onesb[:], rhs=r, start=s, stop=False),
                 reads=[hib], writes=[S.pssb[ci]], inc=False)
            R.op("pe", lambda e, o=S.pss[ci][:, 0:n], r=lo[:, cc:cc + n], p=(dk == 15):
                 e.matmul(o, lhsT=C.onesb[:], rhs=r, start=False, stop=p),
                 reads=[lob], writes=[S.pssb[ci]], inc=True)

    def FIN():
        for ci, (cc, n) in enumerate(ch):
            R.op("act", lambda e, o=S.srt[:, cc:cc + n], i=S.pss[ci][:, 0:n]:
                 e.activation(out=o, in_=i, func=AF.Sqrt, bias=C.epsb[:, 0:1], scale=1.0 / D),
                 reads=[S.pssb[ci]], writes=[S.srtb])
        R.op("dve", lambda e: e.reciprocal(out=S.rstd[:, 0:W], in_=S.srt[:, 0:W]),
             reads=[S.srtb], writes=[S.rstdb])

    def A2(dk):
        xt, xb = S.xr.next()
        R.dma("sp", xt[:, 0:W], Xsrc[dk * 128:(dk + 1) * 128, c0:c0 + W],
              reads=[xbuf(C, xname, dk, si)], writes=[xb])
        hold[("x", dk)] = (xt, xb)

    def B2(dk):
        xt, xb = hold.pop(("x", dk))
        o = hout(dk)
        R.op("dve", lambda e, o=o, i=xt[:, 0:W], g=C.gall[:, gi, dk:dk + 1]:
             e.scalar_tensor_tensor(out=o, in0=i, scalar=g, in1=S.rstd[:, 0:W], op0=ALU.mult, op1=ALU.mult),
             reads=[xb, S.rstdb], writes=[hbuf(dk)])

    stats = [lambda: A(0)]
    for dk in range(16):
        if dk + 1 < 16:
            stats.append(lambda dk=dk: A(dk + 1))
        stats.append(lambda dk=dk: B(dk))
    stats.append(FIN)
    app = [lambda: A2(0)]
    for dk in range(16):
        if dk + 1 < 16:
            app.append(lambda dk=dk: A2(dk + 1))
        app.append(lambda dk=dk: B2(dk))
    return stats, app


def run_tasks(tasks, k):
    for _ in range(k):
        if tasks:
            tasks.pop(0)()


def alloc_norm(es, nc, S):
    S.xr = Ring(es, nc, "xr", 4, [128, 896], F32)
    S.sqr = Ring(es, nc, "sqr", 2, [128, 896], F32)
    S.hir = Ring(es, nc, "sqhi", 2, [128, 896], BF)
    S.lor = Ring(es, nc, "sqlo", 2, [128, 896], BF)
    S.pss = [es.enter_context(_pst(nc, "pss%d" % i, [128, 512], F32)) for i in range(2)]
    S.pssb = [Buf(), Buf()]
    S.srt = es.enter_context(_sbt(nc, "srt", [128, 896], F32))
    S.srtb = Buf()
    S.rstd = es.enter_context(_sbt(nc, "rstd", [128, 896], F32))
    S.rstdb = Buf()


def ffn_stage(R, nc, C, Xsrc, sname, Xdst, dname, wg, wu, wd, gi):
    wgv = wg.rearrange("(kt p) f -> p kt f", p=128)
    wuv = wu.rearrange("(kt p) f -> p kt f", p=128)
    wdv = wd.rearrange("(fc p) d -> p fc d", p=128)
    with ExitStack() as es:
        S = Ctx()
        alloc_norm(es, nc, S)
        hfm = es.enter_context(_sbt(nc, "hfm", [128, 16, 896], BF))
        hb = [Buf() for _ in range(16)]
        act = es.enter_context(_sbt(nc, "actf", [128, NFC, 896], BF))
        ab = [Buf() for _ in range(NFC)]
        wgr = Ring(es, nc, "wg", 2, [128, 16, 128], BF)
        wur = Ring(es, nc, "wu", 2, [128, 16, 128], BF)
        wdr = Ring(es, nc, "wd", 2, [128, NFC, 128], BF)
        pgr = Ring(es, nc, "pg", 2, [128, 512], F32, psum=True)
        pur = Ring(es, nc, "pu", 2, [128, 512], F32, psum=True)
        sgr = Ring(es, nc, "sg", 2, [128, 448], F32)
        xor_ = Ring(es, nc, "xo", 2, [128, 448], F32)
        rxr = Ring(es, nc, "rxr", 3, [128, 448], F32)
        def mk(si):
            c0n, Wn = SUP[si]
            return norm_tasks(R, C, S, Xsrc, sname, si, c0n, Wn, gi,
                              hout=lambda dk, Wn=Wn: hfm[:, dk, 0:Wn], hbuf=lambda dk: hb[dk])
        st0, ap0 = mk(0)
        run_tasks(st0, 99)
        run_tasks(ap0, 99)
        for si, (c0, W) in enumerate(SUP):
            if si + 1 < len(SUP):
                stt, apt = mk(si + 1)
            else:
                stt, apt = [], []
            ch = cchunks(0, W)
            for fc in range(NFC):
                wgt, wgb = wgr.next()
                wut, wub = wur.next()
                R.dma("pool", wgt[:], wgv[:, :, fc * 128:(fc + 1) * 128], writes=[wgb])
                R.dma("pool", wut[:], wuv[:, :, fc * 128:(fc + 1) * 128], writes=[wub])
                for (cc, n) in ch:
                    pg, pgb = pgr.next()
                    pu, pub = pur.next()
                    for kt in range(16):
                        R.op("pe", lambda e, o=pg[:, 0:n], l=wgt[:, kt, :], r=hfm[:, kt, cc:cc + n], s=(kt == 0), p=(kt == 15):
                             e.matmul(o, lhsT=l, rhs=r, start=s, stop=p),
                             reads=[wgb, hb[kt]], writes=[pgb], inc=(kt == 15))
                    for kt in range(16):
                        R.op("pe", lambda e, o=pu[:, 0:n], l=wut[:, kt, :], r=hfm[:, kt, cc:cc + n], s=(kt == 0), p=(kt == 15):
                             e.matmul(o, lhsT=l, rhs=r, start=s, stop=p),
                             reads=[wub, hb[kt]], writes=[pub], inc=(kt == 15))
                    sg, sgb = sgr.next()
                    R.op("act", lambda e, o=sg[:, 0:n], i=pg[:, 0:n]: e.activation(out=o, in_=i, func=AF.Silu),
                         reads=[pgb], writes=[sgb])
                    R.op("dve", lambda e, o=act[:, fc, cc:cc + n], a=sg[:, 0:n], b=pu[:, 0:n]:
                         e.tensor_tensor(out=o, in0=a, in1=b, op=ALU.mult),
                         reads=[sgb, pub], writes=[ab[fc]])
                run_tasks(stt, 1)
            run_tasks(stt, 99)
            for dcc in range(16):
                wdt, wdb = wdr.next()
                R.dma("pool", wdt[:], wdv[:, :, dcc * 128:(dcc + 1) * 128], writes=[wdb])
                for (cc, n) in ch:
                    py, pyb = pgr.next()
                    for fc in range(NFC):
                        R.op("pe", lambda e, o=py[:, 0:n], l=wdt[:, fc, :], r=act[:, fc, cc:cc + n], s=(fc == 0), p=(fc == NFC - 1):
                             e.matmul(o, lhsT=l, rhs=r, start=s, stop=p),
                             reads=[wdb, ab[fc]], writes=[pyb], inc=(fc == NFC - 1))
                    xt, xb = rxr.next()
                    R.dma("sp", xt[:, 0:n], Xsrc[dcc * 128:(dcc + 1) * 128, c0 + cc:c0 + cc + n],
                          reads=[xbuf(C, sname, dcc, si)], writes=[xb])
                    xo, xob = xor_.next()
                    R.op("dve", lambda e, o=xo[:, 0:n], a=py[:, 0:n], b=xt[:, 0:n]:
                         e.scalar_tensor_tensor(out=o, in0=a, scalar=0.5, in1=b, op0=ALU.mult, op1=ALU.add),
                         reads=[pyb, xb], writes=[xob])
                    R.dma("sp", Xdst[dcc * 128:(dcc + 1) * 128, c0 + cc:c0 + cc + n], xo[:, 0:n],
                          reads=[xob], writes=[xbuf(C, dname, dcc, si)])
                run_tasks(apt, 2 if dcc < 15 else 99)
        R.flush()
    R.barrier()


def norm_linear_stage(R, nc, C, Xsrc, sname, gi, Wd, M, out_cb):
    wv = Wd.rearrange("(kt p) m -> p kt m", p=128)
    with ExitStack() as es:
        S = Ctx()
        alloc_norm(es, nc, S)
        hfms = [es.enter_context(_sbt(nc, "hfm%d" % i, [128, 16, 896], BF)) for i in range(2)]
        hbs = [[Buf() for _ in range(16)] for _ in range(2)]
        wr = Ring(es, nc, "wl", 2, [128, 16, 128], BF)
        pr = Ring(es, nc, "pl", 3, [128, 512], F32, psum=True)
        S.es = es
        cbs = out_cb(es)

        def mk(si):
            c0n, Wn = SUP[si]
            hf_, hb_ = hfms[si % 2], hbs[si % 2]
            return norm_tasks(R, C, S, Xsrc, sname, si, c0n, Wn, gi,
                              hout=lambda dk, Wn=Wn, hf_=hf_: hf_[:, dk, 0:Wn], hbuf=lambda dk, hb_=hb_: hb_[dk])
        st0, ap0 = mk(0)
        run_tasks(st0, 99)
        run_tasks(ap0, 99)
        MG = M // 128
        for si, (c0, W) in enumerate(SUP):
            hfm, hb = hfms[si % 2], hbs[si % 2]
            if si + 1 < len(SUP):
                stt, apt = mk(si + 1)
                tasks = stt + apt
            else:
                tasks = []
            per = -(-len(tasks) // max(1, MG - 2))
            for mc in range(MG):
                wt, wb = wr.next()
                R.dma("pool", wt[:], wv[:, :, mc * 128:(mc + 1) * 128], writes=[wb])
                for (cc, n) in cchunks(0, W):
                    p, pb = pr.next()
                    for kt in range(16):
                        R.op("pe", lambda e, o=p[:, 0:n], l=wt[:, kt, :], r=hfm[:, kt, cc:cc + n], s=(kt == 0), q=(kt == 15):
                             e.matmul(o, lhsT=l, rhs=r, start=s, stop=q),
                             reads=[wb, hb[kt]], writes=[pb], inc=(kt == 15))
                    cbs(si, mc, c0 + cc, n, p, pb)
                run_tasks(tasks, per)
            run_tasks(tasks, 999)
        R.flush()
    R.barrier()


def linear_residual_stage(R, nc, C, Asrc, aname, Wd, X, xname):
    wv = Wd.rearrange("(kt p) m -> p kt m", p=128)
    with ExitStack() as es:
        afm = es.enter_context(_sbt(nc, "afm", [128, 16, 896], BF))
        ab = [Buf() for _ in range(16)]
        wr = Ring(es, nc, "wl", 2, [128, 16, 128], BF)
        pr = Ring(es, nc, "pl", 3, [128, 512], F32, psum=True)
        xr = Ring(es, nc, "xr", 3, [128, 448], F32)
        xor_ = Ring(es, nc, "xo", 3, [128, 448], F32)
        for si, (c0, W) in enumerate(SUP):
            for kt in range(16):
                R.dma("sp", afm[:, kt, 0:W], Asrc[kt * 128:(kt + 1) * 128, c0:c0 + W],
                      reads=[xbuf(C, aname, kt, si)], writes=[ab[kt]])
            for mc in range(16):
                wt, wb = wr.next()
                R.dma("pool", wt[:], wv[:, :, mc * 128:(mc + 1) * 128], writes=[wb])
                for (cc, n) in cchunks(0, W):
                    p, pb = pr.next()
                    for kt in range(16):
                        R.op("pe", lambda e, o=p[:, 0:n], l=wt[:, kt, :], r=afm[:, kt, cc:cc + n], s=(kt == 0), q=(kt == 15):
                             e.matmul(o, lhsT=l, rhs=r, start=s, stop=q),
                             reads=[wb, ab[kt]], writes=[pb], inc=(kt == 15))
                    xt, xb = xr.next()
                    R.dma("sp", xt[:, 0:n], X[mc * 128:(mc + 1) * 128, c0 + cc:c0 + cc + n],
                          reads=[xbuf(C, xname, mc, si)], writes=[xb])
                    xo, xob = xor_.next()
                    R.op("dve", lambda e, o=xo[:, 0:n], a=p[:, 0:n], b=xt[:, 0:n]:
                         e.tensor_tensor(out=o, in0=a, in1=b, op=ALU.add),
                         reads=[pb, xb], writes=[xob])
                    R.dma("sp", X[mc * 128:(mc + 1) * 128, c0 + cc:c0 + cc + n], xo[:, 0:n],
                          reads=[xob], writes=[xbuf(C, xname, mc, si)])
        R.flush()
    R.barrier()


def final_stage2(R, nc, C, X, xname, Y):
    with ExitStack() as es:
        S = Ctx()
        alloc_norm(es, nc, S)
        yr = Ring(es, nc, "yr", 3, [128, 896], F32)
        for si, (c0, W) in enumerate(SUP):
            ch = cchunks(0, W)
            for dk in range(16):
                xt, xb = S.xr.next()
                R.dma("sp", xt[:, 0:W], X[dk * 128:(dk + 1) * 128, c0:c0 + W],
                      reads=[xbuf(C, xname, dk, si)], writes=[xb])
                st, sb = S.sqr.next()
                R.op("act", lambda e, o=st[:, 0:W], i=xt[:, 0:W]: e.activation(out=o, in_=i, func=AF.Square),
                     reads=[xb], writes=[sb])
                for ci, (cc, n) in enumerate(ch):
                    R.op("pe", lambda e, o=S.pss[ci][:, 0:n], r=st[:, cc:cc + n], s=(dk == 0), p=(dk == 15):
                         e.matmul(o, lhsT=C.ones32[:], rhs=r, start=s, stop=p),
                         reads=[sb], writes=[S.pssb[ci]], inc=True)
            for ci, (cc, n) in enumerate(ch):
                R.op("act", lambda e, o=S.srt[:, cc:cc + n], i=S.pss[ci][:, 0:n]:
                     e.activation(out=o, in_=i, func=AF.Sqrt, bias=C.epsb[:, 0:1], scale=1.0 / D),
                     reads=[S.pssb[ci]], writes=[S.srtb])
            R.op("dve", lambda e: e.reciprocal(out=S.rstd[:, 0:W], in_=S.srt[:, 0:W]),
                 reads=[S.srtb], writes=[S.rstdb])
            for dk in range(16):
                xt, xb = S.xr.next()
                R.dma("sp", xt[:, 0:W], X[dk * 128:(dk + 1) * 128, c0:c0 + W],
                      reads=[xbuf(C, xname, dk, si)], writes=[xb])
                yt, yb = yr.next()
                R.op("dve", lambda e, o=yt[:, 0:W], i=xt[:, 0:W], g=C.gall[:, 6, dk:dk + 1]:
                     e.scalar_tensor_tensor(out=o, in0=i, scalar=g, in1=S.rstd[:, 0:W], op0=ALU.mult, op1=ALU.mult),
                     reads=[xb, S.rstdb], writes=[yb])
                R.dma("sp", Y[dk * 128:(dk + 1) * 128, c0:c0 + W], yt[:, 0:W], reads=[yb],
                      writes=[xbuf(C, "Y", dk, si)])
        R.flush()
    R.barrier()


def pool_stage(R, nc, C, U, YP, I):
    PADC = 16
    with ExitStack() as es:
        NCOL = PADC + NPR
        lev = [es.enter_context(_sbt(nc, "lev%d" % i, [128, NCOL], F32)) for i in range(3)]
        levb = [Buf() for _ in range(3)]
        slev = [es.enter_context(_sbt(nc, "slev%d" % i, [128, NSQ, 32], F32)) for i in range(3)]
        slevb = [Buf() for _ in range(3)]
        dfr = Ring(es, nc, "dfm", 2, [128, RP], BF)
        wp = es.enter_context(_sbt(nc, "wp", [128, 2, 256], BF))
        wpb = Buf()
        pr = Ring(es, nc, "pp", 2, [128, 512], F32, psum=True)
        orr = Ring(es, nc, "po", 2, [128, 512], BF)
        tfix = es.enter_context(_sbt(nc, "tfix", [128, 16], F32))
        tfb = Buf()
        for i in range(3):
            R.op("pool", lambda e, t=lev[i]: e.memset(t[:, 0:PADC], 0.0), writes=[levb[i]])
            R.op("pool", lambda e, t=slev[i]: e.memset(t[:], 0.0), writes=[slevb[i]])
        for g in range(4):
            w = 2 << g
            nst = g + 1
            R.dma("pool", wp[:], I["pool_w"][g].rearrange("(kt p) m -> p kt m", p=128), writes=[wpb])
            dts = []
            for kt2 in range(2):
                ct = 2 * g + kt2
                R.dma("sp", lev[0][:, PADC:PADC + NPR], U[ct * 128:(ct + 1) * 128, 0:NPR],
                      reads=[xbuf(C, "U", ct, 0)], writes=[levb[0]])
                R.dma("sp", slev[0][:, :, 1:16], I["pool_hist"][:, ct, :, :], writes=[slevb[0]])
                R.dma("sp", slev[0][:, :, 16:32],
                      U[ct * 128:(ct + 1) * 128, SC0:SC0 + NSQ * SQL].rearrange("p (s t) -> p s t", t=SQL),
                      reads=[xbuf(C, "U", ct, 0)], writes=[slevb[0]])
                src = 0
                srcb = levb[0]
                ssrc = 0
                for s in range(nst):
                    sh = 1 << s
                    dst = 1 if src != 1 else 2
                    R.op("dve", lambda e, o=lev[dst][:, PADC:NCOL], a=lev[src][:, PADC:NCOL], b=lev[src][:, PADC - sh:NCOL - sh]:
                         e.tensor_tensor(out=o, in0=a, in1=b, op=ALU.add),
                         reads=[levb[src]], writes=[levb[dst]])
                    R.op("pool", lambda e, o=slev[dst][:, :, sh:32], a=slev[ssrc][:, :, sh:32], b=slev[ssrc][:, :, 0:32 - sh]:
                         e.tensor_tensor(out=o, in0=a, in1=b, op=ALU.add),
                         reads=[slevb[ssrc]], writes=[slevb[dst]])
                    src = dst
                    ssrc = dst
                df, dfb = dfr.next()
                dts.append((df, dfb))
                R.op("dve", lambda e, o=df[:, 0:NPR], a=lev[src][:, PADC:NCOL], b=lev[0][:, PADC:NCOL]:
                     e.scalar_tensor_tensor(out=o, in0=a, scalar=1.0 / w, in1=b, op0=ALU.mult, op1=ALU.subtract),
                     reads=[levb[src], levb[0]], writes=[dfb])
                R.op("dve", lambda e, a=lev[src][:, PADC:PADC + 16], b=C.invcnt[:, g, :]:
                     e.tensor_tensor(out=tfix[:], in0=a, in1=b, op=ALU.mult),
                     reads=[levb[src]], writes=[tfb])
                R.op("dve", lambda e, o=df[:, 0:16], b=lev[0][:, PADC:PADC + 16]:
                     e.tensor_tensor(out=o, in0=tfix[:], in1=b, op=ALU.subtract),
                     reads=[tfb, levb[0]], writes=[dfb])
                R.op("dve", lambda e, o=df[:, SC0:SC0 + NSQ * SQL].rearrange("p (s t) -> p s t", t=SQL), a=slev[ssrc][:, :, 16:32], b=slev[0][:, :, 16:32]:
                     e.scalar_tensor_tensor(out=o, in0=a, scalar=1.0 / w, in1=b, op0=ALU.mult, op1=ALU.subtract),
                     reads=[slevb[ssrc], slevb[0]], writes=[dfb])
                R.op("pool", lambda e, o=df[:, SC0 + NSQ * SQL:RP]: e.memset(o, 0.0), writes=[dfb])
            for m in range(2):
                oc = 2 * g + m
                for cc in range(0, RP, 512):
                    n = min(512, RP - cc)
                    p, pb = pr.next()
                    for kt2 in range(2):
                        R.op("pe", lambda e, o=p[:, 0:n], l=wp[:, kt2, m * 128:(m + 1) * 128], r=dts[kt2][0][:, cc:cc + n], s=(kt2 == 0), q=(kt2 == 1):
                             e.matmul(o, lhsT=l, rhs=r, start=s, stop=q),
                             reads=[wpb, dts[kt2][1]], writes=[pb], inc=(kt2 == 1))
                    ot, ob = orr.next()
                    R.op("act", lambda e, o=ot[:, 0:n], i=p[:, 0:n], sc=C.pscale[:, oc:oc + 1]:
                         e.activation(out=o, in_=i, func=AF.Copy, scale=sc),
                         reads=[pb], writes=[ob])
                    R.dma("sp", YP[oc * 128:(oc + 1) * 128, cc:cc + n], ot[:, 0:n], reads=[ob],
                          writes=[xbuf(C, "YP", oc, 0)])
        R.flush()
    R.barrier()


def ssm_stage(R, nc, C, U, Z, I, O):
    L = LCH
    HT = 16
    with ExitStack() as es:
        def sb(name, shape, dt=F32):
            return es.enter_context(_sbt(nc, name, shape, dt))
        are = sb("are", [128, 32]); aim = sb("aim", [128, 32]); ldt = sb("ldt", [128, 32])
        t0 = sb("t0", [128, 32]); t1 = sb("t1", [128, 32]); t2 = sb("t2", [128, 32]); t3 = sb("t3", [128, 32])
        cs = sb("cs", [128, 32]); sn = sb("sn", [128, 32]); mag = sb("mag", [128, 32])
        cre = sb("cre", [128, 32]); cim = sb("cim", [128, 32])
        Ec = sb("Ec", [128, 32, L]); Es = sb("Es", [128, 32, L])
        nEs = sb("nEs", [128, 32, L]); nEc = sb("nEc", [128, 32, L])
        TA = sb("TA", [128, 32, L]); TB = sb("TB", [128, 32, L])
        Rb = sb("Rb", [128, 32, L])
        ELc = sb("ELc", [128, 32]); ELs = sb("ELs", [128, 32])
        one = Buf()
        Bre = sb("Bre", [128, 32, 128], BF); Bim = sb("Bim", [128, 32, 128], BF)
        Cre = sb("Cre", [128, 32, 128], BF); Cim = sb("Cim", [128, 32, 128], BF)
        wb_ = Buf()
        R.dma("sp", are[:], I["ssm_are"], writes=[one])
        R.dma("sp", aim[:], I["ssm_aim"], writes=[one])
        R.dma("sp", ldt[:], I["ssm_ldt"], writes=[one])
        R.dma("pool", Bre[:], I["ssm_Bre"], writes=[wb_])
        R.dma("pool", Bim[:], I["ssm_Bim"], writes=[wb_])
        R.dma("pool", Cre[:], I["ssm_Cre"], writes=[wb_])
        R.dma("pool", Cim[:], I["ssm_Cim"], writes=[wb_])

        def A(fn):
            R.op("act", fn, reads=[one], writes=[one])

        def V(fn):
            R.op("dve", fn, reads=[one], writes=[one])
        A(lambda e: e.activation(out=t0[:], in_=ldt[:], func=AF.Exp))
        V(lambda e: e.tensor_tensor(out=t1[:], in0=are[:], in1=t0[:], op=ALU.mult))
        V(lambda e: e.tensor_tensor(out=t2[:], in0=aim[:], in1=t0[:], op=ALU.mult))
        A(lambda e: e.activation(out=mag[:], in_=t1[:], func=AF.Exp))
        A(lambda e: e.activation(out=sn[:], in_=t2[:], func=AF.Sin, scale=1.0 / 16))
        A(lambda e: e.activation(out=cs[:], in_=t2[:], func=AF.Sin, scale=-1.0 / 16, bias=C.halfpi[:, 0:1]))
        for _ in range(4):
            V(lambda e: e.tensor_tensor(out=t0[:], in0=cs[:], in1=cs[:], op=ALU.mult))
            V(lambda e: e.tensor_tensor(out=t1[:], in0=sn[:], in1=sn[:], op=ALU.mult))
            V(lambda e: e.tensor_tensor(out=t3[:], in0=cs[:], in1=sn[:], op=ALU.mult))
            V(lambda e: e.tensor_tensor(out=cs[:], in0=t0[:], in1=t1[:], op=ALU.subtract))
            V(lambda e: e.tensor_scalar(out=sn[:], in0=t3[:], scalar1=2.0, scalar2=None, op0=ALU.mult))
        V(lambda e: e.tensor_tensor(out=t0[:], in0=mag[:], in1=cs[:], op=ALU.mult))
        V(lambda e: e.tensor_tensor(out=t1[:], in0=mag[:], in1=sn[:], op=ALU.mult))
        V(lambda e: e.tensor_scalar(out=t0[:], in0=t0[:], scalar1=-1.0, scalar2=None, op0=ALU.add))
        V(lambda e: e.tensor_tensor(out=t2[:], in0=are[:], in1=are[:], op=ALU.mult))
        V(lambda e: e.tensor_tensor(out=t3[:], in0=aim[:], in1=aim[:], op=ALU.mult))
        V(lambda e: e.tensor_tensor(out=t2[:], in0=t2[:], in1=t3[:], op=ALU.add))
        V(lambda e: e.reciprocal(out=t2[:], in_=t2[:]))
        V(lambda e: e.tensor_tensor(out=cre[:], in0=t0[:], in1=are[:], op=ALU.mult))
        V(lambda e: e.tensor_tensor(out=t3[:], in0=t1[:], in1=aim[:], op=ALU.mult))
        V(lambda e: e.tensor_tensor(out=cre[:], in0=cre[:], in1=t3[:], op=ALU.add))
        V(lambda e: e.tensor_tensor(out=cre[:], in0=cre[:], in1=t2[:], op=ALU.mult))
        V(lambda e: e.tensor_tensor(out=cim[:], in0=t1[:], in1=are[:], op=ALU.mult))
        V(lambda e: e.tensor_tensor(out=t3[:], in0=t0[:], in1=aim[:], op=ALU.mult))
        V(lambda e: e.tensor_tensor(out=cim[:], in0=cim[:], in1=t3[:], op=ALU.subtract))
        V(lambda e: e.tensor_tensor(out=cim[:], in0=cim[:], in1=t2[:], op=ALU.mult))
        V(lambda e: e.tensor_copy(out=Ec[:, :, 0:1], in_=cs[:].unsqueeze(2)))
        V(lambda e: e.tensor_copy(out=Es[:, :, 0:1], in_=sn[:].unsqueeze(2)))
        m = 1
        while m < L:
            bc = Ec[:, :, m - 1:m].broadcast_to([128, 32, m])
            bs = Es[:, :, m - 1:m].broadcast_to([128, 32, m])
            V(lambda e, m=m, bc=bc: e.tensor_tensor(out=TA[:, :, 0:m], in0=Ec[:, :, 0:m], in1=bc, op=ALU.mult))
            V(lambda e, m=m, bs=bs: e.tensor_tensor(out=TB[:, :, 0:m], in0=Es[:, :, 0:m], in1=bs, op=ALU.mult))
            V(lambda e, m=m: e.tensor_tensor(out=Ec[:, :, m:2 * m], in0=TA[:, :, 0:m], in1=TB[:, :, 0:m], op=ALU.subtract))
            V(lambda e, m=m, bs=bs: e.tensor_tensor(out=TA[:, :, 0:m], in0=Ec[:, :, 0:m], in1=bs, op=ALU.mult))
            V(lambda e, m=m, bc=bc: e.tensor_tensor(out=TB[:, :, 0:m], in0=Es[:, :, 0:m], in1=bc, op=ALU.mult))
            V(lambda e, m=m: e.tensor_tensor(out=Es[:, :, m:2 * m], in0=TA[:, :, 0:m], in1=TB[:, :, 0:m], op=ALU.add))
            m *= 2
        V(lambda e: e.tensor_scalar(out=nEs[:], in0=Es[:], scalar1=-1.0, scalar2=None, op0=ALU.mult))
        V(lambda e: e.tensor_scalar(out=nEc[:], in0=Ec[:], scalar1=-1.0, scalar2=None, op0=ALU.mult))
        crb = cre[:].unsqueeze(2).broadcast_to([128, 32, L])
        cib = cim[:].unsqueeze(2).broadcast_to([128, 32, L])
        V(lambda e: e.tensor_tensor(out=TA[:], in0=Ec[:], in1=crb, op=ALU.mult))
        V(lambda e: e.tensor_tensor(out=Rb[:], in0=Es[:], in1=cib, op=ALU.mult))
        V(lambda e: e.tensor_tensor(out=TA[:], in0=TA[:], in1=Rb[:], op=ALU.add))
        V(lambda e: e.tensor_tensor(out=TB[:], in0=Ec[:], in1=cib, op=ALU.mult))
        V(lambda e: e.tensor_tensor(out=Rb[:], in0=Es[:], in1=crb, op=ALU.mult))
        V(lambda e: e.tensor_tensor(out=TB[:], in0=TB[:], in1=Rb[:], op=ALU.subtract))
        V(lambda e: e.tensor_copy(out=Rb[:], in_=mag[:].unsqueeze(2).broadcast_to([128, 32, L])))
        tabs = one

        ur = Ring(es, nc, "ubf", 3, [128, 8, L], BF)
        u32 = Ring(es, nc, "u32", 3, [128, 8, L], F32)
        Pre = es.enter_context(_pst(nc, "Pre", [128, HT, L], F32)); Preb = Buf()
        Pim = es.enter_context(_pst(nc, "Pim", [128, HT, L], F32)); Pimb = Buf()
        Yp = Ring(es, nc, "Yp", 2, [128, 8, L], F32, psum=True)
        m1 = sb("m1", [128, HT, L]); m2 = sb("m2", [128, HT, L])
        cr_ = sb("cr_", [128, HT, L]); ci_ = sb("ci_", [128, HT, L])
        kr = sb("kr", [128, 32, L]); ki = sb("ki", [128, 32, L])
        m1b, m2b, crb_, cib_ = Buf(), Buf(), Buf(), Buf()
        krb = [Buf(), Buf()]; kib = [Buf(), Buf()]
        qr = [[Ring(es, nc, "q%d%d" % (k, hf), 2, [128, HT, L], BF) for hf in range(2)] for k in range(4)]
        hre = sb("hre", [128, 32]); him = sb("him", [128, 32]); hb_ = Buf()
        hta = sb("hta", [128, 32]); htb = sb("htb", [128, 32]); hsc = Buf()
        ysr = Ring(es, nc, "ys", 2, [128, 8, L], F32)
        g1 = Ring(es, nc, "g1", 2, [128, 8, L], F32)
        g2 = Ring(es, nc, "g2", 2, [128, 8, L], F32)
        zr = Ring(es, nc, "zr", 2, [128, 8, L], BF)

        zpad = sb("zpad", [128, 8, RP - SC0 - NSQ * SQL], BF)
        zpb = Buf()
        R.op("pool", lambda e: e.memset(zpad[:], 0.0), writes=[zpb])
        R.dma("sp", Z[:, SC0 + NSQ * SQL:RP].rearrange("(ct p) c -> p ct c", p=128), zpad[:], reads=[zpb], writes=[xbuf(C, "Zpad", 0, 0)])
        seqs = [(0, NPR, None)] + [(SC0 + s * SQL, SQL, s) for s in range(NSQ)]
        chunks = []
        for (q0, qlen, sidx) in seqs:
            t0s = list(range(0, qlen, L))
            for k, t0_ in enumerate(t0s):
                chunks.append(dict(col=q0 + t0_, n=min(L, qlen - t0_), sidx=sidx, first=(k == 0), last=(k == len(t0s) - 1)))

        def head(c):
            n = c["n"]; col = c["col"]; sidx = c["sidx"]
            if c["first"]:
                if sidx is None:
                    R.op("dve", lambda e: e.memset(hre[:], 0.0), writes=[hb_])
                    R.op("dve", lambda e: e.memset(him[:], 0.0), writes=[hb_])
                else:
                    R.dma("sp", hre[:], I["ssm_h0re"][sidx], writes=[hb_])
                    R.dma("sp", him[:], I["ssm_h0im"][sidx], writes=[hb_])
            ub, ubb = ur.next()
            uf, ufb = u32.next()
            R.dma("pool", ub[:, :, 0:n], U[1024:2048, col:col + n].rearrange("(ct p) c -> p ct c", p=128),
                  reads=[xbuf(C, "U", 8, 0)], writes=[ubb])
            R.dma("sp", uf[:, :, 0:n], U[1024:2048, col:col + n].rearrange("(ct p) c -> p ct c", p=128),
                  reads=[xbuf(C, "U", 8, 0)], writes=[ufb])
            prods = [[None, None] for _ in range(4)]
            for hf in range(2):
                for j in range(HT):
                    st = hf * HT + j
                    R.op("pe", lambda e, o=Pre[:, j, 0:n], l=Bre[:, st, :], r=ub[:, st // 4, 0:n]:
                         e.matmul(o, lhsT=l, rhs=r, start=True, stop=True),
                         reads=[wb_, ubb], writes=[Preb], inc=False)
                    R.op("pe", lambda e, o=Pim[:, j, 0:n], l=Bim[:, st, :], r=ub[:, st // 4, 0:n]:
                         e.matmul(o, lhsT=l, rhs=r, start=True, stop=True),
                         reads=[wb_, ubb], writes=[Pimb], inc=(j == HT - 1))
                sl = slice(hf * HT, (hf + 1) * HT)
                R.op("dve", lambda e, sl=sl: e.tensor_tensor(out=m1[:, :, 0:n], in0=Pre[:, :, 0:n], in1=TA[:, sl, 0:n], op=ALU.mult),
                     reads=[Preb, tabs], writes=[m1b])
                R.op("dve", lambda e, sl=sl: e.tensor_tensor(out=m2[:, :, 0:n], in0=Pim[:, :, 0:n], in1=TB[:, sl, 0:n], op=ALU.mult),
                     reads=[Pimb, tabs], writes=[m2b])
                R.op("dve", lambda e: e.tensor_tensor(out=cr_[:, :, 0:n], in0=m1[:, :, 0:n], in1=m2[:, :, 0:n], op=ALU.subtract),
                     reads=[m1b, m2b], writes=[crb_])
                R.op("dve", lambda e, sl=sl: e.tensor_tensor(out=m1[:, :, 0:n], in0=Pre[:, :, 0:n], in1=TB[:, sl, 0:n], op=ALU.mult),
                     reads=[Preb, tabs], writes=[m1b])
                R.op("dve", lambda e, sl=sl: e.tensor_tensor(out=m2[:, :, 0:n], in0=Pim[:, :, 0:n], in1=TA[:, sl, 0:n], op=ALU.mult),
                     reads=[Pimb, tabs], writes=[m2b])
                R.op("dve", lambda e: e.tensor_tensor(out=ci_[:, :, 0:n], in0=m1[:, :, 0:n], in1=m2[:, :, 0:n], op=ALU.add),
                     reads=[m1b, m2b], writes=[cib_])
                for j in range(HT):
                    st = hf * HT + j
                    R.op("dve", lambda e, st=st, j=j: e.tensor_tensor_scan(out=kr[:, st, 0:n], data0=Rb[:, st, 0:n], data1=cr_[:, j, 0:n],
                                                                         initial=hre[:, st:st + 1], op0=ALU.mult, op1=ALU.add),
                         reads=[crb_, hb_, tabs], writes=[krb[hf]])
                    R.op("dve", lambda e, st=st, j=j: e.tensor_tensor_scan(out=ki[:, st, 0:n], data0=Rb[:, st, 0:n], data1=ci_[:, j, 0:n],
                                                                         initial=him[:, st:st + 1], op0=ALU.mult, op1=ALU.add),
                         reads=[cib_, hb_, tabs], writes=[kib[hf]])
                for k, (srcT, srcB, tab) in enumerate(((kr, krb, Ec), (ki, kib, nEs), (kr, krb, nEs), (ki, kib, nEc))):
                    pt_, pb_ = qr[k][hf].next()
                    R.op("pool", lambda e, o=pt_, s_=srcT, t_=tab, sl=sl: e.tensor_tensor(out=o[:, :, 0:n], in0=s_[:, sl, 0:n], in1=t_[:, sl, 0:n], op=ALU.mult),
                         reads=[srcB[hf], tabs], writes=[pb_])
                    prods[k][hf] = (pt_, pb_)
            R.op("dve", lambda e: e.tensor_tensor(out=hta[:].unsqueeze(2), in0=kr[:, :, n - 1:n], in1=Ec[:, :, n - 1:n], op=ALU.mult),
                 reads=[krb[0], krb[1], tabs], writes=[hsc])
            R.op("dve", lambda e: e.tensor_tensor(out=htb[:].unsqueeze(2), in0=ki[:, :, n - 1:n], in1=Es[:, :, n - 1:n], op=ALU.mult),
                 reads=[kib[0], kib[1], tabs], writes=[hsc])
            R.op("dve", lambda e: e.tensor_tensor(out=hre[:], in0=hta[:], in1=htb[:], op=ALU.subtract),
                 reads=[hsc], writes=[hb_])
            R.op("dve", lambda e: e.tensor_tensor(out=hta[:].unsqueeze(2), in0=kr[:, :, n - 1:n], in1=Es[:, :, n - 1:n], op=ALU.mult),
                 reads=[krb[0], krb[1], tabs], writes=[hsc])
            R.op("dve", lambda e: e.tensor_tensor(out=htb[:].unsqueeze(2), in0=ki[:, :, n - 1:n], in1=Ec[:, :, n - 1:n], op=ALU.mult),
                 reads=[kib[0], kib[1], tabs], writes=[hsc])
            R.op("dve", lambda e: e.tensor_tensor(out=him[:], in0=hta[:], in1=htb[:], op=ALU.add),
                 reads=[hsc], writes=[hb_])
            if c["last"]:
                if sidx is None:
                    R.dma("sp", O["hre_p"], hre[:], reads=[hb_], writes=[xbuf(C, "hst", 0, 0)])
                    R.dma("sp", O["him_p"], him[:], reads=[hb_], writes=[xbuf(C, "hst", 1, 0)])
                else:
                    R.dma("sp", O["hre_s"][sidx], hre[:], reads=[hb_], writes=[xbuf(C, "hst", 2, sidx)])
                    R.dma("sp", O["him_s"][sidx], him[:], reads=[hb_], writes=[xbuf(C, "hst", 3, sidx)])
            c["uf"] = (uf, ufb)
            c["prods"] = prods

        def tail(c):
            n = c["n"]; col = c["col"]
            uf, ufb = c["uf"]
            prods = c["prods"]
            yp, ypb = Yp.next()
            first = True
            for ct in range(8):
                for jj in range(4):
                    st = ct * 4 + jj
                    hf = st // HT
                    j = st % HT
                    for k, Wt in ((0, Cre), (1, Cre), (2, Cim), (3, Cim)):
                        pp, ppb = prods[k][hf]
                        last = (ct == 7 and jj == 3 and k == 3)
                        R.op("pe", lambda e, o=yp[:, ct, 0:n], l=Wt[:, st, :], r=pp[:, j, 0:n], s=first:
                             e.matmul(o, lhsT=l, rhs=r, start=s, stop=False, skip_group_check=True),
                             reads=[wb_, ppb], writes=[ypb], inc=last)
                        first = False
            ys, ysb = ysr.next()
            for ct in range(8):
                R.op("dve", lambda e, ct=ct, ys=ys, uf=uf, yp=yp: e.scalar_tensor_tensor(out=ys[:, ct, 0:n], in0=uf[:, ct, 0:n], scalar=C.ssmd[:, ct:ct + 1],
                                                                    in1=yp[:, ct, 0:n], op0=ALU.mult, op1=ALU.add),
                     reads=[ufb, ypb], writes=[ysb])
            a1, a1b = g1.next(); a2, a2b = g2.next()
            R.op("act", lambda e, a1=a1, ys=ys: e.activation(out=a1[:, :, 0:n], in_=ys[:, :, 0:n], func=AF.Square), reads=[ysb], writes=[a1b])
            R.op("dve", lambda e, a1=a1, a2=a2: e.tensor_scalar(out=a2[:, :, 0:n], in0=a1[:, :, 0:n], scalar1=0.044715, scalar2=1.0, op0=ALU.mult, op1=ALU.add),
                 reads=[a1b], writes=[a2b])
            R.op("dve", lambda e, a1=a1, a2=a2, ys=ys: e.tensor_tensor(out=a1[:, :, 0:n], in0=a2[:, :, 0:n], in1=ys[:, :, 0:n], op=ALU.mult),
                 reads=[a2b, ysb], writes=[a1b])
            R.op("act", lambda e, a1=a1, a2=a2: e.activation(out=a2[:, :, 0:n], in_=a1[:, :, 0:n], func=AF.Sigmoid, scale=2.0 * 0.7978845608028654),
                 reads=[a1b], writes=[a2b])
            zt, ztb = zr.next()
            R.op("dve", lambda e, zt=zt, a2=a2, ys=ys: e.tensor_tensor(out=zt[:, :, 0:n], in0=a2[:, :, 0:n], in1=ys[:, :, 0:n], op=ALU.mult),
                 reads=[a2b, ysb], writes=[ztb])
            R.dma("sp", Z[:, col:col + n].rearrange("(ct p) c -> p ct c", p=128), zt[:, :, 0:n], reads=[ztb],
                  writes=[xbuf(C, "Z", 0, 0)])

        pending = None
        for c in chunks:
            head(c)
            if pending is not None:
                tail(pending)
            pending = c
        tail(pending)
        R.flush()
    R.barrier()


def glu_stage(R, nc, C, Z, YP, I):
    wv = I["ssm_w_glu"].rearrange("(kt p) m -> p kt m", p=128)
    with ExitStack() as es:
        zf = es.enter_context(_sbt(nc, "zf", [128, 8, RP], BF))
        zb = Buf()
        wr = Ring(es, nc, "wg", 2, [128, 8, 128], BF)
        pr = Ring(es, nc, "pgl", 3, [128, 512], F32, psum=True)
        sr = Ring(es, nc, "sgl", 2, [128, 512], F32)
        orr = Ring(es, nc, "ogl", 2, [128, 512], BF)
        for ct in range(8):
            R.dma("sp", zf[:, ct, :], Z[ct * 128:(ct + 1) * 128, :], reads=[xbuf(C, "Z", 0, 0)], writes=[zb])
        for mc in range(8):
            wt, wb = wr.next()
            R.dma("pool", wt[:], wv[:, :, mc * 128:(mc + 1) * 128], writes=[wb])
            for cc in range(0, RP, 512):
                n = min(512, RP - cc)
                p, pb = pr.next()
                for kt in range(8):
                    R.op("pe", lambda e, o=p[:, 0:n], l=wt[:, kt, :], r=zf[:, kt, cc:cc + n], s=(kt == 0), q=(kt == 7):
                         e.matmul(o, lhsT=l, rhs=r, start=s, stop=q), reads=[wb, zb], writes=[pb], inc=(kt == 7))
                st, sb_ = sr.next()
                R.op("act", lambda e, o=st[:, 0:n], i=p[:, 0:n], b=C.bglu[:, mc:mc + 1]:
                     e.activation(out=o, in_=i, func=AF.Sigmoid, bias=b), reads=[pb], writes=[sb_])
                ot, ob = orr.next()
                R.op("dve", lambda e, o=ot[:, 0:n], a=st[:, 0:n], b=zf[:, mc, cc:cc + n]:
                     e.tensor_tensor(out=o, in0=a, in1=b, op=ALU.mult), reads=[sb_, zb], writes=[ob])
                R.dma("sp", YP[(8 + mc) * 128:(9 + mc) * 128, cc:cc + n], ot[:, 0:n], reads=[ob],
                      writes=[xbuf(C, "YP", 8 + mc, 0)])
        R.flush()
    R.barrier()


def attn_core(R, nc, C, A, groups, kblocks, ncols, diag):
    blks = []
    for bi in range(len(kblocks) - 1, -1, -1):
        k0, nk = kblocks[bi]
        cstart, mask = diag(k0, nk)
        if cstart < ncols:
            blks.append((k0, nk, cstart, mask))
    n = len(blks)
    st = [dict() for _ in range(n)]
    Rs = []
    for _ in range(3):
        rt, rb = A.rr.next()
        R.op("pool", lambda e, o=rt: e.memset(o[:, 0:ncols], 0.0), writes=[rb])
        Rs.append((rt, rb))
    firstpv = [True]

    def P1(i):
        k0, nk, cs, mask = blks[i]
        ps, psb = A.psr.next()
        for (gc0, gn, qT, kTf, vf, gdeps) in groups:
            lo = max(gc0, cs)
            if lo >= gc0 + gn:
                continue
            R.op("pe", lambda e, o=ps[0:nk, lo:gc0 + gn], l=kTf(k0, nk), r=qT[:, lo - gc0:gn]:
                 e.matmul(o, lhsT=l, rhs=r, start=True, stop=True), reads=gdeps, writes=[psb], inc=True)
        st[i]["ps"] = (ps, psb)

    def A1(i):
        k0, nk, cs, mask = blks[i]
        ps, psb = st[i]["ps"]
        et, eb = A.er.next()
        R.op("act", lambda e, o=et[0:nk, cs:ncols], i_=ps[0:nk, cs:ncols]:
             e.activation(out=o, in_=i_, func=AF.Exp, scale=A.scale), reads=[psb], writes=[eb])
        spt, spb = A.spr.next()
        R.op("act", lambda e, o=spt[0:nk, cs:ncols], i_=et[0:nk, cs:ncols], b=C.oneb[0:nk, 0:1]:
             e.activation(out=o, in_=i_, func=AF.Ln, bias=b), reads=[eb], writes=[spb])
        if mask is not None:
            for (mc0, mn, map_) in mask:
                R.op("dve", lambda e, o=spt[0:nk, mc0:mc0 + mn], m_=map_:
                     e.tensor_tensor(out=o, in0=o, in1=m_, op=ALU.mult), reads=[spb], writes=[spb])
        st[i]["e"] = (et, eb)
        st[i]["sp"] = (spt, spb)

    def D1(i):
        if i == n - 1:
            return
        k0, nk, cs, mask = blks[i]
        spt, spb = st[i]["sp"]
        rp, rpb = Rs[i % 3]
        rn, rnb = Rs[(i + 1) % 3]
        R.op("dve", lambda e, o=rn[0:nk, cs:ncols], a=spt[0:nk, cs:ncols], b=rp[0:nk, cs:ncols]:
             e.tensor_tensor(out=o, in0=a, in1=b, op=ALU.add), reads=[spb, rpb], writes=[rnb])

    def P2(i):
        k0, nk, cs, mask = blks[i]
        spt, spb = st[i]["sp"]
        pc, pcb = A.pcr.next()
        R.op("pe", lambda e, o=pc[0:nk, cs:ncols], l=C.LTb[0:nk, 0:nk], r=spt[0:nk, cs:ncols], s_=(i == 0):
             e.matmul(o, lhsT=l, rhs=r, start=True, stop=s_), reads=[spb], writes=[pcb], inc=(i == 0))
        if i > 0:
            rp, rpb = Rs[i % 3]
            R.op("pe", lambda e, o=pc[0:nk, cs:ncols], l=C.onesb[0:128, 0:nk], r=rp[0:128, cs:ncols]:
                 e.matmul(o, lhsT=l, rhs=r, start=False, stop=True, skip_group_check=True), reads=[rpb], writes=[pcb], inc=True)
        st[i]["pc"] = (pc, pcb)

    def A2(i):
        k0, nk, cs, mask = blks[i]
        pc, pcb = st[i]["pc"]
        rt2, r2b = A.xr.next()
        R.op("act", lambda e, o=rt2[0:nk, cs:ncols], i_=pc[0:nk, cs:ncols]:
             e.activation(out=o, in_=i_, func=AF.Exp, scale=-1.0), reads=[pcb], writes=[r2b])
        st[i]["r"] = (rt2, r2b)

    def D2(i):
        k0, nk, cs, mask = blks[i]
        et, eb = st[i]["e"]
        rt2, r2b = st[i]["r"]
        wt, wb = A.wr.next()
        R.op("dve", lambda e, o=wt[0:nk, cs:ncols], a=et[0:nk, cs:ncols], b=rt2[0:nk, cs:ncols]:
             e.tensor_tensor(out=o, in0=a, in1=b, op=ALU.mult), reads=[eb, r2b], writes=[wb])
        if mask is not None:
            for (mc0, mn, map_) in mask:
                R.op("dve", lambda e, o=wt[0:nk, mc0:mc0 + mn], m_=map_:
                     e.tensor_tensor(out=o, in0=o, in1=m_, op=ALU.mult), reads=[wb], writes=[wb])
        st[i]["w"] = (wt, wb)

    def P3(i):
        k0, nk, cs, mask = blks[i]
        wt, wb = st[i]["w"]
        for (gc0, gn, qT, kTf, vf, gdeps) in groups:
            lo = max(gc0, cs)
            if lo >= gc0 + gn:
                continue
            R.op("pe", lambda e, o=A.po[:, lo:gc0 + gn], l=vf(k0, nk), r=wt[0:nk, lo:gc0 + gn], s_=firstpv[0]:
                 e.matmul(o, lhsT=l, rhs=r, start=s_, stop=False, skip_group_check=True),
                 reads=[wb] + list(gdeps), writes=[A.pob], inc=True)
            firstpv[0] = False

    for step in range(n + 2):
        if step < n:
            P1(step)
        if 0 <= step - 1 < n:
            P2(step - 1)
        if 0 <= step - 2 < n:
            P3(step - 2)
        if step < n:
            A1(step)
            D1(step)
        if 0 <= step - 1 < n:
            A2(step - 1)
            D2(step - 1)


def alloc_attn(es, nc, A):
    A.psr = Ring(es, nc, "aps", 2, [128, 512], F32, psum=True)
    A.pcr = Ring(es, nc, "apc", 2, [128, 512], F32, psum=True)
    A.po = es.enter_context(_pst(nc, "apo", [128, 512], F32))
    A.pob = Buf()
    A.er = Ring(es, nc, "ae", 4, [128, 512], F32)
    A.spr = Ring(es, nc, "asp", 4, [128, 512], BF)
    A.rr = Ring(es, nc, "arr", 3, [128, 512], BF)
    A.xr = Ring(es, nc, "axr", 3, [128, 512], F32)
    A.wr = Ring(es, nc, "awr", 4, [128, 512], BF)
    A.scale = 1.0 / math.sqrt(128.0)


def attn_prompt_stage(R, nc, C, Q, K, V, OUT):
    with ExitStack() as es:
        A = Ctx()
        alloc_attn(es, nc, A)
        qr = Ring(es, nc, "aq", 2, [128, NPR], BF)
        kr = Ring(es, nc, "ak", 2, [128, NPR], BF)
        vfr = Ring(es, nc, "avf", 2, [128, NPR], BF)
        vtr = Ring(es, nc, "avt", 2, [128, 33, 128], BF)
        ptr = Ring(es, nc, "apt", 2, [128, 512], BF, psum=True)
        oo = Ring(es, nc, "aoo", 2, [128, 512], BF)
        kblocks = [(k0, min(128, NPR - k0)) for k0 in range(0, NPR, 128)]
        for h in range(NHEAD):
            qt, qb = qr.next(); kt, kb = kr.next(); vf, vfb = vfr.next(); vt, vtb = vtr.next()
            R.dma("sp", qt[:], Q[h * 128:(h + 1) * 128, 0:NPR], reads=[xbuf(C, "Q", h, 0)], writes=[qb])
            R.dma("sp", kt[:], K[h * 128:(h + 1) * 128, 0:NPR], reads=[xbuf(C, "K", h, 0)], writes=[kb])
            R.dma("sp", vf[:], V[h * 128:(h + 1) * 128, 0:NPR], reads=[xbuf(C, "V", h, 0)], writes=[vfb])
            for b0 in range(0, len(kblocks), 4):
                pt, ptb = ptr.next()
                blks = kblocks[b0:b0 + 4]
                for j, (k0, nk) in enumerate(blks):
                    R.op("pe", lambda e, o=pt[0:nk, j * 128:(j + 1) * 128], i=vf[:, k0:k0 + nk]:
                         e.transpose(o, i, C.identb[:]), reads=[vfb], writes=[ptb], inc=(j == len(blks) - 1))
                for j, (k0, nk) in enumerate(blks):
                    R.op("dve", lambda e, o=vt[0:nk, b0 + j, :], i=pt[0:nk, j * 128:(j + 1) * 128]:
                         e.tensor_copy(out=o, in_=i), reads=[ptb], writes=[vtb])
            for q0 in range(0, NPR, 512):
                nq = min(512, NPR - q0)
                kb_list = [kbk for kbk in kblocks if kbk[0] < q0 + nq]

                def diag(k0, nk, q0=q0, nq=nq):
                    if k0 + nk <= q0:
                        return 0, None
                    cs = k0 - q0
                    mn = min(nk, nq - cs)
                    return cs, [(cs, mn, C.trib[0:nk, 0:mn])]
                groups = [(0, nq, qt[:, q0:q0 + nq],
                           (lambda k0, nk, kt=kt: kt[:, k0:k0 + nk]),
                           (lambda k0, nk, vt=vt: vt[0:nk, k0 // 128, :]),
                           [qb, kb, vtb])]
                attn_core(R, nc, C, A, groups, kb_list, nq, diag)
                ot, ob = oo.next()
                R.op("act", lambda e, o=ot[:, 0:nq], i=A.po[:, 0:nq]: e.activation(out=o, in_=i, func=AF.Copy),
                     reads=[A.pob], writes=[ob])
                R.dma("sp", OUT[h * 128:(h + 1) * 128, q0:q0 + nq], ot[:, 0:nq], reads=[ob],
                      writes=[xbuf(C, "YP", h, 0)])
        R.flush()
    R.barrier()


def attn_sample_stage(R, nc, C, Q, K, V, OUT, I):
    HG = 4
    NK = PAST + SQL
    with ExitStack() as es:
        A = Ctx()
        alloc_attn(es, nc, A)
        kr = Ring(es, nc, "sk", 2, [128, HG, NK], BF)
        vr = Ring(es, nc, "sv", 2, [128, 32, HG * 128], BF)
        qr = Ring(es, nc, "sq", 2, [128, HG, SQL], BF)
        vnf = Ring(es, nc, "svn", 2, [128, HG, SQL], BF)
        vnt = Ring(es, nc, "svt", 2, [SQL, HG, 128], BF)
        ptr = Ring(es, nc, "spt", 2, [128, 512], BF, psum=True)
        oo = Ring(es, nc, "soo", 2, [128, HG * SQL], BF)
        kblocks = [(k0, 128) for k0 in range(0, PAST, 128)] + [(PAST, SQL)]
        for s in range(NSQ):
            col = SC0 + s * SQL
            for hg in range(16 // HG):
                kt, kb = kr.next(); vt, vb = vr.next(); qt, qb = qr.next(); vn, vnb = vnf.next(); vtt, vttb = vnt.next()
                h0 = hg * HG
                R.dma("pool", kt[:, :, 0:PAST], I["cache_kT"][s, h0:h0 + HG].rearrange("h d k -> d h k"), writes=[kb])
                R.dma("sp", kt[:, :, PAST:NK], K[h0 * 128:(h0 + HG) * 128, col:col + SQL].rearrange("(h d) c -> d h c", d=128),
                      reads=[xbuf(C, "K", 0, 0)], writes=[kb])
                R.dma("pool", vt[:], I["cache_v"][s][:, h0 * 128:(h0 + HG) * 128].rearrange("(b p) c -> p b c", p=128), writes=[vb])
                R.dma("sp", qt[:], Q[h0 * 128:(h0 + HG) * 128, col:col + SQL].rearrange("(h d) c -> d h c", d=128),
                      reads=[xbuf(C, "Q", 0, 0)], writes=[qb])
                R.dma("sp", vn[:], V[h0 * 128:(h0 + HG) * 128, col:col + SQL].rearrange("(h d) c -> d h c", d=128),
                      reads=[xbuf(C, "V", 0, 0)], writes=[vnb])
                pt, ptb = ptr.next()
                for j in range(HG):
                    R.op("pe", lambda e, o=pt[0:SQL, j * 128:(j + 1) * 128], i=vn[:, j, :]:
                         e.transpose(o, i, C.identb[:]), reads=[vnb], writes=[ptb], inc=(j == HG - 1))
                R.op("dve", lambda e, o=vtt[:], i=pt[0:SQL, 0:HG * 128].rearrange("p (h d) -> p h d", d=128):
                     e.tensor_copy(out=o, in_=i), reads=[ptb], writes=[vttb])

                def diag(k0, nk):
                    if k0 < PAST:
                        return 0, None
                    return 0, [(j * SQL, SQL, C.trib[0:SQL, 0:SQL]) for j in range(HG)]
                groups = []
                for j in range(HG):
                    groups.append((j * SQL, SQL, qt[:, j, :],
                                   (lambda k0, nk, kt=kt, j=j: kt[:, j, k0:k0 + nk]),
                                   (lambda k0, nk, vt=vt, vtt=vtt, j=j: (vt[:, k0 // 128, j * 128:(j + 1) * 128] if k0 < PAST else vtt[:, j, :])),
                                   [qb, kb, vb, vttb]))
                attn_core(R, nc, C, A, groups, kblocks, HG * SQL, diag)
                ot, ob = oo.next()
                R.op("act", lambda e, o=ot[:], i=A.po[:, 0:HG * SQL]: e.activation(out=o, in_=i, func=AF.Copy),
                     reads=[A.pob], writes=[ob])
                R.dma("sp", OUT[h0 * 128:(h0 + HG) * 128, col:col + SQL].rearrange("(h d) c -> d h c", d=128),
                      ot[:].rearrange("p (h c) -> p h c", c=SQL), reads=[ob], writes=[xbuf(C, "YP", 0, 1)])
        R.flush()
    R.barrier()


def build_program(upto=99):
    nc = bass.Bass("TRN2", target_bir_lowering=False)
    I = {}

    def inp(name, shape, dt=F32):
        I[name] = nc.dram_tensor(name, list(shape), dt, kind="ExternalInput").ap()
        return I[name]
    inp("xin", [D, RP])
    inp("ffn_w_gate", [4, D, FF]); inp("ffn_w_up", [4, D, FF]); inp("ffn_w_down", [4, FF, D])
    inp("ab_w_in", [D, D]); inp("ab_w_out", [D, D]); inp("pool_w", [4, 256, 256])
    inp("ssm_w_glu", [1024, 1024]); inp("sb_w_qkv", [D, 3 * D]); inp("sb_w_out", [D, D])
    inp("gall", [128, 7, 16]); inp("pscale", [128, 8]); inp("ssmd", [128, 8]); inp("bglu", [128, 8])
    inp("invcnt", [128, 4, 16])
    inp("ssm_are", [128, 32]); inp("ssm_aim", [128, 32]); inp("ssm_ldt", [128, 32])
    inp("ssm_Bre", [128, 32, 128]); inp("ssm_Bim", [128, 32, 128])
    inp("ssm_Cre", [128, 32, 128]); inp("ssm_Cim", [128, 32, 128])
    inp("ssm_h0re", [NSQ, 128, 32]); inp("ssm_h0im", [NSQ, 128, 32])
    inp("pool_hist", [128, 8, NSQ, 15])
    inp("cache_kT", [NSQ, 16, 128, PAST]); inp("cache_v", [NSQ, PAST, D])
    inp("c_tri", [128, 128]); inp("c_lt", [128, 128]); inp("c_ident", [128, 128])
    O = {}

    def outp(name, shape, dt=F32):
        O[name] = nc.dram_tensor(name, list(shape), dt, kind="ExternalOutput").ap()
        return O[name]
    outp("y", [D, RP]); outp("kout", [D, RP]); outp("vout", [D, RP])
    outp("pool_p", [1024, 15]); outp("pool_s", [1024, NSQ, 15])
    outp("hre_p", [128, 32]); outp("him_p", [128, 32]); outp("hre_s", [NSQ, 128, 32]); outp("him_s", [NSQ, 128, 32])
    X = nc.dram_tensor("Xs", [D, RP], F32).ap()
    U = nc.dram_tensor("Us", [D, RP], F32).ap()
    Z = nc.dram_tensor("Zs", [1024, RP], BF).ap()
    YP = nc.dram_tensor("YPs", [D, RP], BF).ap()
    Qs = nc.dram_tensor("Qs", [D, RP], BF).ap()
    Ks = nc.dram_tensor("Ks", [D, RP], BF).ap()
    Vs = nc.dram_tensor("Vs", [D, RP], BF).ap()

    R = Rec(nc)
    C = Ctx()
    C.dbufs = {}
    with ExitStack() as es:
        R.begin(es)
        def sb(name, shape, dt=F32):
            return es.enter_context(_sbt(nc, name, shape, dt))
        C.ones32 = sb("ones32", [128, 128]); C.onesb = sb("onesb", [128, 128], BF)
        C.trib = sb("trib", [128, 128], BF); C.LTb = sb("LTb", [128, 128], BF); C.identb = sb("identb", [128, 128], BF)
        C.gall = sb("gall", [128, 7, 16]); C.pscale = sb("pscale", [128, 8]); C.ssmd = sb("ssmd", [128, 8])
        C.bglu = sb("bglu", [128, 8]); C.invcnt = sb("invcnt", [128, 4, 16])
        C.epsb = sb("epsb", [128, 1]); C.oneb = sb("oneb", [128, 1]); C.halfpi = sb("halfpi", [128, 1])
        cb = Buf()
        R.op("dve", lambda e: e.memset(C.ones32[:], 1.0), writes=[cb])
        R.op("dve", lambda e: e.memset(C.onesb[:], 1.0), writes=[cb])
        R.op("dve", lambda e: e.memset(C.epsb[:], EPS), writes=[cb])
        R.op("dve", lambda e: e.memset(C.oneb[:], 1.0), writes=[cb])
        R.op("dve", lambda e: e.memset(C.halfpi[:], math.pi / 2), writes=[cb])
        R.dma("pool", C.trib[:], I["c_tri"], writes=[cb])
        R.dma("pool", C.LTb[:], I["c_lt"], writes=[cb])
        R.dma("pool", C.identb[:], I["c_ident"], writes=[cb])
        for nm in ("gall", "pscale", "ssmd", "bglu", "invcnt"):
            R.dma("sp", getattr(C, nm)[:], I[nm], writes=[cb])
        R.barrier()

        wgate = I["ffn_w_gate"]; wup = I["ffn_w_up"]; wdn = I["ffn_w_down"]
        stage = 0

        def go():
            nonlocal stage
            stage += 1
            return stage <= upto
        skipffn = os.environ.get("MK_SKIPFFN", "0") == "1"
        if go():
            if skipffn:
                for dk in range(16):
                    R.dma("sp", X[dk * 128:(dk + 1) * 128, :], I["xin"][dk * 128:(dk + 1) * 128, :], writes=[xbuf(C, "X", dk, 0)])
                R.barrier()
            else:
                ffn_stage(R, nc, C, I["xin"], "xin", X, "X", wgate[0], wup[0], wdn[0], 0)
        if go():
            def cb_u(es2):
                orr = Ring(es2, nc, "uo", 3, [128, 448], F32)

                def f(si, mc, c0, n, p, pb):
                    ot, ob = orr.next()
                    R.op("act", lambda e, o=ot[:, 0:n], i=p[:, 0:n]: e.activation(out=o, in_=i, func=AF.Copy), reads=[pb], writes=[ob])
                    R.dma("act", U[mc * 128:(mc + 1) * 128, c0:c0 + n], ot[:, 0:n], reads=[ob],
                          writes=[xbuf(C, "U", mc, 0), xbuf(C, "U", 8, 0)] if mc >= 8 else [xbuf(C, "U", mc, 0)])
                return f
            norm_linear_stage(R, nc, C, X, "X", 4, I["ab_w_in"], D, cb_u)
            R.dma("sp", O["pool_p"], U[0:1024, NPR - 15:NPR], reads=[xbuf(C, "U", k, 0) for k in range(8)], writes=[xbuf(C, "pp", 0, 0)])
            for s in range(NSQ):
                R.dma("sp", O["pool_s"][:, s, :], U[0:1024, SC0 + s * SQL + 1:SC0 + (s + 1) * SQL],
                      reads=[xbuf(C, "U", k, 0) for k in range(8)], writes=[xbuf(C, "pp", 1, s)])
        if go():
            pool_stage(R, nc, C, U, YP, I)
        if go():
            ssm_stage(R, nc, C, U, Z, I, O)
        if go():
            glu_stage(R, nc, C, Z, YP, I)
        if go():
            for kt in range(16):
                for si in range(len(SUP)):
                    C.dbufs[("YPl", kt, si)] = Buf()
            linear_residual_stage(R, nc, C, YP, "YPl", I["ab_w_out"], X, "X")
        if go() and not skipffn:
            ffn_stage(R, nc, C, X, "X", X, "X", wgate[1], wup[1], wdn[1], 1)
        if go() and not skipffn:
            ffn_stage(R, nc, C, X, "X", X, "X", wgate[2], wup[2], wdn[2], 2)
        if go():
            def cb_qkv(es2):
                o32 = Ring(es2, nc, "qo32", 3, [128, 448], F32)
                obf = Ring(es2, nc, "qobf", 3, [128, 448], BF)

                def f(si, mc, c0, n, p, pb):
                    which = mc // 16
                    hh = mc % 16
                    dst = (Qs, Ks, Vs)[which]
                    ot, ob = obf.next()
                    if which == 0:
                        R.op("act", lambda e, o=ot[:, 0:n], i=p[:, 0:n]: e.activation(out=o, in_=i, func=AF.Copy), reads=[pb], writes=[ob])
                    else:
                        o2, o2b = o32.next()
                        R.op("act", lambda e, o=o2[:, 0:n], i=p[:, 0:n]: e.activation(out=o, in_=i, func=AF.Copy), reads=[pb], writes=[o2b])
                        R.dma("act", (O["kout"], O["vout"])[which - 1][hh * 128:(hh + 1) * 128, c0:c0 + n], o2[:, 0:n],
                              reads=[o2b], writes=[xbuf(C, "kvout", which, mc)])
                        R.op("dve", lambda e, o=ot[:, 0:n], i=o2[:, 0:n]: e.tensor_copy(out=o, in_=i), reads=[o2b], writes=[ob])
                    R.dma("act" if which == 0 else "sp", dst[hh * 128:(hh + 1) * 128, c0:c0 + n], ot[:, 0:n], reads=[ob],
                          writes=[xbuf(C, "QKV", which, mc)])
                return f
            norm_linear_stage(R, nc, C, X, "X", 5, I["sb_w_qkv"], 3 * D, cb_qkv)
        if go():
            attn_prompt_stage(R, nc, C, Qs, Ks, Vs, YP)
        if go():
            attn_sample_stage(R, nc, C, Qs, Ks, Vs, YP, I)
        if go():
            for kt in range(16):
                for si in range(len(SUP)):
                    C.dbufs[("YPm", kt, si)] = Buf()
            linear_residual_stage(R, nc, C, YP, "YPm", I["sb_w_out"], X, "X")
        if go() and not skipffn:
            ffn_stage(R, nc, C, X, "X", X, "X", wgate[3], wup[3], wdn[3], 3)
        if go():
            final_stage2(R, nc, C, X, "X", O["y"])
        if upto < 99:
            dsrc = {1: X, 2: U, 6: X, 7: X, 8: X, 12: X, 13: X}.get(upto, X)
            for dk in range(16):
                R.dma("sp", O["y"][dk * 128:(dk + 1) * 128, :], dsrc[dk * 128:(dk + 1) * 128, :], writes=[xbuf(C, "Ydump", dk, 0)])
        R.finish()
        print('NOPS', upto, R.nops, 'sem cnt', R.cnt, 'dq', R.dq)
    return nc


def _state_layout(a):
    return np.ascontiguousarray(a.reshape(32, 2, 64).transpose(1, 2, 0).reshape(128, 32))


def _vec128(v):
    return np.ascontiguousarray(v.reshape(-1, 128).T)


_PROG = {}


def kernel(**inp):
    f32 = np.float32
    g = lambda k: np.asarray(inp[k], dtype=f32)
    upto = int(os.environ.get("MK_UPTO", "99"))
    if upto not in _PROG:
        _PROG[upto] = build_program(upto)
    nc = _PROG[upto]
    x_prompt = g("x_prompt"); x_sample = g("x_sample"); meta = g("meta_tokens")
    shared = {}
    shared["ffn_w_gate"] = g("ffn_w_gate").reshape(4, D, FF)
    shared["ffn_w_up"] = g("ffn_w_up").reshape(4, D, FF)
    shared["ffn_w_down"] = g("ffn_w_down").reshape(4, FF, D)
    shared["ab_w_in"] = g("ab_w_in")[0]; shared["ab_w_out"] = g("ab_w_out")[0]
    shared["pool_w"] = g("pool_w")[0]; shared["ssm_w_glu"] = g("ssm_w_glu")[0]
    shared["sb_w_qkv"] = g("sb_w_qkv")[0]; shared["sb_w_out"] = g("sb_w_out")[0]
    fn = g("ffn_norm").reshape(4, D); mn = g("mix_norm"); fin = g("final_norm")
    gall = np.stack([_vec128(fn[0]), _vec128(fn[1]), _vec128(fn[2]), _vec128(fn[3]),
                     _vec128(mn[0]), _vec128(mn[1]), _vec128(fin)], axis=1)
    shared["gall"] = np.ascontiguousarray(gall)
    shared["pscale"] = _vec128(g("pool_scale")[0]); shared["ssmd"] = _vec128(g("ssm_d")[0])
    shared["bglu"] = _vec128(g("ssm_b_glu")[0])
    ic = np.zeros((128, 4, 16), f32)
    for gi in range(4):
        w = 2 << gi
        ic[:, gi, :] = 1.0 / np.minimum(np.arange(16) + 1, w)
    shared["invcnt"] = ic
    shared["ssm_are"] = _state_layout(g("ssm_a_re")[0]); shared["ssm_aim"] = _state_layout(g("ssm_a_im")[0])
    shared["ssm_ldt"] = _state_layout(np.repeat(g("ssm_log_dt")[0][:, None], 64, axis=1))
    bre = g("ssm_b_re")[0]; bim = g("ssm_b_im")[0]; cre = g("ssm_c_re")[0]; cim = g("ssm_c_im")[0]
    Bre = np.zeros((128, 32, 128), f32); Bim = np.zeros((128, 32, 128), f32)
    Cre = np.zeros((128, 32, 128), f32); Cim = np.zeros((128, 32, 128), f32)
    for gg in range(64):
        st = gg // 2; gl = gg % 2
        ch0 = (gg % 8) * 16
        Bre[ch0:ch0 + 16, st, gl * 64:(gl + 1) * 64] = bre[gg].T
        Bim[ch0:ch0 + 16, st, gl * 64:(gl + 1) * 64] = bim[gg].T
        Cre[gl * 64:(gl + 1) * 64, st, ch0:ch0 + 16] = cre[gg].T
        Cim[gl * 64:(gl + 1) * 64, st, ch0:ch0 + 16] = cim[gg].T
    shared["ssm_Bre"] = Bre; shared["ssm_Bim"] = Bim; shared["ssm_Cre"] = Cre; shared["ssm_Cim"] = Cim
    jj = np.arange(128)
    shared["c_tri"] = (jj[:, None] < jj[None, :]).astype(f32)
    shared["c_lt"] = (jj[:, None] >= jj[None, :]).astype(f32)
    shared["c_ident"] = np.eye(128, dtype=f32)
    cache_pool = g("cache_pool")[0]; sre = g("state_ssm_re")[0]; sim_ = g("state_ssm_im")[0]
    cache_k = np.asarray(inp["cache_k"], dtype=f32)[0]; cache_v = np.asarray(inp["cache_v"], dtype=f32)[0]
    in_maps = []
    for c in range(8):
        b = c % 4
        m = dict(shared)
        xin = np.zeros((D, RP), f32)
        xin[:, 0:16] = meta.T
        xin[:, 16:NPR] = x_prompt[b].T
        sq = list(range(4 * c, 4 * c + 4))
        xin[:, SC0:SC0 + NSQ * SQL] = x_sample[sq].reshape(NSQ * SQL, D).T
        m["xin"] = xin
        m["ssm_h0re"] = np.ascontiguousarray(np.stack([_state_layout(sre[s]) for s in sq], axis=0))
        m["ssm_h0im"] = np.ascontiguousarray(np.stack([_state_layout(sim_[s]) for s in sq], axis=0))
        ph = cache_pool[sq]
        m["pool_hist"] = np.ascontiguousarray(ph.transpose(2, 0, 1).reshape(8, 128, NSQ, 15).transpose(1, 0, 2, 3))
        m["cache_kT"] = np.ascontiguousarray(cache_k[sq].transpose(0, 2, 3, 1))
        m["cache_v"] = np.ascontiguousarray(cache_v[sq].reshape(NSQ, PAST, D))
        in_maps.append(m)
    ncore = int(os.environ.get("MK_NCORE", "8"))
    if ncore < 8:
        res = run_bass_kernel_spmd(nc, in_maps[:ncore], core_ids=list(range(ncore)))
        return res.results
    res = run_bass_kernel_spmd(nc, in_maps, core_ids=list(range(8)))
    rs = res.results
    y_prompt = np.stack([rs[b]["y"][:, 16:NPR].T for b in range(4)], 0)
    y_sample = np.concatenate([rs[c]["y"][:, SC0:SC0 + NSQ * SQL].T.reshape(NSQ, SQL, D) for c in range(8)], 0)
    pool_p = np.stack([rs[b]["pool_p"].T for b in range(4)], 0)[None]
    pool_s = np.concatenate([rs[c]["pool_s"].transpose(1, 2, 0) for c in range(8)], 0)[None]

    def unstate(a):
        return a.reshape(2, 64, 32).transpose(2, 0, 1).reshape(64, 64)
    re_p = np.stack([unstate(rs[b]["hre_p"]) for b in range(4)], 0)[None]
    im_p = np.stack([unstate(rs[b]["him_p"]) for b in range(4)], 0)[None]
    re_s = np.stack([unstate(rs[c]["hre_s"][s]) for c in range(8) for s in range(NSQ)], 0)[None]
    im_s = np.stack([unstate(rs[c]["him_s"][s]) for c in range(8) for s in range(NSQ)], 0)[None]
    k_p = np.stack([rs[b]["kout"][:, 0:NPR].T.reshape(NPR, 16, 128) for b in range(4)], 0)[None]
    v_p = np.stack([rs[b]["vout"][:, 0:NPR].T.reshape(NPR, 16, 128) for b in range(4)], 0)[None]
    k_s = np.concatenate([rs[c]["kout"][:, SC0:SC0 + NSQ * SQL].T.reshape(NSQ, SQL, 16, 128) for c in range(8)], 0)[None]
    v_s = np.concatenate([rs[c]["vout"][:, SC0:SC0 + NSQ * SQL].T.reshape(NSQ, SQL, 16, 128) for c in range(8)], 0)[None]
    outs = (y_prompt, y_sample, pool_p, pool_s, re_p, im_p, re_s, im_s, k_p, v_p, k_s, v_s)
    return tuple(np.ascontiguousarray(o, dtype=f32) for o in outs)
```
